# Optimizing a Trainium2 kernel written in Bass

```python
import math
import jax, jax.numpy as jnp
from jax import lax
import numpy as np

D_MODEL = 2048
BATCH = 4
SEQ = 2048
DEPTH = 4
DEC_BATCH = 16
DEC_SEQ = 64
PAST_LEN = 2048

CHUNK = 64
D_SSM = D_MODEL // 2
SSM_GROUP = 16
N_SSM_GROUPS = D_SSM // SSM_GROUP
SSM_STATE = 64
HEAD_DIM = 64
N_HEADS = (D_MODEL // 2) // HEAD_DIM
N_KV_HEADS = 4
KV_REP = N_HEADS // N_KV_HEADS
D_ATTN = N_HEADS * HEAD_DIM
D_KV = N_KV_HEADS * HEAD_DIM
IN_WIDTH = D_SSM + D_ATTN + 2 * D_KV
WINDOW = 128
WINDOW_CHUNKS = WINDOW // CHUNK
ROPE_DIM = HEAD_DIM // 4
ROPE_THETA = 500000.0
D_FF = ((8 * D_MODEL // 3 + 255) // 256) * 256
EPS = 1e-6

kernel_name = 'hybrid_s5_swa_streaming_step'


def rms_norm(x, g):
    xf = x.astype(jnp.float32)
    y = xf * lax.rsqrt(jnp.mean(xf * xf, axis=-1, keepdims=True) + EPS)
    return (y * g.astype(jnp.float32)).astype(x.dtype)


def partial_rope(x, pos):
    half = ROPE_DIM // 2
    inv_freq = ROPE_THETA ** (-jnp.arange(half, dtype=jnp.float32) / half)
    ang = pos.astype(jnp.float32)[:, None] * inv_freq[None, :]
    cos = jnp.cos(ang)[None, :, None, :]
    sin = jnp.sin(ang)[None, :, None, :]
    xr = x[..., :ROPE_DIM].astype(jnp.float32)
    x1, x2 = xr[..., :half], xr[..., half:]
    rot = jnp.concatenate([x1 * cos - x2 * sin, x2 * cos + x1 * sin], axis=-1)
    return jnp.concatenate([rot.astype(x.dtype), x[..., ROPE_DIM:]], axis=-1)


def s5_discretize(a_re, a_im, log_dt, b_re, b_im):
    a_re = a_re.astype(jnp.float32)
    a_im = a_im.astype(jnp.float32)
    dt = jnp.exp(log_dt.astype(jnp.float32))[:, None]
    z_re, z_im = a_re * dt, a_im * dt
    mag = jnp.exp(z_re)
    l_re, l_im = mag * jnp.cos(z_im), mag * jnp.sin(z_im)
    den = a_re * a_re + a_im * a_im
    n_re = l_re - 1.0
    f_re = (n_re * a_re + l_im * a_im) / den
    f_im = (l_im * a_re - n_re * a_im) / den
    b_re = b_re.astype(jnp.float32)
    b_im = b_im.astype(jnp.float32)
    bb_re = f_re[..., None] * b_re - f_im[..., None] * b_im
    bb_im = f_re[..., None] * b_im + f_im[..., None] * b_re
    return z_re, z_im, l_re, l_im, bb_re, bb_im


def _complex_affine_combine(e1, e2):
    a1r, a1i, b1r, b1i = e1
    a2r, a2i, b2r, b2i = e2
    return (a1r * a2r - a1i * a2i,
            a1r * a2i + a1i * a2r,
            a2r * b1r - a2i * b1i + b2r,
            a2r * b1i + a2i * b1r + b2i)


def s5_mixer(u, a_re, a_im, log_dt, b_re, b_im, c_re, c_im, d_skip, w_glu, b_glu, h0):
    bsz, t_len, _ = u.shape
    z_re, z_im, l_re, l_im, bb_re, bb_im = s5_discretize(a_re, a_im, log_dt, b_re, b_im)
    uf = u.astype(jnp.float32)
    ug = uf.reshape(bsz, t_len, N_SSM_GROUPS, SSM_GROUP)
    bu_re = jnp.einsum('btgc,gpc->btgp', ug, bb_re)
    bu_im = jnp.einsum('btgc,gpc->btgp', ug, bb_im)
    a_re_b = jnp.broadcast_to(l_re, bu_re.shape)
    a_im_b = jnp.broadcast_to(l_im, bu_im.shape)
    _, _, h_re, h_im = lax.associative_scan(_complex_affine_combine,
                                            (a_re_b, a_im_b, bu_re, bu_im), axis=1)
    if h0 is not None:
        steps = jnp.arange(1, t_len + 1, dtype=jnp.float32)[:, None, None]
        mag = jnp.exp(z_re[None] * steps)
        ang = z_im[None] * steps
        p_re, p_im = mag * jnp.cos(ang), mag * jnp.sin(ang)
        h0_re = h0[0].astype(jnp.float32)[:, None]
        h0_im = h0[1].astype(jnp.float32)[:, None]
        h_re = h_re + p_re * h0_re - p_im * h0_im
        h_im = h_im + p_re * h0_im + p_im * h0_re
    y = (jnp.einsum('btgp,gcp->btgc', h_re, c_re.astype(jnp.float32))
         - jnp.einsum('btgp,gcp->btgc', h_im, c_im.astype(jnp.float32)))
    y = y.reshape(bsz, t_len, D_SSM) + d_skip.astype(jnp.float32) * uf
    z = jax.nn.gelu(y)
    out = z * jax.nn.sigmoid(z @ w_glu.astype(jnp.float32) + b_glu.astype(jnp.float32))
    return out.astype(u.dtype), h_re[:, -1], h_im[:, -1]


def banded_attention(q, k, v, valid, sink):
    s = jnp.einsum('bnqhrd,bnkhd->bnhrqk', q.astype(jnp.float32), k.astype(jnp.float32))
    s = s * (HEAD_DIM ** -0.5)
    s = jnp.where(valid[None, :, None, None, None, :], s, -1e30)
    sk = sink.astype(jnp.float32).reshape(1, 1, N_KV_HEADS, KV_REP, 1, 1)
    m = jnp.maximum(jnp.max(s, axis=-1, keepdims=True), sk)
    p = jnp.exp(s - m)
    den = jnp.sum(p, axis=-1, keepdims=True) + jnp.exp(sk - m)
    return jnp.einsum('bnhrqk,bnkhd->bnqhrd', p / den, v.astype(jnp.float32))


def chunk_windows(t, n_chunks):
    bsz = t.shape[0]
    tc = t.reshape(bsz, n_chunks, CHUNK, N_KV_HEADS, HEAD_DIM)
    tp = jnp.pad(tc, ((0, 0), (WINDOW_CHUNKS, 0), (0, 0), (0, 0), (0, 0)))
    return jnp.concatenate([tp[:, i:i + n_chunks] for i in range(WINDOW_CHUNKS + 1)], axis=2)


def layer(x, c, pos, p, kv_cache, ssm_state):
    bsz, t_len, _ = x.shape
    mod = (c @ p['w_mod'] + p['b_mod'])[:, None, :]
    sh1, sc1, g1, sh2, sc2, g2 = jnp.split(mod, 6, axis=-1)
    h = rms_norm(x, p['norm1_g']) * (1 + sc1) + sh1
    proj = h @ p['w_in']
    u = proj[..., :D_SSM]
    q = proj[..., D_SSM:D_SSM + D_ATTN].reshape(bsz, t_len, N_HEADS, HEAD_DIM)
    k = proj[..., D_SSM + D_ATTN:D_SSM + D_ATTN + D_KV].reshape(bsz, t_len, N_KV_HEADS, HEAD_DIM)
    v = proj[..., D_SSM + D_ATTN + D_KV:].reshape(bsz, t_len, N_KV_HEADS, HEAD_DIM)
    q = partial_rope(rms_norm(q, p['q_norm_g']), pos)
    k = partial_rope(rms_norm(k, p['k_norm_g']), pos)

    ssm_out, s_re, s_im = s5_mixer(u, p['ssm_a_re'], p['ssm_a_im'], p['ssm_log_dt'],
                                   p['ssm_b_re'], p['ssm_b_im'], p['ssm_c_re'], p['ssm_c_im'],
                                   p['ssm_d'], p['w_glu'], p['b_glu'], ssm_state)

    if kv_cache is None:
        n_chunks = t_len // CHUNK
        qc = q.reshape(bsz, n_chunks, CHUNK, N_KV_HEADS, KV_REP, HEAD_DIM)
        key_chunk = (jnp.arange(n_chunks)[:, None] - WINDOW_CHUNKS
                     + jnp.arange((WINDOW_CHUNKS + 1) * CHUNK)[None, :] // CHUNK)
        valid = key_chunk >= 0
        o = banded_attention(qc, chunk_windows(k, n_chunks), chunk_windows(v, n_chunks),
                             valid, p['attn_sink'])
        new_k, new_v = k[:, -WINDOW:], v[:, -WINDOW:]
    else:
        kk = jnp.concatenate([kv_cache[0].astype(k.dtype), k], axis=1)
        vv = jnp.concatenate([kv_cache[1].astype(v.dtype), v], axis=1)
        qc = q.reshape(bsz, 1, t_len, N_KV_HEADS, KV_REP, HEAD_DIM)
        valid = jnp.ones((1, kk.shape[1]), dtype=bool)
        o = banded_attention(qc, kk[:, None], vv[:, None], valid, p['attn_sink'])
        new_k, new_v = kk[:, -WINDOW:], vv[:, -WINDOW:]
    attn_out = o.reshape(bsz, t_len, D_ATTN).astype(x.dtype)

    gates = jax.nn.sigmoid(h @ p['w_gate'] + p['b_gate'])
    gate_a, gate_b = jnp.split(gates, 2, axis=-1)
    mixed = gate_a * (ssm_out @ p['w_proj_ssm']) + gate_b * (attn_out @ p['w_proj_attn'])
    x = x + g1 * (mixed @ p['w_out'])

    h2 = rms_norm(x, p['norm2_g']) * (1 + sc2) + sh2
    ffn = (jax.nn.silu(h2 @ p['w_ffn_gate']) * (h2 @ p['w_ffn_up'])) @ p['w_ffn_down']
    x = x + g2 * ffn
    return x, new_k, new_v, s_re, s_im


def setup_inputs(seed: int = 0) -> dict:
    key = jax.random.key(seed)
    keys = iter(jax.random.split(key, 48))
    f32 = jnp.float32

    def nrm(shape, scale):
        return scale * jax.random.normal(next(keys), shape, f32)

    def gain(shape):
        return 1.0 + nrm(shape, 0.02)

    L, G, P = DEPTH, N_SSM_GROUPS, SSM_STATE
    a_im_init = math.pi * jnp.arange(P, dtype=f32)
    return {
        'x_prompt': nrm((BATCH, SEQ, D_MODEL), 1.0),
        'x_sample': nrm((DEC_BATCH, DEC_SEQ, D_MODEL), 1.0),
        'cache_k': nrm((DEPTH, DEC_BATCH, WINDOW, N_KV_HEADS, HEAD_DIM), 1.0),
        'cache_v': nrm((DEPTH, DEC_BATCH, WINDOW, N_KV_HEADS, HEAD_DIM), 1.0),
        'state_ssm_re': nrm((DEPTH, DEC_BATCH, G, P), 0.1),
        'state_ssm_im': nrm((DEPTH, DEC_BATCH, G, P), 0.1),
        'c_prompt': nrm((BATCH, D_MODEL), 1.0),
        'c_sample': nrm((DEC_BATCH, D_MODEL), 1.0),
        'w_mod': nrm((L, D_MODEL, 6 * D_MODEL), 0.2 * D_MODEL ** -0.5),
        'b_mod': nrm((L, 6 * D_MODEL), 0.01),
        'norm1_g': gain((L, D_MODEL)),
        'norm2_g': gain((L, D_MODEL)),
        'w_in': nrm((L, D_MODEL, IN_WIDTH), D_MODEL ** -0.5),
        'ssm_a_re': -0.5 + nrm((L, G, P), 0.01),
        'ssm_a_im': a_im_init[None, None, :] + nrm((L, G, P), 0.01),
        'ssm_log_dt': jax.random.uniform(next(keys), (L, G), f32, math.log(1e-3), math.log(1e-1)),
        'ssm_b_re': nrm((L, G, P, SSM_GROUP), (0.5 / SSM_GROUP) ** 0.5),
        'ssm_b_im': nrm((L, G, P, SSM_GROUP), (0.5 / SSM_GROUP) ** 0.5),
        'ssm_c_re': nrm((L, G, SSM_GROUP, P), (0.5 / P) ** 0.5),
        'ssm_c_im': nrm((L, G, SSM_GROUP, P), (0.5 / P) ** 0.5),
        'ssm_d': nrm((L, D_SSM), 0.5),
        'w_glu': nrm((L, D_SSM, D_SSM), D_SSM ** -0.5),
        'b_glu': nrm((L, D_SSM), 0.01),
        'q_norm_g': gain((L, HEAD_DIM)),
        'k_norm_g': gain((L, HEAD_DIM)),
        'attn_sink': nrm((L, N_HEADS), 0.5),
        'w_gate': nrm((L, D_MODEL, 2 * D_MODEL), D_MODEL ** -0.5),
        'b_gate': nrm((L, 2 * D_MODEL), 0.01),
        'w_proj_ssm': nrm((L, D_SSM, D_MODEL), D_SSM ** -0.5),
        'w_proj_attn': nrm((L, D_ATTN, D_MODEL), D_ATTN ** -0.5),
        'w_out': nrm((L, D_MODEL, D_MODEL), D_MODEL ** -0.5),
        'w_ffn_gate': nrm((L, D_MODEL, D_FF), D_MODEL ** -0.5),
        'w_ffn_up': nrm((L, D_MODEL, D_FF), D_MODEL ** -0.5),
        'w_ffn_down': nrm((L, D_FF, D_MODEL), D_FF ** -0.5),
    }


def reference(x_prompt, x_sample, cache_k, cache_v, state_ssm_re, state_ssm_im, c_prompt, c_sample,
              w_mod, b_mod, norm1_g, norm2_g, w_in, ssm_a_re, ssm_a_im, ssm_log_dt, ssm_b_re, ssm_b_im,
              ssm_c_re, ssm_c_im, ssm_d, w_glu, b_glu, q_norm_g, k_norm_g, attn_sink, w_gate, b_gate,
              w_proj_ssm, w_proj_attn, w_out, w_ffn_gate, w_ffn_up, w_ffn_down):
    pos_prompt = jnp.arange(x_prompt.shape[1])
    pos_sample = PAST_LEN + jnp.arange(x_sample.shape[1])
    xp, xs = x_prompt, x_sample
    pk, pv, pre, pim = [], [], [], []
    sk, sv, sre, sim = [], [], [], []
    for l in range(DEPTH):
        p = {
            'w_mod': w_mod[l], 'b_mod': b_mod[l], 'norm1_g': norm1_g[l], 'norm2_g': norm2_g[l],
            'w_in': w_in[l], 'ssm_a_re': ssm_a_re[l], 'ssm_a_im': ssm_a_im[l],
            'ssm_log_dt': ssm_log_dt[l], 'ssm_b_re': ssm_b_re[l], 'ssm_b_im': ssm_b_im[l],
            'ssm_c_re': ssm_c_re[l], 'ssm_c_im': ssm_c_im[l], 'ssm_d': ssm_d[l],
            'w_glu': w_glu[l], 'b_glu': b_glu[l], 'q_norm_g': q_norm_g[l], 'k_norm_g': k_norm_g[l],
            'attn_sink': attn_sink[l], 'w_gate': w_gate[l], 'b_gate': b_gate[l],
            'w_proj_ssm': w_proj_ssm[l], 'w_proj_attn': w_proj_attn[l], 'w_out': w_out[l],
            'w_ffn_gate': w_ffn_gate[l], 'w_ffn_up': w_ffn_up[l], 'w_ffn_down': w_ffn_down[l],
        }
        xp, k_p, v_p, re_p, im_p = layer(xp, c_prompt, pos_prompt, p, None, None)
        xs, k_s, v_s, re_s, im_s = layer(xs, c_sample, pos_sample, p,
                                         (cache_k[l], cache_v[l]),
                                         (state_ssm_re[l], state_ssm_im[l]))
        pk.append(k_p); pv.append(v_p); pre.append(re_p); pim.append(im_p)
        sk.append(k_s); sv.append(v_s); sre.append(re_s); sim.append(im_s)
    return (xp, xs,
            jnp.stack(pk), jnp.stack(pv), jnp.stack(pre), jnp.stack(pim),
            jnp.stack(sk), jnp.stack(sv), jnp.stack(sre), jnp.stack(sim))
```

```python
import contextlib
import math

import numpy as np
import concourse.bass as bass
import concourse.mybir as mybir
from concourse.bass_utils import run_bass_kernel_spmd

F32 = mybir.dt.float32
BF16 = mybir.dt.bfloat16
AF = mybir.ActivationFunctionType
ALU = mybir.AluOpType
AX = mybir.AxisListType

ENGS = ("pe", "act", "dve", "pool", "sp")

D = 2048
L = 4
TP = 512
TS = 64
T = TP + TS
NGRP = 4
DFF = 5632
EPS = 1e-6
MAGIC = 12582912.0
TWO_PI = 2.0 * math.pi


class _Stop(Exception):
    pass


def chk(name):
    import os
    if os.environ.get("KSTOP", "") == name:
        raise _Stop()


class Prog:
    def __init__(self):
        self.ops = []
        self.lastw = {}
        self.readers = {}
        self.nameacc = {}
        self.aliases = {}
        self.dma_count = {}

    def alias(self, a, b):
        self.aliases.setdefault(a, set()).add(b)
        self.aliases.setdefault(b, set()).add(a)

    def add(self, eng, fn, R=(), W=(), dma=None):
        oid = len(self.ops)
        deps = set()
        names = set()
        for k in R:
            names.add(k[0])
            if k in self.lastw:
                deps.add(self.lastw[k])
        for k in W:
            names.add(k[0])
            if k in self.lastw:
                deps.add(self.lastw[k])
            for r in self.readers.get(k, ()):
                deps.add(r)
        for n in names:
            for o in self.aliases.get(n, ()):
                for v in self.nameacc.get(o, {}).values():
                    deps.add(v)
        op = dict(id=oid, eng=eng, fn=fn, deps=deps, dma=dma, inc=False)
        if dma is not None:
            self.dma_count[dma] = self.dma_count.get(dma, 0) + 1
            op["dma_val"] = 16 * self.dma_count[dma]
        self.ops.append(op)
        for k in W:
            self.lastw[k] = oid
            self.readers[k] = []
        for k in R:
            if k not in W:
                self.readers.setdefault(k, []).append(oid)
        for n in names:
            self.nameacc.setdefault(n, {})[(eng, dma)] = oid
        return oid

    def emit(self, nc):
        ops = self.ops
        needed = set()
        for op in ops:
            for d in op["deps"]:
                p = ops[d]
                if p["dma"] is None:
                    if p["eng"] == "pe" and op["eng"] == "pe":
                        continue
                    needed.add(d)
        cnt = {e: 0 for e in ENGS}
        for op in ops:
            if op["dma"] is None and op["id"] in needed:
                cnt[op["eng"]] += 1
                op["inc"] = True
                op["val"] = cnt[op["eng"]]
        waited = {}
        for op in ops:
            w = {}
            for d in op["deps"]:
                p = ops[d]
                if p["dma"] is not None:
                    key = ("dma", p["dma"])
                    val = p["dma_val"]
                else:
                    if p["eng"] == "pe" and op["eng"] == "pe":
                        continue
                    key = ("eng", p["eng"])
                    val = p["val"]
                if w.get(key, 0) < val:
                    w[key] = val
            mw = waited.setdefault(op["eng"], {})
            waits = []
            for key, val in w.items():
                if mw.get(key, 0) >= val:
                    continue
                mw[key] = val
                waits.append((key, val))
            op["waits"] = waits
        dma_keys = sorted(self.dma_count.keys(), key=str)
        if _os.environ.get("KSIM"):
            semv = {}
            per = {e: [op for op in ops if op["eng"] == e] for e in ENGS}
            ptr = {e: 0 for e in ENGS}
            prog = True
            while prog:
                prog = False
                for e in ENGS:
                    while ptr[e] < len(per[e]):
                        op = per[e][ptr[e]]
                        if all(semv.get(k, 0) >= v for k, v in op["waits"]):
                            if op["dma"] is not None:
                                semv[("dma", op["dma"])] = semv.get(("dma", op["dma"]), 0) + 16
                            elif op["inc"]:
                                semv[("eng", e)] = semv.get(("eng", e), 0) + 1
                            ptr[e] += 1
                            prog = True
                        else:
                            break
            for e in ENGS:
                if ptr[e] < len(per[e]):
                    op = per[e][ptr[e]]
                    print("DEADLOCK", e, ptr[e], len(per[e]), op["id"], op["waits"], {k: semv.get(k, 0) for k, _ in op["waits"]})
            print("SIM done", {e: (ptr[e], len(per[e])) for e in ENGS}, {k: v for k, v in semv.items()}, flush=True)
        with contextlib.ExitStack() as st:
            sems = {}
            for e in ENGS:
                sems[("eng", e)] = st.enter_context(nc.semaphore("s_" + e))
            for i, k in enumerate(dma_keys):
                sems[("dma", k)] = st.enter_context(nc.semaphore("d%d" % i))
            block = st.enter_context(nc.Block())
            per_eng = {e: [op for op in ops if op["eng"] == e] for e in ENGS}

            def run(engobj, lst):
                for op in lst:
                    for key, val in op["waits"]:
                        engobj.wait_ge(sems[key], val)
                    ins = op["fn"](engobj)
                    if op["dma"] is not None:
                        ins.then_inc(sems[("dma", op["dma"])], 16)
                    elif op["inc"]:
                        ins.then_inc(sems[("eng", op["eng"])], 1)

            @block.tensor
            def _(e):
                run(e, per_eng["pe"])

            @block.scalar
            def _(e):
                run(e, per_eng["act"])

            @block.vector
            def _(e):
                run(e, per_eng["dve"])

            @block.gpsimd
            def _(e):
                run(e, per_eng["pool"])

            @block.sync
            def _(e):
                run(e, per_eng["sp"])
        return cnt


import os as _os
WL = int(_os.environ.get("KWL", L))
IN_SHAPES = {
    "xp": [2048, D], "xs": [2, TS, D],
    "ck": [L, 2, 128, 256], "cv": [L, 2, 128, 256],
    "st_re": [L, 2, 128, 32], "st_im": [L, 2, 128, 32],
    "cT": [128, 16, 3],
    "w_mod": [WL, D, 6 * D], "w_in": [WL, D, 2560], "w_glu": [WL, 1024, 1024],
    "w_gate": [WL, D, 2 * D], "w_proj_ssm": [WL, 1024, D], "w_proj_attn": [WL, 1024, D],
    "w_out": [WL, D, D], "w_ffn_gate": [WL, D, DFF], "w_ffn_up": [WL, D, DFF],
    "w_ffn_down": [WL, DFF, D],
    "b_modT": [128, L, 96], "n1T": [128, L, 16], "n2T": [128, L, 16],
    "b_gateT": [128, L, 32], "b_gluT": [128, L, 8], "dT": [128, L, 8],
    "gq": [128, L, 64], "gk": [128, L, 64], "sinkT": [128, L, 16],
    "are_b": [128, L, 4096], "aim_b": [128, L, 4096], "ldt_b": [128, L, 64],
    "are_sm": [128, L, 32], "aim_sm": [128, L, 32], "ldt_sm": [128, L, 32],
    "BBre": [L, 128, 8, 512], "BBim": [L, 128, 8, 512],
    "CCre": [L, 128, 32, 128], "CCim": [L, 128, 32, 128],
    "ropec": [NGRP, 128, 5, 8], "ropes": [NGRP, 128, 5, 8],
    "tri": [128, 128], "tp1": [128, 128], "negsp1": [128, 1],
}
OUT_SHAPES = {
    "yp": [2048, D], "ys": [2, TS, D],
    "pk": [L, 128, 256], "pv": [L, 128, 256],
    "pre": [L, 32, 128], "pim": [L, 32, 128],
    "sk": [L, 2, 128, 256], "sv": [L, 2, 128, 256],
    "sre": [L, 2, 32, 128], "sim": [L, 2, 32, 128],
}

HALVES = [(0, 288), (288, 576)]
SEGS = [[(0, 288, 0)], [(288, 512, 0), (512, 576, 1)]]
ALLSEG = [(0, 512, 0), (512, 576, 1)]


def build_program(n_layers=L, n_groups=NGRP):
    nc = bass.Bass("TRN2", target_bir_lowering=False)
    P = Prog()
    din = {k: nc.dram_tensor(k, s, F32, kind="ExternalInput").ap() for k, s in IN_SHAPES.items()}
    dout = {k: nc.dram_tensor(k, s, F32, kind="ExternalOutput").ap() for k, s in OUT_SHAPES.items()}
    st = contextlib.ExitStack()

    def sb(name, shape, dt=F32):
        return st.enter_context(nc.sbuf_tensor("sb_" + name, shape, dt))

    xT = sb("xT", [128, 16, T])
    modT = sb("modT", [128, L, 96, 3])
    scp = sb("scp", [128, L, 2, 16, 3])
    ones_b = sb("ones_b", [128, 128], BF16)
    ident_f = sb("ident_f", [128, 128])
    ident_b = sb("ident_b", [128, 128], BF16)
    tri_b = sb("tri_b", [128, 128], BF16)
    tp1 = sb("tp1", [128, 128])
    negsp1 = sb("negsp1", [128, 1])
    cTb = sb("cTb", [128, 16, 3], BF16)
    b_modT = sb("b_modT", [128, L, 96])
    n1T = sb("n1T", [128, L, 16])
    n2T = sb("n2T", [128, L, 16])
    b_gateT = sb("b_gateT", [128, L, 32])
    b_gluT = sb("b_gluT", [128, L, 8])
    dTs = sb("dTs", [128, L, 8])
    gq = sb("gq", [128, L, 64])
    gk = sb("gk", [128, L, 64])
    esink = sb("esink", [128, L, 16])
    are_sm = sb("are_sm", [128, L, 32])
    aim_sm = sb("aim_sm", [128, L, 32])
    ldt_sm = sb("ldt_sm", [128, L, 32])
    ldt_b = sb("ldt_b", [128, L, 64])
    haloKT = sb("haloKT", [64, L, 4, 128], BF16)
    haloV = sb("haloV", [128, L, 4, 65], BF16)
    h0p = sb("h0p", [128, L, 2, 32])
    hends = sb("hends", [128, 2, 32])
    h0s = sb("h0s", [128, 2, 32])
    ropec = sb("ropec", [128, 5, 8])
    ropes = sb("ropes", [128, 5, 8])
    rstd = sb("rstd", [128, T])
    ntmp = sb("ntmp", [128, 2, T])
    ring = sb("ring", [128, 5, 4096], BF16)
    qsq = sb("qsq", [128, 256])
    qn = sb("qn", [128, 2, 256])
    qss = sb("qss", [128, 2, 4])
    qrt = sb("qrt", [128, 4, 4, 8])
    qtm = sb("qtm", [128, 2, 256], BF16)
    vf = sb("vf", [128, 2, 256])
    ckst = sb("ckst", [128, 256])
    cvst = sb("cvst", [128, 256])
    ckb = sb("ckb", [128, 256], BF16)
    pT = sb("pT", [128, 2, 2, 256], BF16)
    adn = sb("adn", [64, 2, 256])
    ARENA = 36864 + 53248 + 512
    arena = sb("arena", [128, ARENA // 2], BF16)
    ps = st.enter_context(nc.psum_tensor("ps", [128, 8, 512], F32))

    X0 = 36864
    arena_bufs = {}

    def av(name, off, shape, dt, parts=128):
        nel = int(np.prod(shape[1:]))
        esz = 2 if dt == BF16 else 4
        nbytes = nel * esz
        assert off % 4 == 0 and off + nbytes <= ARENA, (name, off, nbytes)
        v = arena[0:parts, off // 2: (off + nbytes) // 2]
        if dt != BF16:
            v = v.bitcast(dt)
        if len(shape) == 3:
            v = v.rearrange("p (a b) -> p a b", a=shape[1])
        elif len(shape) == 4:
            v = v.rearrange("p (a b c) -> p a b c", a=shape[1], b=shape[2])
        for n2, (o2, b2) in arena_bufs.items():
            if off < o2 + b2 and o2 < off + nbytes:
                P.alias(name, n2)
        arena_bufs[name] = (off, nbytes)
        return v

    hT = av("hT", 0, [128, 16, T], BF16)
    attnT = av("attnT", 18432, [64, 16, T], BF16, parts=64)
    sq = av("sq", X0, [128, 16, T], BF16)
    actT = av("actT", X0, [128, 22, T], BF16)
    xstage = av("xstage", X0, [128, 2, 2048], F32)
    uT = av("uT", X0, [128, 8, T], BF16)
    QT = av("QT", X0 + 9216, [64, 16, T], BF16, parts=64)
    KT = av("KT", X0 + 27648, [64, 4, 832], BF16, parts=64)
    Vaug = av("Vaug", X0 + 34304, [128, 7, 4, 65], BF16)
    zT = av("zT", X0 + 9216, [128, 8, T], BF16)
    ssm_outT = av("ssm_outT", X0 + 18432, [128, 8, T], BF16)
    gaT = av("gaT", X0, [128, 16, T], BF16)
    mixedT = av("mixedT", X0 + 27648, [128, 16, T], BF16)
    SW = X0 + 18432
    SmF = av("SmF", SW, [128, 2, 512], F32)
    SpT = av("SpT", SW + 4096, [128, 2, 4, 128], F32)
    BBs = av("BBs", SW + 8192, [128, 2, 512], BF16)
    CCs = av("CCs", SW + 10240, [128, 2, 4, 128], BF16)
    Wt = av("Wt", SW + 12288, [128, 2, 512], BF16)
    Gs = av("Gs", SW + 14336, [128, 2, 4, 128], F32)
    hS = av("hS", SW + 18432, [128, 2, 4, 128], BF16)
    tmpA = av("T", SW + 20480, [128, 4, 512], F32)
    tabA = av("tabA", SW + 28672, [128, 2, 512], F32)
    ypre = av("ypre", SW + 32768, [128, T], F32)
    assert SW + 32768 + T * 4 <= ARENA

    state = dict(bank=0, wslot=0)

    def next_bank():
        b = state["bank"]
        state["bank"] = (b + 1) % 7
        return b

    def dve(fn, R, W):
        return P.add("dve", fn, R, W)

    def act(fn, R, W):
        return P.add("act", fn, R, W)

    def pe(fn, R, W):
        return P.add("pe", fn, R, W)

    def wnext(src, parts, kt, ncols):
        s = state["wslot"]
        state["wslot"] = (s + 1) % 5
        view = ring[0:parts, s, 0:kt * ncols].rearrange("p (k n) -> p k n", k=kt)
        P.add("pool", lambda e: e.dma_start(out=view, in_=src), W=[("ring", s)], dma=("ring", s))
        return view, ("ring", s)

    def wsrc(w, l, k0, kt, c0, ncols, kp=128):
        return w[l, k0 * kp:(k0 + kt) * kp, c0:c0 + ncols].rearrange("(k p) n -> p k n", p=kp)

    def proj_fm(w, l, ktot, c0, ncols, rhs_fn, rhs_keys, evac, colsplit=HALVES, kp=128):
        kt_max = 16
        tile_cols = 4096 // min(ktot, kt_max)
        tile_cols = min(tile_cols, 512, ncols)
        kchunks = [(k0, min(kt_max, ktot - k0)) for k0 in range(0, ktot, kt_max)]
        for cb in range(c0, c0 + ncols, tile_cols):
            nm = tile_cols // 128
            groups = [(m, hi) for m in range(nm) for hi in range(len(colsplit))]
            if len(kchunks) == 1:
                k0, kt = kchunks[0]
                view, wkey = wnext(wsrc(w, l, k0, kt, cb, tile_cols, kp), kp, kt, tile_cols)
                for (m, hi) in groups:
                    b = next_bank()
                    a0, a1 = colsplit[hi]

                    def mm(e, view=view, m=m, a0=a0, a1=a1, b=b, kt=kt):
                        for k in range(kt):
                            ins = e.matmul(ps[:, b, 0:a1 - a0], lhsT=view[:, k, m * 128:(m + 1) * 128],
                                           rhs=rhs_fn(k)[:, a0:a1], start=(k == 0), stop=(k == kt - 1))
                        return ins
                    pe(mm, R=[wkey] + rhs_keys, W=[("ps", b)])
                    evac((cb - c0) // 128 + m, hi, b)
            else:
                banks = {gk_: next_bank() for gk_ in groups}
                for (k0, kt) in kchunks:
                    view, wkey = wnext(wsrc(w, l, k0, kt, cb, tile_cols, kp), kp, kt, tile_cols)
                    for (m, hi) in groups:
                        b = banks[(m, hi)]
                        a0, a1 = colsplit[hi]

                        def mm(e, view=view, m=m, a0=a0, a1=a1, b=b, kt=kt, k0=k0):
                            for k in range(kt):
                                ins = e.matmul(ps[:, b, 0:a1 - a0], lhsT=view[:, k, m * 128:(m + 1) * 128],
                                               rhs=rhs_fn(k0 + k)[:, a0:a1], start=(k0 + k == 0),
                                               stop=(k0 + k == ktot - 1))
                            return ins
                        pe(mm, R=[wkey] + rhs_keys, W=[("ps", b)])
                for (m, hi) in groups:
                    evac((cb - c0) // 128 + m, hi, banks[(m, hi)])

    def small_load(dst, src, eng="sp"):
        P.add(eng, lambda e: e.dma_start(out=dst, in_=src), W=[("small", 0)], dma="small" + eng)

    for dst, nm in [(b_modT, "b_modT"), (n1T, "n1T"), (n2T, "n2T"), (b_gateT, "b_gateT"), (b_gluT, "b_gluT"),
                    (dTs, "dT"), (gq, "gq"), (gk, "gk"), (esink, "sinkT"), (are_sm, "are_sm"),
                    (aim_sm, "aim_sm"), (ldt_sm, "ldt_sm"), (ldt_b, "ldt_b"), (tp1, "tp1"), (negsp1, "negsp1")]:
        small_load(dst[:], din[nm])
    P.add("pool", lambda e: e.dma_start(out=tri_b[:], in_=din["tri"]), W=[("tri_b", 0)], dma="tri")
    P.add("pool", lambda e: e.dma_start(out=cTb[:], in_=din["cT"]), W=[("cTb", 0)], dma="cTb")
    dve(lambda e: e.memset(ones_b[:], 1.0), [], [("ones_b", 0)])
    dve(lambda e: e.memset(ident_f[:], 0.0), [], [("ident_f", 0)])
    P.add("pool", lambda e: e.affine_select(out=ident_f[:], in_=ident_f[:], pattern=[[-1, 128]],
                                            compare_op=ALU.not_equal, fill=1.0, base=0, channel_multiplier=1),
          R=[("ident_f", 0)], W=[("ident_f", 0)])
    dve(lambda e: e.tensor_copy(out=ident_b[:], in_=ident_f[:]), [("ident_f", 0)], [("ident_b", 0)])
    act(lambda e: e.activation(out=esink[:], in_=esink[:], func=AF.Exp), [("small", 0)], [("esink", 0)])
    dve(lambda e: e.memset(h0p[:], 0.0), [], [("h0p", l_) for l_ in range(L)])

    for l in range(n_layers):
        def ev_mod(mi, hi, b, l=l):
            act(lambda e: e.activation(out=modT[:, l, mi, :], in_=ps[:, b, 0:3], func=AF.Identity,
                                       bias=b_modT[:, l, mi:mi + 1], scale=1.0),
                [("ps", b), ("small", 0)], [("modT", l)])
        proj_fm(din["w_mod"], l, 16, 0, 6 * D, lambda k: cTb[:, k, :], [("cTb", 0)], ev_mod, colsplit=[(0, 3)])
        for which, (nT, off) in enumerate([(n1T, 16), (n2T, 64)]):
            dve(lambda e, l=l, which=which, off=off: e.tensor_scalar(
                out=scp[:, l, which], in0=modT[:, l, off:off + 16, :], scalar1=1.0, scalar2=None, op0=ALU.add),
                [("modT", l)], [("scp", l)])
            dve(lambda e, l=l, which=which, nT=nT: e.tensor_tensor(
                out=scp[:, l, which], in0=scp[:, l, which],
                in1=nT[:, l, :].unsqueeze(2).broadcast_to([128, 16, 3]), op=ALU.mult),
                [("scp", l), ("small", 0)], [("scp", l)])

    def hkeys():
        return [("hT", k, hi) for k in range(16) for hi in range(2)]

    def norm_phase(l, which, seqs):
        shoff = 0 if which == 0 else 48
        act(lambda e: e.activation(out=sq[:], in_=xT[:], func=AF.Square),
            [("xT", k, hi) for k in range(16) for hi in range(2)], [("sq", 0)])
        for hi, (a0, a1) in enumerate(HALVES):
            b = next_bank()

            def mm(e, a0=a0, a1=a1, b=b):
                for k in range(16):
                    ins = e.matmul(ps[:, b, 0:a1 - a0], lhsT=ones_b[:], rhs=sq[:, k, a0:a1],
                                   start=(k == 0), stop=(k == 15))
                return ins
            pe(mm, [("sq", 0), ("ones_b", 0)], [("ps", b)])
            act(lambda e, a0=a0, a1=a1, b=b: e.activation(out=rstd[:, a0:a1], in_=ps[:, b, 0:a1 - a0],
                                                         func=AF.Sqrt, bias=EPS, scale=1.0 / D),
                [("ps", b)], [("rstd", hi)])
            dve(lambda e, a0=a0, a1=a1: e.reciprocal(out=rstd[:, a0:a1], in_=rstd[:, a0:a1]),
                [("rstd", hi)], [("rstd", hi)])
        for k in range(16):
            tb = k % 2
            for (a0, a1, s) in ALLSEG:
                sidx = seqs[s]
                dve(lambda e, k=k, a0=a0, a1=a1, sidx=sidx, tb=tb: e.scalar_tensor_tensor(
                    out=ntmp[:, tb, a0:a1], in0=xT[:, k, a0:a1], scalar=scp[:, l, which, k, sidx:sidx + 1],
                    in1=rstd[:, a0:a1], op0=ALU.mult, op1=ALU.mult),
                    [("xT", k, 0), ("xT", k, 1), ("rstd", 0), ("rstd", 1), ("scp", l)], [("ntmp", tb, s)])
                act(lambda e, k=k, a0=a0, a1=a1, sidx=sidx, tb=tb: e.activation(
                    out=hT[:, k, a0:a1], in_=ntmp[:, tb, a0:a1], func=AF.Identity,
                    bias=modT[:, l, shoff + k, sidx:sidx + 1], scale=1.0),
                    [("ntmp", tb, s), ("modT", l)], [("hT", k, 0), ("hT", k, 1)])

    def layer(g, l):
        sidx_s = g % 2
        seqs = [0, 1 + sidx_s]
        write_s = g < 2
        last_g = (g == n_groups - 1)
        chk("xload")
        norm_phase(l, 0, seqs)
        chk("norm1")

        def ev_u(mi, hi, b):
            a0, a1 = HALVES[hi]
            act(lambda e: e.copy(out=uT[:, mi, a0:a1], in_=ps[:, b, 0:a1 - a0]), [("ps", b)], [("uT", mi, hi)])
        proj_fm(din["w_in"], l, 16, 0, 1024, lambda k: hT[:, k, :], hkeys(), ev_u)

        chk("uproj")
        dve(lambda e: e.memset(Vaug[:, :, :, 64:65], 1.0), [], [("Vaug", t_) for t_ in range(7)])
        if g > 0:
            dve(lambda e: e.tensor_copy(out=KT[:, :, 0:128], in_=haloKT[:, l]), [("haloKT", l)], [("KT", 0)])
            dve(lambda e: e.tensor_copy(out=Vaug[:, 0, :, 0:64], in_=haloV[:, l, :, 0:64]), [("haloV", l)], [("Vaug", 0)])
        P.add("sp", lambda e: e.dma_start(out=ckst[:], in_=din["ck"][l, sidx_s]), W=[("ckst", 0)], dma="ckst")
        P.add("sp", lambda e: e.dma_start(out=cvst[:], in_=din["cv"][l, sidx_s]), W=[("cvst", 0)], dma="cvst")
        if write_s:
            P.add("sp", lambda e: e.dma_start(out=dout["sk"][l, sidx_s, 0:64, :], in_=din["ck"][l, sidx_s, 64:128, :]),
                  W=[("o_skc", 0)], dma="o_skc")
            P.add("sp", lambda e: e.dma_start(out=dout["sv"][l, sidx_s, 0:64, :], in_=din["cv"][l, sidx_s, 64:128, :]),
                  W=[("o_svc", 0)], dma="o_svc")
        dve(lambda e: e.tensor_copy(out=ckb[:], in_=ckst[:]), [("ckst", 0)], [("ckb", 0)])
        dve(lambda e: e.tensor_copy(out=Vaug[:, 5, :, 0:64], in_=cvst[:].rearrange("p (h d) -> p h d", h=4)),
            [("cvst", 0)], [("Vaug", 5)])
        b = next_bank()
        psb = ps[:, b, :].bitcast(BF16)

        def tr_ck(e, psb=psb):
            for h in range(4):
                ins = e.transpose(out=psb[0:64, h * 128:(h + 1) * 128], in_=ckb[:, h * 64:(h + 1) * 64],
                                  identity=ident_b[:])
            return ins
        pe(tr_ck, [("ckb", 0), ("ident_b", 0)], [("ps", b)])
        act(lambda e, psb=psb: e.copy(out=KT[:, :, 640:768], in_=psb[0:64, 0:512].rearrange("p (h t) -> p h t", h=4)),
            [("ps", b)], [("KT", 5)])

        chk("halo")
        for wt in range(6):
            view, wkey = wnext(wsrc(din["w_in"], l, 0, 16, 1024 + 256 * wt, 256), 128, 16, 256)
            for tt in range(5):
                rows = 128 if tt < 4 else 64
                c0 = tt * 128
                b = next_bank()

                def mm(e, view=view, c0=c0, rows=rows, b=b):
                    for k in range(16):
                        ins = e.matmul(ps[0:rows, b, 0:256], lhsT=hT[:, k, c0:c0 + rows], rhs=view[:, k, :],
                                       start=(k == 0), stop=(k == 15))
                    return ins
                pe(mm, [wkey] + hkeys(), [("ps", b)])
                src = ps[0:rows, b, 0:256]
                _qs = _os.environ.get("KQSUB", "")
                state["qit"] = state.get("qit", 0) + 1
                if state["qit"] > int(_os.environ.get("KQKV", "1000")):
                    raise _Stop()
                if _qs == "mm":
                    continue
                if wt == 5:
                    vt = tt + 1 if tt < 4 else 6
                    dve(lambda e, src=src, rows=rows, vt=vt: e.tensor_copy(
                        out=Vaug[0:rows, vt, :, 0:64], in_=src.rearrange("p (h d) -> p h d", h=4)),
                        [("ps", b)], [("Vaug", vt)])
                    need_out = (tt == 3 and last_g) or (tt == 4 and write_s)
                    if need_out:
                        vb = tt % 2
                        dve(lambda e, src=src, rows=rows, vb=vb: e.tensor_copy(out=vf[0:rows, vb, :], in_=src),
                            [("ps", b)], [("vf", vb)])
                        dst = dout["pv"][l] if tt == 3 else dout["sv"][l, sidx_s, 64:128, :]
                        P.add("sp", lambda e, dst=dst, rows=rows, vb=vb: e.dma_start(out=dst, in_=vf[0:rows, vb, :]),
                              R=[("vf", vb)], W=[("o_v", vb)], dma=("o_v", vb))
                    continue
                isk = (wt == 4)
                gtab = gk if isk else gq
                qb = (wt * 5 + tt) % 2
                act(lambda e, src=src, rows=rows: e.activation(out=qsq[0:rows, :], in_=src, func=AF.Square),
                    [("ps", b)], [("qsq", 0)])
                dve(lambda e, rows=rows, qb=qb: e.tensor_reduce(
                    out=qss[0:rows, qb, :], in_=qsq[0:rows, :].rearrange("p (h d) -> p h d", h=4), axis=AX.X, op=ALU.add),
                    [("qsq", 0)], [("qss", qb)])
                act(lambda e, rows=rows, qb=qb: e.activation(out=qss[0:rows, qb, :], in_=qss[0:rows, qb, :],
                                                            func=AF.Sqrt, bias=EPS, scale=1.0 / 64),
                    [("qss", qb)], [("qss", qb)])
                dve(lambda e, rows=rows, qb=qb: e.reciprocal(out=qss[0:rows, qb, :], in_=qss[0:rows, qb, :]),
                    [("qss", qb)], [("qss", qb)])
                dve(lambda e, src=src, rows=rows, qb=qb: e.tensor_tensor(
                    out=qn[0:rows, qb, :].rearrange("p (h d) -> p h d", h=4),
                    in0=src.rearrange("p (h d) -> p h d", h=4),
                    in1=qss[0:rows, qb, :].unsqueeze(2).broadcast_to([rows, 4, 64]), op=ALU.mult),
                    [("ps", b), ("qss", qb)], [("qn", qb)])
                dve(lambda e, rows=rows, qb=qb, gtab=gtab: e.tensor_tensor(
                    out=qn[0:rows, qb, :].rearrange("p (h d) -> p h d", h=4),
                    in0=qn[0:rows, qb, :].rearrange("p (h d) -> p h d", h=4),
                    in1=gtab[0:rows, l, :].unsqueeze(1).broadcast_to([rows, 4, 64]), op=ALU.mult),
                    [("qn", qb), ("small", 0)], [("qn", qb)])
                if _qs == "norm":
                    continue
                qv = qn[0:rows, qb, :].rearrange("p (h d) -> p h d", h=4)
                cosb = ropec[0:rows, tt, :].unsqueeze(1).broadcast_to([rows, 4, 8])
                sinb = ropes[0:rows, tt, :].unsqueeze(1).broadcast_to([rows, 4, 8])
                for i_, (xa, tb_) in enumerate([(qv[:, :, 0:8], cosb), (qv[:, :, 8:16], sinb),
                                                (qv[:, :, 8:16], cosb), (qv[:, :, 0:8], sinb)]):
                    dve(lambda e, xa=xa, tb_=tb_, i_=i_, rows=rows: e.tensor_tensor(
                        out=qrt[0:rows, i_], in0=xa, in1=tb_, op=ALU.mult),
                        [("qn", qb), ("rope", 0)], [("qrt", i_)])
                dve(lambda e, qv=qv, rows=rows: e.tensor_tensor(out=qv[:, :, 0:8], in0=qrt[0:rows, 0],
                                                               in1=qrt[0:rows, 1], op=ALU.subtract),
                    [("qrt", 0), ("qrt", 1)], [("qn", qb)])
                dve(lambda e, qv=qv, rows=rows: e.tensor_tensor(out=qv[:, :, 8:16], in0=qrt[0:rows, 2],
                                                               in1=qrt[0:rows, 3], op=ALU.add),
                    [("qrt", 2), ("qrt", 3)], [("qn", qb)])
                act(lambda e, rows=rows, qb=qb: e.copy(out=qtm[0:rows, qb, :], in_=qn[0:rows, qb, :]),
                    [("qn", qb)], [("qtm", qb)])
                if _qs == "rope":
                    continue
                if isk:
                    need_out = ((tt == 3 and last_g) or (tt == 4 and write_s)) and not _os.environ.get("KNOKDMA")
                    if need_out:
                        dst = dout["pk"][l] if tt == 3 else dout["sk"][l, sidx_s, 64:128, :]
                        P.add("sp", lambda e, dst=dst, rows=rows, qb=qb: e.dma_start(out=dst, in_=qn[0:rows, qb, :]),
                              R=[("qn", qb)], W=[("o_k", qb)], dma=("o_k", qb))
                if _qs == "kdma":
                    continue
                b2 = next_bank()
                psb2 = ps[:, b2, :].bitcast(BF16)

                def trq(e, psb2=psb2, rows=rows, qb=qb):
                    for h in range(4):
                        ins = e.transpose(out=psb2[0:64, h * 128:h * 128 + rows],
                                          in_=qtm[0:rows, qb, h * 64:(h + 1) * 64], identity=ident_b[0:rows, 0:rows])
                    return ins
                pe(trq, [("qtm", qb), ("ident_b", 0)], [("ps", b2)])
                pv_ = psb2[0:64, 0:512].rearrange("p (h t) -> p h t", h=4)[:, :, 0:rows]
                if isk:
                    kc0 = 128 + c0 if tt < 4 else 768
                    kkey = ("KT", tt + 1 if tt < 4 else 6)
                    act(lambda e, pv_=pv_, kc0=kc0, rows=rows: e.copy(out=KT[:, :, kc0:kc0 + rows], in_=pv_),
                        [("ps", b2)], [kkey])
                else:
                    act(lambda e, pv_=pv_, c0=c0, rows=rows, wt=wt: e.copy(
                        out=QT[:, 4 * wt:4 * wt + 4, c0:c0 + rows], in_=pv_),
                        [("ps", b2)], [("QT", wt, tt)])

        chk("qkv")
        if not last_g:
            dve(lambda e: e.tensor_copy(out=haloKT[:, l], in_=KT[:, :, 512:640]), [("KT", 4)], [("haloKT", l)])
            dve(lambda e: e.tensor_copy(out=haloV[:, l, :, 0:64], in_=Vaug[:, 4, :, 0:64]), [("Vaug", 4)], [("haloV", l)])

        chunks = [("p", lc) for lc in range(8)] + [("s", 0)]
        it = 0
        for (kind, lc) in chunks:
            if kind == "p":
                qc0 = 64 * lc
                tA, tB = lc // 2, lc // 2 + 1
                kcA, kcB = 128 * tA, 128 * tB
                odd = lc % 2
                skipA = (g == 0 and lc < 2)
                tt_q = lc // 2
            else:
                qc0 = 512
                tA, tB = 5, 6
                kcA, kcB = 640, 768
                odd = 0
                skipA = False
                tt_q = 4
            rA = (64, 128) if odd else (0, 128)
            rB = (0, 128) if odd else (0, 64)
            for h in range(4):
                pb = it % 2
                it += 1
                bS = next_bank()
                qkeys = [("QT", h, tt_q)]
                parts = []
                if not skipA:
                    parts.append((0, tA, kcA, rA, 128))
                parts.append((1, tB, kcB, rB, rB[1]))
                for (slot, tX, kc, rr_, mrows) in parts:
                    pe(lambda e, slot=slot, kc=kc, mrows=mrows, bS=bS, h=h, qc0=qc0: e.matmul(
                        ps[0:mrows, bS, slot * 256:(slot + 1) * 256], lhsT=KT[:, h, kc:kc + mrows],
                        rhs=QT[:, 4 * h:4 * h + 4, qc0:qc0 + 64], start=True, stop=True),
                        [("KT", tX)] + qkeys, [("ps", bS)])
                    act(lambda e, slot=slot, mrows=mrows, bS=bS, pb=pb: e.activation(
                        out=pT[0:mrows, pb, slot, :], in_=ps[0:mrows, bS, slot * 256:(slot + 1) * 256],
                        func=AF.Exp, scale=0.125),
                        [("ps", bS)], [("pT", pb, slot)])
                bO = next_bank()

                def pvmm(e, parts=parts, bO=bO, pb=pb, h=h):
                    n = len(parts)
                    for i_, (slot, tX, kc, rr_, mrows) in enumerate(parts):
                        e.matmul(ps[0:64, bO, 0:256], lhsT=Vaug[rr_[0]:rr_[1], tX, h, 0:64],
                                 rhs=pT[rr_[0]:rr_[1], pb, slot, :], start=(i_ == 0), stop=(i_ == n - 1))
                    for i_, (slot, tX, kc, rr_, mrows) in enumerate(parts):
                        ins = e.matmul(ps[0:64, bO, 256:512], lhsT=ones_b[rr_[0]:rr_[1], 0:64],
                                       rhs=pT[rr_[0]:rr_[1], pb, slot, :], start=(i_ == 0), stop=(i_ == n - 1))
                    return ins
                pe(pvmm, [("pT", pb, s_[0]) for s_ in parts] + [("Vaug", s_[1]) for s_ in parts] + [("ones_b", 0)],
                   [("ps", bO)])
                dve(lambda e, bO=bO, pb=pb, h=h: e.tensor_tensor(
                    out=adn[:, pb, :].rearrange("p (r q) -> p r q", r=4),
                    in0=ps[0:64, bO, 256:512].rearrange("p (r q) -> p r q", r=4),
                    in1=esink[0:64, l, 4 * h:4 * h + 4].unsqueeze(2).broadcast_to([64, 4, 64]), op=ALU.add),
                    [("ps", bO), ("esink", 0)], [("adn", pb)])
                dve(lambda e, pb=pb: e.reciprocal(out=adn[:, pb, :], in_=adn[:, pb, :]), [("adn", pb)], [("adn", pb)])
                dve(lambda e, bO=bO, pb=pb, h=h, qc0=qc0: e.tensor_tensor(
                    out=attnT[:, 4 * h:4 * h + 4, qc0:qc0 + 64],
                    in0=ps[0:64, bO, 0:256].rearrange("p (r q) -> p r q", r=4),
                    in1=adn[:, pb, :].rearrange("p (r q) -> p r q", r=4), op=ALU.mult),
                    [("ps", bO), ("adn", pb)], [("attnT", h, qc0)])

        chk("attn")
        ssm_phase(g, l, sidx_s, write_s, last_g)
        chk("ssm")

        def ev_glu(mi, hi, b):
            a0, a1 = HALVES[hi]
            act(lambda e: e.activation(out=ntmp[:, hi, 0:a1 - a0], in_=ps[:, b, 0:a1 - a0], func=AF.Sigmoid,
                                       bias=b_gluT[:, l, mi:mi + 1], scale=1.0),
                [("ps", b), ("small", 0)], [("ntmp", hi, 0), ("ntmp", hi, 1)])
            dve(lambda e: e.tensor_tensor(out=ssm_outT[:, mi, a0:a1], in0=ntmp[:, hi, 0:a1 - a0],
                                          in1=zT[:, mi, a0:a1], op=ALU.mult),
                [("ntmp", hi, 0), ("ntmp", hi, 1), ("zT", mi)], [("ssm_outT", mi, hi)])
        proj_fm(din["w_glu"], l, 8, 0, 1024, lambda k: zT[:, k, :], [("zT", k) for k in range(8)], ev_glu)

        chk("glu")
        def ev_gate(boff):
            def ev(mi, hi, b):
                a0, a1 = HALVES[hi]
                act(lambda e: e.activation(out=gaT[:, mi, a0:a1], in_=ps[:, b, 0:a1 - a0], func=AF.Sigmoid,
                                           bias=b_gateT[:, l, boff + mi:boff + mi + 1], scale=1.0),
                    [("ps", b), ("small", 0)], [("gaT", mi, hi)])
            return ev
        proj_fm(din["w_gate"], l, 16, 0, D, lambda k: hT[:, k, :], hkeys(), ev_gate(0))

        def ev_ps(mi, hi, b):
            a0, a1 = HALVES[hi]
            dve(lambda e: e.tensor_tensor(out=mixedT[:, mi, a0:a1], in0=ps[:, b, 0:a1 - a0], in1=gaT[:, mi, a0:a1],
                                          op=ALU.mult), [("ps", b), ("gaT", mi, hi)], [("mixedT", mi, hi)])
        proj_fm(din["w_proj_ssm"], l, 8, 0, D, lambda k: ssm_outT[:, k, :],
                [("ssm_outT", k, hi) for k in range(8) for hi in range(2)], ev_ps)
        proj_fm(din["w_gate"], l, 16, D, D, lambda k: hT[:, k, :], hkeys(), ev_gate(16))

        def ev_pa(mi, hi, b):
            a0, a1 = HALVES[hi]
            dve(lambda e: e.tensor_tensor(out=ntmp[:, hi, 0:a1 - a0], in0=ps[:, b, 0:a1 - a0], in1=gaT[:, mi, a0:a1],
                                          op=ALU.mult), [("ps", b), ("gaT", mi, hi)], [("ntmp", hi, 0), ("ntmp", hi, 1)])
            dve(lambda e: e.tensor_tensor(out=mixedT[:, mi, a0:a1], in0=ntmp[:, hi, 0:a1 - a0],
                                          in1=mixedT[:, mi, a0:a1], op=ALU.add),
                [("ntmp", hi, 0), ("ntmp", hi, 1), ("mixedT", mi, hi)], [("mixedT", mi, hi)])
        akeys = [("attnT", h, qc) for h in range(4) for qc in list(range(0, 512, 64)) + [512]]
        proj_fm(din["w_proj_attn"], l, 16, 0, D, lambda k: attnT[:, k, :], akeys, ev_pa, kp=64)

        chk("merge")
        def ev_res(goff):
            def ev(mi, hi, b):
                for (a0, a1, s) in SEGS[hi]:
                    sidx = seqs[s]
                    h0_ = HALVES[hi][0]
                    dve(lambda e, a0=a0, a1=a1, sidx=sidx, h0_=h0_: e.scalar_tensor_tensor(
                        out=xT[:, mi, a0:a1], in0=ps[:, b, a0 - h0_:a1 - h0_],
                        scalar=modT[:, l, goff + mi, sidx:sidx + 1], in1=xT[:, mi, a0:a1],
                        op0=ALU.mult, op1=ALU.add),
                        [("ps", b), ("modT", l), ("xT", mi, hi)], [("xT", mi, hi)])
            return ev
        proj_fm(din["w_out"], l, 16, 0, D, lambda k: mixedT[:, k, :],
                [("mixedT", k, hi) for k in range(16) for hi in range(2)], ev_res(32))

        chk("outproj")
        norm_phase(l, 1, seqs)
        for sl in range(2):
            for jb in range(11):
                cb = sl * 2816 + jb * 256
                def ev_g(mi, hi, b):
                    a0, a1 = HALVES[hi]
                    act(lambda e: e.activation(out=rstd_g[hi][mi % 2][:, 0:a1 - a0], in_=ps[:, b, 0:a1 - a0],
                                               func=AF.Silu),
                        [("ps", b)], [("gs", mi % 2, hi)])
                proj_fm(din["w_ffn_gate"], l, 16, cb, 256, lambda k: hT[:, k, :], hkeys(), ev_g)

                def ev_up(mi, hi, b, jb=jb):
                    a0, a1 = HALVES[hi]
                    dve(lambda e: e.tensor_tensor(out=actT[:, 2 * jb + mi, a0:a1], in0=ps[:, b, 0:a1 - a0],
                                                  in1=rstd_g[hi][mi % 2][:, 0:a1 - a0], op=ALU.mult),
                        [("ps", b), ("gs", mi % 2, hi)], [("actT", 2 * jb + mi, hi)])
                proj_fm(din["w_ffn_up"], l, 16, cb, 256, lambda k: hT[:, k, :], hkeys(), ev_up)
            wdn = din["w_ffn_down"][:, sl * 2816:(sl + 1) * 2816, :]
            proj_fm(wdn, l, 22, 0, D, lambda k: actT[:, k, :],
                    [("actT", k, hi) for k in range(22) for hi in range(2)], ev_res(80))

    gsb = av("gs", X0 + 27648, [128, 4, 288], BF16)
    rstd_g = [[gsb[:, hi * 2 + m2, :] for m2 in range(2)] for hi in range(2)]

    def rr(out_ap, in_ap, shift, wtmp, ktmp, Rk, Wk, tmpk):
        dve(lambda e: e.tensor_scalar(out=wtmp, in0=in_ap, scalar1=float(shift), scalar2=None, op0=ALU.add),
            Rk, [tmpk[0]])
        dve(lambda e: e.tensor_scalar(out=ktmp, in0=wtmp, scalar1=1.0 / TWO_PI, scalar2=MAGIC, op0=ALU.mult, op1=ALU.add),
            [tmpk[0]], [tmpk[1]])
        dve(lambda e: e.tensor_scalar(out=ktmp, in0=ktmp, scalar1=-MAGIC, scalar2=None, op0=ALU.add),
            [tmpk[1]], [tmpk[1]])
        dve(lambda e: e.scalar_tensor_tensor(out=wtmp, in0=ktmp, scalar=-TWO_PI, in1=wtmp, op0=ALU.mult, op1=ALU.add),
            [tmpk[0], tmpk[1]], [tmpk[0]])
        dve(lambda e: e.tensor_scalar(out=out_ap, in0=wtmp, scalar1=3.1415925, scalar2=-3.1415925, op0=ALU.min, op1=ALU.max),
            [tmpk[0]], Wk)

    dt8 = sb("dt8", [128, 8])
    zsm = sb("zsm", [128, 3, 4])
    trst = ckst[0:32, :].rearrange("p (a b) -> p a b", a=2)

    def ssm_phase(g, l, sidx_s, write_s, last_g):
        P.add("sp", lambda e: e.dma_start(out=h0s[:, 0, :], in_=din["st_re"][l, sidx_s]), W=[("h0s", 0)], dma="h0s0")
        P.add("sp", lambda e: e.dma_start(out=h0s[:, 1, :], in_=din["st_im"][l, sidx_s]), W=[("h0s", 1)], dma="h0s1")
        T0, T1, T2, T3 = (tmpA[:, i_, :] for i_ in range(4))
        for ct in range(8):
            cs = slice(ct * 512, (ct + 1) * 512)
            P.add("sp", lambda e, cs=cs: e.dma_start(out=tabA[:, 0, :], in_=din["are_b"][:, l, cs]), W=[("tabA", 0)], dma="tabA0")
            P.add("sp", lambda e, cs=cs: e.dma_start(out=tabA[:, 1, :], in_=din["aim_b"][:, l, cs]), W=[("tabA", 1)], dma="tabA1")
            P.add("pool", lambda e, ct=ct: e.dma_start(out=BBs[:, 0, :], in_=din["BBre"][l, :, ct, :]), W=[("BBs", 0)], dma="BBs0")
            P.add("pool", lambda e, ct=ct: e.dma_start(out=BBs[:, 1, :], in_=din["BBim"][l, :, ct, :]), W=[("BBs", 1)], dma="BBs1")
            P.add("pool", lambda e, ct=ct: e.dma_start(out=CCs[:, 0], in_=din["CCre"][l, :, 4 * ct:4 * ct + 4, :]), W=[("CCs", 0)], dma="CCs0")
            P.add("pool", lambda e, ct=ct: e.dma_start(out=CCs[:, 1], in_=din["CCim"][l, :, 4 * ct:4 * ct + 4, :]), W=[("CCs", 1)], dma="CCs1")
            are, aim = tabA[:, 0, :], tabA[:, 1, :]
            act(lambda e, ct=ct: e.activation(out=dt8[:], in_=ldt_b[:, l, 8 * ct:8 * ct + 8], func=AF.Exp),
                [("small", 0)], [("dt8", 0)])
            dtb = dt8[:].unsqueeze(2).broadcast_to([128, 8, 64])
            v3 = lambda a: a.rearrange("p (g q) -> p g q", g=8)
            zr, zi = SmF[:, 0, :], SmF[:, 1, :]
            dve(lambda e: e.tensor_tensor(out=v3(zr), in0=v3(are), in1=dtb, op=ALU.mult), [("tabA", 0), ("dt8", 0)], [("SmF", 0)])
            dve(lambda e: e.tensor_tensor(out=v3(zi), in0=v3(aim), in1=dtb, op=ALU.mult), [("tabA", 1), ("dt8", 0)], [("SmF", 1)])
            act(lambda e: e.activation(out=T0, in_=zr, func=AF.Exp), [("SmF", 0)], [("T", 0)])
            rr(T1, zi, 0.0, T1, T3, [("SmF", 1)], [("T", 1)], [("T", 1), ("T", 3)])
            act(lambda e: e.activation(out=T1, in_=T1, func=AF.Sin), [("T", 1)], [("T", 1)])
            rr(T2, zi, math.pi / 2, T2, T3, [("SmF", 1)], [("T", 2)], [("T", 2), ("T", 3)])
            act(lambda e: e.activation(out=T2, in_=T2, func=AF.Sin), [("T", 2)], [("T", 2)])
            dve(lambda e: e.tensor_tensor(out=T2, in0=T2, in1=T0, op=ALU.mult), [("T", 2), ("T", 0)], [("T", 2)])
            dve(lambda e: e.tensor_scalar(out=T2, in0=T2, scalar1=-1.0, scalar2=None, op0=ALU.add), [("T", 2)], [("T", 2)])
            dve(lambda e: e.tensor_tensor(out=T1, in0=T1, in1=T0, op=ALU.mult), [("T", 1), ("T", 0)], [("T", 1)])
            dve(lambda e: e.tensor_tensor(out=T0, in0=are, in1=are, op=ALU.mult), [("tabA", 0), ("T", 0)], [("T", 0)])
            dve(lambda e: e.tensor_tensor(out=T3, in0=aim, in1=aim, op=ALU.mult), [("tabA", 1)], [("T", 3)])
            dve(lambda e: e.tensor_tensor(out=T0, in0=T0, in1=T3, op=ALU.add), [("T", 0), ("T", 3)], [("T", 0)])
            dve(lambda e: e.reciprocal(out=T0, in_=T0), [("T", 0)], [("T", 0)])
            Gf = Gs[:].rearrange("p a b c -> p (a b c)")
            F0, F1 = Gf[:, 0:512], Gf[:, 512:1024]
            dve(lambda e: e.tensor_tensor(out=T3, in0=T2, in1=are, op=ALU.mult), [("T", 2), ("tabA", 0)], [("T", 3)])
            dve(lambda e: e.tensor_tensor(out=F0, in0=T1, in1=aim, op=ALU.mult), [("T", 1), ("tabA", 1)], [("Gs", 0)])
            dve(lambda e: e.tensor_tensor(out=F0, in0=F0, in1=T3, op=ALU.add), [("Gs", 0), ("T", 3)], [("Gs", 0)])
            dve(lambda e: e.tensor_tensor(out=F0, in0=F0, in1=T0, op=ALU.mult), [("Gs", 0), ("T", 0)], [("Gs", 0)])
            dve(lambda e: e.tensor_tensor(out=T3, in0=T1, in1=are, op=ALU.mult), [("T", 1), ("tabA", 0)], [("T", 3)])
            dve(lambda e: e.tensor_tensor(out=F1, in0=T2, in1=aim, op=ALU.mult), [("T", 2), ("tabA", 1)], [("Gs", 1)])
            dve(lambda e: e.tensor_tensor(out=F1, in0=T3, in1=F1, op=ALU.subtract), [("Gs", 1), ("T", 3)], [("Gs", 1)])
            dve(lambda e: e.tensor_tensor(out=F1, in0=F1, in1=T0, op=ALU.mult), [("Gs", 1), ("T", 0)], [("Gs", 1)])
            fre, fim = tabA[:, 0, :], tabA[:, 1, :]
            dve(lambda e: e.tensor_copy(out=fre, in_=F0), [("Gs", 0)], [("tabA", 0)])
            dve(lambda e: e.tensor_copy(out=fim, in_=F1), [("Gs", 1)], [("tabA", 1)])
            act(lambda e: e.activation(out=T0, in_=zr, func=AF.Exp, scale=negsp1[:, 0:1]), [("SmF", 0), ("small", 0), ("T", 0)], [("T", 0)])
            dve(lambda e: e.tensor_scalar(out=F0, in0=zi, scalar1=negsp1[:, 0:1], scalar2=None, op0=ALU.mult),
                [("SmF", 1), ("small", 0)], [("Gs", 0)])
            rr(T1, F0, 0.0, T1, T3, [("Gs", 0)], [("T", 1)], [("T", 1), ("T", 3)])
            act(lambda e: e.activation(out=T1, in_=T1, func=AF.Sin), [("T", 1)], [("T", 1)])
            rr(T2, F0, math.pi / 2, T2, T3, [("Gs", 0)], [("T", 2)], [("T", 2), ("T", 3)])
            act(lambda e: e.activation(out=T2, in_=T2, func=AF.Sin), [("T", 2)], [("T", 2)])
            dve(lambda e: e.tensor_tensor(out=T1, in0=T1, in1=T0, op=ALU.mult), [("T", 1), ("T", 0)], [("T", 1)])
            dve(lambda e: e.tensor_tensor(out=T2, in0=T2, in1=T0, op=ALU.mult), [("T", 2), ("T", 0)], [("T", 2)])
            dve(lambda e: e.tensor_tensor(out=T0, in0=fre, in1=T2, op=ALU.mult), [("tabA", 0), ("T", 2)], [("T", 0)])
            dve(lambda e: e.tensor_tensor(out=T3, in0=fim, in1=T1, op=ALU.mult), [("tabA", 1), ("T", 1)], [("T", 3)])
            dve(lambda e: e.tensor_tensor(out=SmF[:, 0, :], in0=T0, in1=T3, op=ALU.subtract), [("T", 0), ("T", 3)], [("SmF", 0)])
            dve(lambda e: e.tensor_tensor(out=T0, in0=fre, in1=T1, op=ALU.mult), [("tabA", 0), ("T", 1)], [("T", 0)])
            dve(lambda e: e.tensor_tensor(out=T3, in0=fim, in1=T2, op=ALU.mult), [("tabA", 1), ("T", 2)], [("T", 3)])
            dve(lambda e: e.tensor_tensor(out=SmF[:, 1, :], in0=T0, in1=T3, op=ALU.add), [("T", 0), ("T", 3)], [("SmF", 1)])
            js = slice(4 * ct, 4 * ct + 4)
            act(lambda e, js=js: e.activation(out=zsm[:, 2, :], in_=ldt_sm[:, l, js], func=AF.Exp), [("small", 0)], [("zsm", 2)])
            dve(lambda e, js=js: e.tensor_tensor(out=zsm[:, 0, :], in0=are_sm[:, l, js], in1=zsm[:, 2, :], op=ALU.mult),
                [("small", 0), ("zsm", 2)], [("zsm", 0)])
            dve(lambda e, js=js: e.tensor_tensor(out=zsm[:, 1, :], in0=aim_sm[:, l, js], in1=zsm[:, 2, :], op=ALU.mult),
                [("small", 0), ("zsm", 2)], [("zsm", 1)])
            t3 = lambda a: a.rearrange("p (j t) -> p j t", j=4)
            tpb = tp1[:].unsqueeze(1).broadcast_to([128, 4, 128])
            dve(lambda e: e.tensor_tensor(out=t3(T0), in0=zsm[:, 0, :].unsqueeze(2).broadcast_to([128, 4, 128]), in1=tpb, op=ALU.mult),
                [("zsm", 0), ("small", 0), ("T", 0)], [("T", 0)])
            act(lambda e: e.activation(out=T0, in_=T0, func=AF.Exp), [("T", 0)], [("T", 0)])
            dve(lambda e: e.tensor_tensor(out=t3(F0), in0=zsm[:, 1, :].unsqueeze(2).broadcast_to([128, 4, 128]), in1=tpb, op=ALU.mult),
                [("zsm", 1), ("small", 0)], [("Gs", 0)])
            rr(T1, F0, 0.0, T1, T3, [("Gs", 0)], [("T", 1)], [("T", 1), ("T", 3)])
            act(lambda e: e.activation(out=T1, in_=T1, func=AF.Sin), [("T", 1)], [("T", 1)])
            rr(T2, F0, math.pi / 2, T2, T3, [("Gs", 0)], [("T", 2)], [("T", 2), ("T", 3)])
            act(lambda e: e.activation(out=T2, in_=T2, func=AF.Sin), [("T", 2)], [("T", 2)])
            SpF = SpT[:].rearrange("p a b c -> p a (b c)")
            dve(lambda e: e.tensor_tensor(out=SpF[:, 0, :], in0=T2, in1=T0, op=ALU.mult), [("T", 2), ("T", 0)], [("SpT", 0)])
            dve(lambda e: e.tensor_tensor(out=SpF[:, 1, :], in0=T1, in1=T0, op=ALU.mult), [("T", 1), ("T", 0)], [("SpT", 1)])

            chunk_list = [("p", c_) for c_ in range(4)] + [("s", 0)]
            for (kind, c_) in chunk_list:
                rows = 128 if kind == "p" else 64
                c0 = 128 * c_ if kind == "p" else 512
                zero_h0 = (kind == "p" and g == 0 and c_ == 0)
                bR, bI = next_bank(), next_bank()
                pe(lambda e, rows=rows, c0=c0, bR=bR, ct=ct: e.matmul(ps[0:rows, bR, :], lhsT=uT[:, ct, c0:c0 + rows],
                                                                     rhs=BBs[:, 0, :], start=True, stop=True),
                   [("uT", ct, 0), ("uT", ct, 1), ("BBs", 0)], [("ps", bR)])
                pe(lambda e, rows=rows, c0=c0, bI=bI, ct=ct: e.matmul(ps[0:rows, bI, :], lhsT=uT[:, ct, c0:c0 + rows],
                                                                     rhs=BBs[:, 1, :], start=True, stop=True),
                   [("uT", ct, 0), ("uT", ct, 1), ("BBs", 1)], [("ps", bI)])
                R_ = slice(0, rows)
                dve(lambda e, R_=R_, bR=bR: e.tensor_tensor(out=T0[R_], in0=ps[R_, bR, :], in1=SmF[R_, 0, :], op=ALU.mult),
                    [("ps", bR), ("SmF", 0), ("T", 0)], [("T", 0)])
                dve(lambda e, R_=R_, bI=bI: e.tensor_tensor(out=T1[R_], in0=ps[R_, bI, :], in1=SmF[R_, 1, :], op=ALU.mult),
                    [("ps", bI), ("SmF", 1), ("T", 1)], [("T", 1)])
                dve(lambda e, R_=R_: e.tensor_tensor(out=Wt[R_, 0, :], in0=T0[R_], in1=T1[R_], op=ALU.subtract),
                    [("T", 0), ("T", 1)], [("Wt", 0)])
                dve(lambda e, R_=R_, bI=bI: e.tensor_tensor(out=T2[R_], in0=ps[R_, bI, :], in1=SmF[R_, 0, :], op=ALU.mult),
                    [("ps", bI), ("SmF", 0), ("T", 2)], [("T", 2)])
                dve(lambda e, R_=R_, bR=bR: e.tensor_tensor(out=T3[R_], in0=ps[R_, bR, :], in1=SmF[R_, 1, :], op=ALU.mult),
                    [("ps", bR), ("SmF", 1), ("T", 3)], [("T", 3)])
                dve(lambda e, R_=R_: e.tensor_tensor(out=Wt[R_, 1, :], in0=T2[R_], in1=T3[R_], op=ALU.add),
                    [("T", 2), ("T", 3)], [("Wt", 1)])
                bG = [next_bank(), next_bank()]
                for ri in range(2):
                    def gmm(e, ri=ri, rows=rows, bgi=bG[ri]):
                        for jj in range(4):
                            ins = e.matmul(ps[:, bgi, jj * 128:jj * 128 + rows], lhsT=Wt[0:rows, ri, jj * 128:(jj + 1) * 128],
                                           rhs=tri_b[0:rows, 0:rows], start=True, stop=True)
                        return ins
                    pe(gmm, [("Wt", ri), ("tri_b", 0)], [("ps", bG[ri])])
                for ri in range(2):
                    if zero_h0:
                        act(lambda e, ri=ri, rows=rows, bgi=bG[ri]: e.copy(
                            out=Gs[:, ri, :, 0:rows], in_=ps[:, bgi, :].rearrange("p (j t) -> p j t", j=4)[:, :, 0:rows]),
                            [("ps", bG[ri])], [("Gs", ri)])
                    else:
                        for jj in range(4):
                            if kind == "p":
                                hsrc = h0p[:, l, ri, 4 * ct + jj:4 * ct + jj + 1]
                                hk = ("h0p", l)
                            else:
                                hsrc = h0s[:, ri, 4 * ct + jj:4 * ct + jj + 1]
                                hk = ("h0s", ri)
                            act(lambda e, ri=ri, jj=jj, rows=rows, bgi=bG[ri], hsrc=hsrc: e.activation(
                                out=Gs[:, ri, jj, 0:rows], in_=ps[:, bgi, jj * 128:jj * 128 + rows], func=AF.Identity,
                                bias=hsrc, scale=1.0),
                                [("ps", bG[ri]), hk], [("Gs", ri)])
                tv = lambda a, rows=rows: a.rearrange("p (j t) -> p j t", j=4)[:, :, 0:rows]
                Gr, Gi = Gs[:, 0, :, 0:rows], Gs[:, 1, :, 0:rows]
                Sr, Si = SpT[:, 0, :, 0:rows], SpT[:, 1, :, 0:rows]
                dve(lambda e, tv=tv, Gr=Gr, Sr=Sr: e.tensor_tensor(out=tv(T0), in0=Sr, in1=Gr, op=ALU.mult), [("SpT", 0), ("Gs", 0), ("T", 0)], [("T", 0)])
                dve(lambda e, tv=tv, Gi=Gi, Si=Si: e.tensor_tensor(out=tv(T1), in0=Si, in1=Gi, op=ALU.mult), [("SpT", 1), ("Gs", 1), ("T", 1)], [("T", 1)])
                dve(lambda e, tv=tv, rows=rows: e.tensor_tensor(out=hS[:, 0, :, 0:rows], in0=tv(T0), in1=tv(T1), op=ALU.subtract),
                    [("T", 0), ("T", 1)], [("hS", 0)])
                dve(lambda e, tv=tv, Gi=Gi, Sr=Sr: e.tensor_tensor(out=tv(T2), in0=Sr, in1=Gi, op=ALU.mult), [("SpT", 0), ("Gs", 1), ("T", 2)], [("T", 2)])
                dve(lambda e, tv=tv, Gr=Gr, Si=Si: e.tensor_tensor(out=tv(T3), in0=Si, in1=Gr, op=ALU.mult), [("SpT", 1), ("Gs", 0), ("T", 3)], [("T", 3)])
                dve(lambda e, tv=tv, rows=rows: e.scalar_tensor_tensor(out=hS[:, 1, :, 0:rows], in0=tv(T2), scalar=-1.0, in1=tv(T3),
                                                                     op0=ALU.mult, op1=ALU.subtract),
                    [("T", 2), ("T", 3)], [("hS", 1)])
                lastc = lambda a, rows=rows: a.rearrange("p (j t) -> p j t", j=4)[:, :, rows - 1]
                if kind == "p":
                    dre, dim_, dk = h0p[:, l, 0, 4 * ct:4 * ct + 4], h0p[:, l, 1, 4 * ct:4 * ct + 4], [("h0p", l)]
                else:
                    dre, dim_, dk = hends[:, 0, 4 * ct:4 * ct + 4], hends[:, 1, 4 * ct:4 * ct + 4], [("hends", 0)]
                dve(lambda e, lastc=lastc, dre=dre: e.tensor_tensor(out=dre, in0=lastc(T0), in1=lastc(T1), op=ALU.subtract),
                    [("T", 0), ("T", 1)] + dk, dk)
                dve(lambda e, lastc=lastc, dim_=dim_: e.tensor_tensor(out=dim_, in0=lastc(T2), in1=lastc(T3), op=ALU.add),
                    [("T", 2), ("T", 3)] + dk, dk)
                bY = next_bank()

                def ymm(e, rows=rows, bY=bY):
                    for jj in range(4):
                        e.matmul(ps[:, bY, 0:rows], lhsT=CCs[:, 0, jj, :], rhs=hS[:, 0, jj, 0:rows], start=(jj == 0), stop=False)
                    for jj in range(4):
                        ins = e.matmul(ps[:, bY, 0:rows], lhsT=CCs[:, 1, jj, :], rhs=hS[:, 1, jj, 0:rows], start=False, stop=(jj == 3))
                    return ins
                pe(ymm, [("CCs", 0), ("CCs", 1), ("hS", 0), ("hS", 1)], [("ps", bY)])
                dve(lambda e, rows=rows, bY=bY, c0=c0, ct=ct: e.scalar_tensor_tensor(
                    out=ypre[:, c0:c0 + rows], in0=uT[:, ct, c0:c0 + rows], scalar=dTs[:, l, ct:ct + 1],
                    in1=ps[:, bY, 0:rows], op0=ALU.mult, op1=ALU.add),
                    [("ps", bY), ("uT", ct, 0), ("uT", ct, 1), ("small", 0)], [("ypre", c0)])
            yk = [("ypre", c) for c in (0, 128, 256, 384, 512)]
            Tf = tmpA[:].rearrange("p a b -> p (a b)")
            Y0, Y1 = Tf[:, 0:T], Tf[:, 1024:1024 + T]
            ka, kb = [("T", 0), ("T", 1)], [("T", 2), ("T", 3)]
            act(lambda e: e.activation(out=Y0, in_=ypre[:], func=AF.Square), yk + ka, ka)
            dve(lambda e: e.tensor_scalar(out=Y0, in0=Y0, scalar1=0.044715, scalar2=1.0, op0=ALU.mult, op1=ALU.add), ka, ka)
            dve(lambda e: e.tensor_tensor(out=Y0, in0=Y0, in1=ypre[:], op=ALU.mult), ka + yk, ka)
            act(lambda e: e.activation(out=Y1, in_=Y0, func=AF.Sigmoid, scale=1.5957691216057308), ka + kb, kb)
            dve(lambda e, ct=ct: e.tensor_tensor(out=zT[:, ct, :], in0=Y1, in1=ypre[:], op=ALU.mult), kb + yk, [("zT", ct)])
        def state_out(src_re, src_im, skeys, dre, dim_, tag):
            b = next_bank()

            def trs(e, b=b):
                e.transpose(out=ps[0:32, b, 0:128], in_=src_re, identity=ident_f[:])
                return e.transpose(out=ps[0:32, b, 128:256], in_=src_im, identity=ident_f[:])
            pe(trs, skeys + [("ident_f", 0)], [("ps", b)])
            act(lambda e, b=b: e.copy(out=trst[:].rearrange("p a b -> p (a b)"), in_=ps[0:32, b, 0:256]), [("ps", b)], [("ckst", 0)])
            P.add("sp", lambda e: e.dma_start(out=dre, in_=trst[:, 0, :]), R=[("ckst", 0)], W=[("o_st", tag, 0)], dma=("o_st", 0))
            P.add("sp", lambda e: e.dma_start(out=dim_, in_=trst[:, 1, :]), R=[("ckst", 0)], W=[("o_st", tag, 1)], dma=("o_st", 1))
        if write_s:
            state_out(hends[:, 0, :], hends[:, 1, :], [("hends", 0)], dout["sre"][l, sidx_s], dout["sim"][l, sidx_s], "s")
        if last_g:
            state_out(h0p[:, l, 0, :], h0p[:, l, 1, :], [("h0p", l)], dout["pre"][l], dout["pim"][l], "p")

    def _groups():
        for g in range(n_groups):
            P.add("sp", lambda e, g=g: e.dma_start(out=ropec[:], in_=din["ropec"][g]), W=[("rope", 0)], dma="ropec")
            P.add("sp", lambda e, g=g: e.dma_start(out=ropes[:], in_=din["ropes"][g]), W=[("rope", 0)], dma="ropes")
            sidx_s = g % 2
            for tt in range(5):
                rows = 128 if tt < 4 else 64
                c0 = tt * 128
                src = din["xp"][g * TP + c0: g * TP + c0 + 128, :] if tt < 4 else din["xs"][sidx_s]
                xb = tt % 2
                P.add("sp", lambda e, src=src, rows=rows, xb=xb: e.dma_start(out=xstage[0:rows, xb, :], in_=src),
                      W=[("xstage", xb)], dma=("xin", xb))
                for kq in range(4):
                    b = next_bank()

                    def trx(e, b=b, rows=rows, xb=xb, kq=kq):
                        for k4 in range(4):
                            k = kq * 4 + k4
                            ins = e.transpose(out=ps[:, b, k4 * 128:k4 * 128 + rows], in_=xstage[0:rows, xb, k * 128:(k + 1) * 128],
                                              identity=ident_f[0:rows, 0:rows])
                        return ins
                    pe(trx, [("xstage", xb), ("ident_f", 0)], [("ps", b)])
                    hi_keys = [("xT", kq * 4 + k4, hi) for k4 in range(4) for hi in range(2)]
                    act(lambda e, b=b, rows=rows, kq=kq, c0=c0: e.copy(
                        out=xT[:, kq * 4:kq * 4 + 4, c0:c0 + rows],
                        in_=ps[:, b, :].rearrange("p (k t) -> p k t", k=4)[:, :, 0:rows]),
                        [("ps", b)], hi_keys)
            for l in range(n_layers):
                layer(g, l)
            for tt in range(5):
                rows = 128 if tt < 4 else 64
                c0 = tt * 128
                if tt == 4 and g >= 2:
                    continue
                xb = tt % 2
                for kq in range(4):
                    b = next_bank()

                    def tro(e, b=b, rows=rows, kq=kq, c0=c0):
                        for k4 in range(4):
                            k = kq * 4 + k4
                            ins = e.transpose(out=ps[0:rows, b, k4 * 128:(k4 + 1) * 128], in_=xT[:, k, c0:c0 + rows],
                                              identity=ident_f[:])
                        return ins
                    pe(tro, [("xT", kq * 4 + k4, hi) for k4 in range(4) for hi in range(2)] + [("ident_f", 0)], [("ps", b)])
                    act(lambda e, b=b, rows=rows, kq=kq, xb=xb: e.copy(out=xstage[0:rows, xb, kq * 512:(kq + 1) * 512],
                                                                      in_=ps[0:rows, b, :]),
                        [("ps", b)], [("xstage", xb)])
                dst = dout["yp"][g * TP + c0: g * TP + c0 + 128, :] if tt < 4 else dout["ys"][sidx_s]
                P.add("sp", lambda e, dst=dst, rows=rows, xb=xb: e.dma_start(out=dst, in_=xstage[0:rows, xb, :]),
                      R=[("xstage", xb)], W=[("xstage", xb), ("o_y", xb)], dma=("o_y", xb))

    try:
        chk("setup")
        _groups()
    except _Stop:
        pass

    okeys = [k for k in P.lastw if isinstance(k[0], str) and k[0].startswith("o_")]
    P.add("sp", lambda e: e.nop(), R=okeys)
    P.emit(nc)
    return nc


def _rep(a, n=128):
    return np.ascontiguousarray(np.broadcast_to(a[None], (n,) + a.shape))


def prep_shared(inp):
    f = np.float32
    sh = {}
    for k in ["w_mod", "w_in", "w_glu", "w_gate", "w_proj_ssm", "w_proj_attn", "w_out", "w_ffn_gate", "w_ffn_up",
              "w_ffn_down"]:
        sh[k] = np.ascontiguousarray(inp[k], dtype=f)

    def colT(a, nt):
        return np.ascontiguousarray(a.reshape(L, nt, 128).transpose(2, 0, 1))
    sh["b_modT"] = colT(inp["b_mod"], 96)
    sh["n1T"] = colT(inp["norm1_g"], 16)
    sh["n2T"] = colT(inp["norm2_g"], 16)
    sh["b_gateT"] = colT(inp["b_gate"], 32)
    sh["b_gluT"] = colT(inp["b_glu"], 8)
    sh["dT"] = colT(inp["ssm_d"], 8)
    sh["gq"] = _rep(inp["q_norm_g"])
    sh["gk"] = _rep(inp["k_norm_g"])
    sh["sinkT"] = _rep(inp["attn_sink"])
    sh["are_b"] = _rep(inp["ssm_a_re"].reshape(L, 4096))
    sh["aim_b"] = _rep(inp["ssm_a_im"].reshape(L, 4096))
    sh["ldt_b"] = _rep(inp["ssm_log_dt"])

    def sm(a):
        return np.ascontiguousarray(a.reshape(L, 32, 2, 64).transpose(2, 3, 0, 1).reshape(128, L, 32))
    sh["are_sm"] = sm(inp["ssm_a_re"])
    sh["aim_sm"] = sm(inp["ssm_a_im"])
    sh["ldt_sm"] = sm(np.broadcast_to(inp["ssm_log_dt"][:, :, None], (L, 64, 64)))
    for nm, src in [("BBre", inp["ssm_b_re"]), ("BBim", inp["ssm_b_im"])]:
        bb = np.zeros((L, 8, 8, 16, 8, 64), f)
        s5 = src.reshape(L, 8, 8, 64, 16)
        for gl in range(8):
            bb[:, :, gl, :, gl, :] = s5[:, :, gl].transpose(0, 1, 3, 2)
        sh[nm] = np.ascontiguousarray(bb.transpose(0, 2, 3, 1, 4, 5).reshape(L, 128, 8, 512))
    for nm, src in [("CCre", inp["ssm_c_re"]), ("CCim", inp["ssm_c_im"])]:
        cc = np.zeros((L, 2, 64, 32, 8, 16), f)
        s6 = src.reshape(L, 32, 2, 16, 64)
        for j in range(32):
            for q in range(2):
                cc[:, q, :, j, (2 * j + q) % 8, :] = s6[:, j, q].transpose(0, 2, 1)
        sh[nm] = np.ascontiguousarray(cc.reshape(L, 128, 32, 128))
    half = 8
    inv_freq = (np.float32(500000.0) ** (-np.arange(half, dtype=f) / np.float32(half))).astype(f)
    rc = np.zeros((NGRP, 128, 5, 8), f)
    rs = np.zeros((NGRP, 128, 5, 8), f)
    for g in range(NGRP):
        for tt in range(5):
            if tt < 4:
                pos = (g * TP + tt * 128 + np.arange(128)).astype(f)
            else:
                pos = np.concatenate([(2048 + np.arange(64)).astype(f), np.zeros(64, f)])
            ang = (pos[:, None] * inv_freq[None, :]).astype(f)
            rc[g, :, tt] = np.cos(ang)
            rs[g, :, tt] = np.sin(ang)
    sh["ropec"] = rc
    sh["ropes"] = rs
    sh["tri"] = np.triu(np.ones((128, 128), f))
    sh["tp1"] = _rep(np.arange(1, 129, dtype=f))
    sh["negsp1"] = -np.arange(1, 129, dtype=f).reshape(128, 1)
    return sh


def prep_core(inp, c):
    f = np.float32
    b = c % 4
    m = {}
    m["xp"] = np.ascontiguousarray(inp["x_prompt"][b], dtype=f)
    m["xs"] = np.ascontiguousarray(inp["x_sample"][2 * c:2 * c + 2], dtype=f)
    m["ck"] = np.ascontiguousarray(inp["cache_k"][:, 2 * c:2 * c + 2].reshape(L, 2, 128, 256), dtype=f)
    m["cv"] = np.ascontiguousarray(inp["cache_v"][:, 2 * c:2 * c + 2].reshape(L, 2, 128, 256), dtype=f)

    def st(a):
        return np.ascontiguousarray(a.reshape(L, 2, 32, 2, 64).transpose(0, 1, 3, 4, 2).reshape(L, 2, 128, 32), dtype=f)
    m["st_re"] = st(inp["state_ssm_re"][:, 2 * c:2 * c + 2])
    m["st_im"] = st(inp["state_ssm_im"][:, 2 * c:2 * c + 2])
    cc = np.stack([inp["c_prompt"][b], inp["c_sample"][2 * c], inp["c_sample"][2 * c + 1]], axis=1)
    m["cT"] = np.ascontiguousarray(cc.reshape(16, 128, 3).transpose(1, 0, 2), dtype=f)
    return m


_NC_CACHE = {}


def kernel(**inputs):
    inp = {k: np.asarray(v) for k, v in inputs.items()}
    if "nc" not in _NC_CACHE:
        _NC_CACHE["nc"] = build_program()
    nc = _NC_CACHE["nc"]
    sh = prep_shared(inp)
    in_maps = []
    for c in range(8):
        m = dict(sh)
        m.update(prep_core(inp, c))
        in_maps.append(m)
    res = run_bass_kernel_spmd(nc, in_maps, core_ids=list(range(8)))
    r = res.results
    f = np.float32
    y_prompt = np.stack([r[b]["yp"] for b in range(4)]).astype(f)
    y_sample = np.concatenate([r[c]["ys"] for c in range(8)], axis=0).astype(f)
    pk = np.stack([r[b]["pk"] for b in range(4)], axis=1).reshape(L, 4, 128, 4, 64).astype(f)
    pv = np.stack([r[b]["pv"] for b in range(4)], axis=1).reshape(L, 4, 128, 4, 64).astype(f)
    pre = np.stack([r[b]["pre"] for b in range(4)], axis=1).reshape(L, 4, 64, 64).astype(f)
    pim = np.stack([r[b]["pim"] for b in range(4)], axis=1).reshape(L, 4, 64, 64).astype(f)
    sk = np.concatenate([r[c]["sk"] for c in range(8)], axis=1).reshape(L, 16, 128, 4, 64).astype(f)
    sv = np.concatenate([r[c]["sv"] for c in range(8)], axis=1).reshape(L, 16, 128, 4, 64).astype(f)
    sre = np.concatenate([r[c]["sre"] for c in range(8)], axis=1).reshape(L, 16, 64, 64).astype(f)
    sim = np.concatenate([r[c]["sim"] for c in range(8)], axis=1).reshape(L, 16, 64, 64).astype(f)
    return (y_prompt, y_sample, pk, pv, pre, pim, sk, sv, sre, sim)
```

```python
import contextlib
import math

import numpy as np
import concourse.bass as bass
import concourse.mybir as mybir
from concourse.bass_utils import run_bass_kernel_spmd

F32 = mybir.dt.float32
BF16 = mybir.dt.bfloat16
AF = mybir.ActivationFunctionType
ALU = mybir.AluOpType
AX = mybir.AxisListType

ENGS = ("pe", "act", "dve", "pool", "sp")

D = 2048
L = 4
TP = 512
TS = 64
T = TP + TS
NGRP = 4
DFF = 5632
EPS = 1e-6
MAGIC = 12582912.0
TWO_PI = 2.0 * math.pi


class _Stop(Exception):
    pass


def chk(name):
    import os
    if os.environ.get("KSTOP", "") == name:
        raise _Stop()


class Prog:
    def __init__(self):
        self.ops = []
        self.lastw = {}
        self.readers = {}
        self.nameacc = {}
        self.aliases = {}
        self.dma_count = {}

    def alias(self, a, b):
        self.aliases.setdefault(a, set()).add(b)
        self.aliases.setdefault(b, set()).add(a)

    def add(self, eng, fn, R=(), W=(), dma=None):
        oid = len(self.ops)
        deps = set()
        names = set()
        for k in R:
            names.add(k[0])
            if k in self.lastw:
                deps.add(self.lastw[k])
        for k in W:
            names.add(k[0])
            if k in self.lastw:
                deps.add(self.lastw[k])
            for r in self.readers.get(k, ()):
                deps.add(r)
        for n in names:
            for o in self.aliases.get(n, ()):
                for v in self.nameacc.get(o, {}).values():
                    deps.add(v)
        op = dict(id=oid, eng=eng, fn=fn, deps=deps, dma=dma, inc=False)
        if dma is not None:
            self.dma_count[dma] = self.dma_count.get(dma, 0) + 1
            op["dma_val"] = 16 * self.dma_count[dma]
        self.ops.append(op)
        for k in W:
            self.lastw[k] = oid
            self.readers[k] = []
        for k in R:
            if k not in W:
                self.readers.setdefault(k, []).append(oid)
        for n in names:
            self.nameacc.setdefault(n, {})[(eng, dma)] = oid
        return oid

    def emit(self, nc):
        ops = self.ops
        needed = set()
        for op in ops:
            for d in op["deps"]:
                p = ops[d]
                if p["dma"] is None:
                    if p["eng"] == "pe" and op["eng"] == "pe":
                        continue
                    needed.add(d)
        cnt = {e: 0 for e in ENGS}
        for op in ops:
            if op["dma"] is None and op["id"] in needed:
                cnt[op["eng"]] += 1
                op["inc"] = True
                op["val"] = cnt[op["eng"]]
        waited = {}
        for op in ops:
            w = {}
            for d in op["deps"]:
                p = ops[d]
                if p["dma"] is not None:
                    key = ("dma", p["dma"])
                    val = p["dma_val"]
                else:
                    if p["eng"] == "pe" and op["eng"] == "pe":
                        continue
                    key = ("eng", p["eng"])
                    val = p["val"]
                if w.get(key, 0) < val:
                    w[key] = val
            mw = waited.setdefault(op["eng"], {})
            waits = []
            for key, val in w.items():
                if mw.get(key, 0) >= val:
                    continue
                mw[key] = val
                waits.append((key, val))
            op["waits"] = waits
        dma_keys = sorted(self.dma_count.keys(), key=str)
        if _os.environ.get("KSIM"):
            semv = {}
            per = {e: [op for op in ops if op["eng"] == e] for e in ENGS}
            ptr = {e: 0 for e in ENGS}
            prog = True
            while prog:
                prog = False
                for e in ENGS:
                    while ptr[e] < len(per[e]):
                        op = per[e][ptr[e]]
                        if all(semv.get(k, 0) >= v for k, v in op["waits"]):
                            if op["dma"] is not None:
                                semv[("dma", op["dma"])] = semv.get(("dma", op["dma"]), 0) + 16
                            elif op["inc"]:
                                semv[("eng", e)] = semv.get(("eng", e), 0) + 1
                            ptr[e] += 1
                            prog = True
                        else:
                            break
            for e in ENGS:
                if ptr[e] < len(per[e]):
                    op = per[e][ptr[e]]
                    print("DEADLOCK", e, ptr[e], len(per[e]), op["id"], op["waits"], {k: semv.get(k, 0) for k, _ in op["waits"]})
            print("SIM done", {e: (ptr[e], len(per[e])) for e in ENGS}, {k: v for k, v in semv.items()}, flush=True)
        with contextlib.ExitStack() as st:
            sems = {}
            for e in ENGS:
                sems[("eng", e)] = st.enter_context(nc.semaphore("s_" + e))
            for i, k in enumerate(dma_keys):
                sems[("dma", k)] = st.enter_context(nc.semaphore("d%d" % i))
            block = st.enter_context(nc.Block())
            per_eng = {e: [op for op in ops if op["eng"] == e] for e in ENGS}

            def run(engobj, lst):
                for op in lst:
                    for key, val in op["waits"]:
                        engobj.wait_ge(sems[key], val)
                    ins = op["fn"](engobj)
                    if op["dma"] is not None:
                        ins.then_inc(sems[("dma", op["dma"])], 16)
                    elif op["inc"]:
                        ins.then_inc(sems[("eng", op["eng"])], 1)

            @block.tensor
            def _(e):
                run(e, per_eng["pe"])

            @block.scalar
            def _(e):
                run(e, per_eng["act"])

            @block.vector
            def _(e):
                run(e, per_eng["dve"])

            @block.gpsimd
            def _(e):
                run(e, per_eng["pool"])

            @block.sync
            def _(e):
                run(e, per_eng["sp"])
        return cnt


import os as _os
WL = int(_os.environ.get("KWL", L))
IN_SHAPES = {
    "xp": [2048, D], "xs": [2, TS, D],
    "ck": [L, 2, 128, 256], "cv": [L, 2, 128, 256],
    "st_re": [L, 2, 128, 32], "st_im": [L, 2, 128, 32],
    "cT": [128, 16, 3],
    "w_mod": [WL, D, 6 * D], "w_in": [WL, D, 2560], "w_glu": [WL, 1024, 1024],
    "w_gate": [WL, D, 2 * D], "w_proj_ssm": [WL, 1024, D], "w_proj_attn": [WL, 1024, D],
    "w_out": [WL, D, D], "w_ffn_gate": [WL, D, DFF], "w_ffn_up": [WL, D, DFF],
    "w_ffn_down": [WL, DFF, D],
    "b_modT": [128, L, 96], "n1T": [128, L, 16], "n2T": [128, L, 16],
    "b_gateT": [128, L, 32], "b_gluT": [128, L, 8], "dT": [128, L, 8],
    "gq": [128, L, 64], "gk": [128, L, 64], "sinkT": [128, L, 16],
    "are_b": [128, L, 4096], "aim_b": [128, L, 4096], "ldt_b": [128, L, 64],
    "are_sm": [128, L, 32], "aim_sm": [128, L, 32], "ldt_sm": [128, L, 32],
    "BBre": [L, 128, 8, 512], "BBim": [L, 128, 8, 512],
    "CCre": [L, 128, 32, 128], "CCim": [L, 128, 32, 128],
    "ropec": [NGRP, 128, 5, 8], "ropes": [NGRP, 128, 5, 8],
    "tri": [128, 128], "tp1": [128, 128], "negsp1": [128, 1],
}
OUT_SHAPES = {
    "yp": [2048, D], "ys": [2, TS, D],
    "pk": [L, 128, 256], "pv": [L, 128, 256],
    "pre": [L, 32, 128], "pim": [L, 32, 128],
    "sk": [L, 2, 128, 256], "sv": [L, 2, 128, 256],
    "sre": [L, 2, 32, 128], "sim": [L, 2, 32, 128],
}

HALVES = [(0, 288), (288, 576)]
SEGS = [[(0, 288, 0)], [(288, 512, 0), (512, 576, 1)]]
ALLSEG = [(0, 512, 0), (512, 576, 1)]


def build_program(n_layers=L, n_groups=NGRP):
    nc = bass.Bass("TRN2", target_bir_lowering=False)
    P = Prog()
    din = {k: nc.dram_tensor(k, s, F32, kind="ExternalInput").ap() for k, s in IN_SHAPES.items()}
    dout = {k: nc.dram_tensor(k, s, F32, kind="ExternalOutput").ap() for k, s in OUT_SHAPES.items()}
    scrS = nc.dram_tensor("scrS", [L, 8, 128, 2, 512], F32).ap()
    scrP = nc.dram_tensor("scrP", [L, 8, 128, 2, 512], F32).ap()
    st = contextlib.ExitStack()

    def sb(name, shape, dt=F32):
        return st.enter_context(nc.sbuf_tensor("sb_" + name, shape, dt))

    xT = sb("xT", [128, 16, T])
    modT = sb("modT", [128, L, 96, 3])
    scp = sb("scp", [128, L, 2, 16, 3])
    ones_b = sb("ones_b", [128, 128], BF16)
    ident_f = sb("ident_f", [128, 128])
    ident_b = sb("ident_b", [128, 128], BF16)
    tri_b = sb("tri_b", [128, 128], BF16)
    tp1 = sb("tp1", [128, 128])
    negsp1 = sb("negsp1", [128, 1])
    cTb = sb("cTb", [128, 16, 3], BF16)
    b_modT = sb("b_modT", [128, L, 96])
    n1T = sb("n1T", [128, L, 16])
    n2T = sb("n2T", [128, L, 16])
    b_gateT = sb("b_gateT", [128, L, 32])
    b_gluT = sb("b_gluT", [128, L, 8])
    dTs = sb("dTs", [128, L, 8])
    gq = sb("gq", [128, L, 64])
    gk = sb("gk", [128, L, 64])
    esink = sb("esink", [128, L, 16])
    are_sm = sb("are_sm", [128, L, 32])
    aim_sm = sb("aim_sm", [128, L, 32])
    ldt_sm = sb("ldt_sm", [128, L, 32])
    ldt_b = sb("ldt_b", [128, L, 64])
    haloKT = sb("haloKT", [64, L, 4, 128], BF16)
    haloV = sb("haloV", [128, L, 4, 65], BF16)
    h0p = sb("h0p", [128, L, 2, 32])
    hends = sb("hends", [128, 2, 32])
    h0s = sb("h0s", [128, 2, 32])
    ropec = sb("ropec", [128, 5, 8])
    ropes = sb("ropes", [128, 5, 8])
    rstd = sb("rstd", [128, T])
    ntmp = sb("ntmp", [128, 2, T])
    ring = sb("ring", [128, 5, 4096], BF16)
    qsq = sb("qsq", [128, 256])
    qn = sb("qn", [128, 2, 256])
    qss = sb("qss", [128, 2, 4])
    qrt = sb("qrt", [128, 4, 4, 8])
    qtm = sb("qtm", [128, 2, 256], BF16)
    vf = sb("vf", [128, 2, 256])
    ckst = sb("ckst", [128, 256])
    cvst = sb("cvst", [128, 256])
    ckb = sb("ckb", [128, 256], BF16)
    pT = sb("pT", [128, 2, 2, 256], BF16)
    adn = sb("adn", [64, 2, 256])
    ARENA = 36864 + 53248 + 512
    arena = sb("arena", [128, ARENA // 2], BF16)
    ps = st.enter_context(nc.psum_tensor("ps", [128, 8, 512], F32))

    X0 = 36864
    arena_bufs = {}

    def av(name, off, shape, dt, parts=128):
        nel = int(np.prod(shape[1:]))
        esz = 2 if dt == BF16 else 4
        nbytes = nel * esz
        assert off % 4 == 0 and off + nbytes <= ARENA, (name, off, nbytes)
        v = arena[0:parts, off // 2: (off + nbytes) // 2]
        if dt != BF16:
            v = v.bitcast(dt)
        if len(shape) == 3:
            v = v.rearrange("p (a b) -> p a b", a=shape[1])
        elif len(shape) == 4:
            v = v.rearrange("p (a b c) -> p a b c", a=shape[1], b=shape[2])
        for n2, (o2, b2) in arena_bufs.items():
            if off < o2 + b2 and o2 < off + nbytes:
                P.alias(name, n2)
        arena_bufs[name] = (off, nbytes)
        return v

    hT = av("hT", 0, [128, 16, T], BF16)
    attnT = av("attnT", 18432, [64, 16, T], BF16, parts=64)
    sq = av("sq", X0, [128, 16, T], BF16)
    actT = av("actT", X0, [128, 22, T], BF16)
    xstage = av("xstage", X0, [128, 2, 2048], F32)
    uT = av("uT", X0, [128, 8, T], BF16)
    QT = av("QT", X0 + 9216, [64, 16, T], BF16, parts=64)
    KT = av("KT", X0 + 27648, [64, 4, 832], BF16, parts=64)
    Vaug = av("Vaug", X0 + 34304, [128, 7, 4, 65], BF16)
    zT = av("zT", X0 + 9216, [128, 8, T], BF16)
    ssm_outT = av("ssm_outT", X0 + 18432, [128, 8, T], BF16)
    gaT = av("gaT", X0, [128, 16, T], BF16)
    mixedT = av("mixedT", X0 + 27648, [128, 16, T], BF16)
    SW = X0 + 18432
    SmF = av("SmF", SW, [128, 2, 512], F32)
    SpT = av("SpT", SW + 4096, [128, 2, 4, 128], F32)
    BBs = av("BBs", SW + 8192, [128, 2, 512], BF16)
    CCs = av("CCs", SW + 10240, [128, 2, 4, 128], BF16)
    Wt = av("Wt", SW + 12288, [128, 2, 512], BF16)
    Gs = av("Gs", SW + 14336, [128, 2, 4, 128], F32)
    hS = av("hS", SW + 18432, [128, 2, 4, 128], BF16)
    tmpA = av("T", SW + 20480, [128, 4, 512], F32)
    tabA = av("tabA", SW + 28672, [128, 2, 512], F32)
    ypre = av("ypre", SW + 32768, [128, T], F32)
    assert SW + 32768 + T * 4 <= ARENA

    state = dict(bank=0, wslot=0)

    def next_bank():
        b = state["bank"]
        state["bank"] = (b + 1) % 7
        return b

    def dve(fn, R, W):
        return P.add("dve", fn, R, W)

    def act(fn, R, W):
        return P.add("act", fn, R, W)

    def pe(fn, R, W):
        return P.add("pe", fn, R, W)

    def wnext(src, parts, kt, ncols):
        s = state["wslot"]
        state["wslot"] = (s + 1) % 5
        view = ring[0:parts, s, 0:kt * ncols].rearrange("p (k n) -> p k n", k=kt)
        P.add("pool", lambda e: e.dma_start(out=view, in_=src), W=[("ring", s)], dma=("ring", s))
        return view, ("ring", s)

    def wsrc(w, l, k0, kt, c0, ncols, kp=128):
        return w[l, k0 * kp:(k0 + kt) * kp, c0:c0 + ncols].rearrange("(k p) n -> p k n", p=kp)

    def proj_fm(w, l, ktot, c0, ncols, rhs_fn, rhs_keys, evac, colsplit=HALVES, kp=128):
        kt_max = 16
        tile_cols = 4096 // min(ktot, kt_max)
        tile_cols = min(tile_cols, 512, ncols)
        kchunks = [(k0, min(kt_max, ktot - k0)) for k0 in range(0, ktot, kt_max)]
        for cb in range(c0, c0 + ncols, tile_cols):
            nm = tile_cols // 128
            groups = [(m, hi) for m in range(nm) for hi in range(len(colsplit))]
            if len(kchunks) == 1:
                k0, kt = kchunks[0]
                view, wkey = wnext(wsrc(w, l, k0, kt, cb, tile_cols, kp), kp, kt, tile_cols)
                for (m, hi) in groups:
                    b = next_bank()
                    a0, a1 = colsplit[hi]

                    def mm(e, view=view, m=m, a0=a0, a1=a1, b=b, kt=kt):
                        for k in range(kt):
                            ins = e.matmul(ps[:, b, 0:a1 - a0], lhsT=view[:, k, m * 128:(m + 1) * 128],
                                           rhs=rhs_fn(k)[:, a0:a1], start=(k == 0), stop=(k == kt - 1))
                        return ins
                    pe(mm, R=[wkey] + rhs_keys, W=[("ps", b)])
                    evac((cb - c0) // 128 + m, hi, b)
            else:
                banks = {gk_: next_bank() for gk_ in groups}
                for (k0, kt) in kchunks:
                    view, wkey = wnext(wsrc(w, l, k0, kt, cb, tile_cols, kp), kp, kt, tile_cols)
                    for (m, hi) in groups:
                        b = banks[(m, hi)]
                        a0, a1 = colsplit[hi]

                        def mm(e, view=view, m=m, a0=a0, a1=a1, b=b, kt=kt, k0=k0):
                            for k in range(kt):
                                ins = e.matmul(ps[:, b, 0:a1 - a0], lhsT=view[:, k, m * 128:(m + 1) * 128],
                                               rhs=rhs_fn(k0 + k)[:, a0:a1], start=(k0 + k == 0),
                                               stop=(k0 + k == ktot - 1))
                            return ins
                        pe(mm, R=[wkey] + rhs_keys, W=[("ps", b)])
                for (m, hi) in groups:
                    evac((cb - c0) // 128 + m, hi, banks[(m, hi)])

    def small_load(dst, src, eng="sp"):
        P.add(eng, lambda e: e.dma_start(out=dst, in_=src), W=[("small", 0)], dma="small" + eng)

    for dst, nm in [(b_modT, "b_modT"), (n1T, "n1T"), (n2T, "n2T"), (b_gateT, "b_gateT"), (b_gluT, "b_gluT"),
                    (dTs, "dT"), (gq, "gq"), (gk, "gk"), (esink, "sinkT"), (are_sm, "are_sm"),
                    (aim_sm, "aim_sm"), (ldt_sm, "ldt_sm"), (ldt_b, "ldt_b"), (tp1, "tp1"), (negsp1, "negsp1")]:
        small_load(dst[:], din[nm])
    P.add("pool", lambda e: e.dma_start(out=tri_b[:], in_=din["tri"]), W=[("tri_b", 0)], dma="tri")
    P.add("pool", lambda e: e.dma_start(out=cTb[:], in_=din["cT"]), W=[("cTb", 0)], dma="cTb")
    dve(lambda e: e.memset(ones_b[:], 1.0), [], [("ones_b", 0)])
    dve(lambda e: e.memset(ident_f[:], 0.0), [], [("ident_f", 0)])
    P.add("pool", lambda e: e.affine_select(out=ident_f[:], in_=ident_f[:], pattern=[[-1, 128]],
                                            compare_op=ALU.not_equal, fill=1.0, base=0, channel_multiplier=1),
          R=[("ident_f", 0)], W=[("ident_f", 0)])
    dve(lambda e: e.tensor_copy(out=ident_b[:], in_=ident_f[:]), [("ident_f", 0)], [("ident_b", 0)])
    act(lambda e: e.activation(out=esink[:], in_=esink[:], func=AF.Exp), [("small", 0)], [("esink", 0)])
    dve(lambda e: e.memset(h0p[:], 0.0), [], [("h0p", l_) for l_ in range(L)])

    for l in range(n_layers):
        def ev_mod(mi, hi, b, l=l):
            act(lambda e: e.activation(out=modT[:, l, mi, :], in_=ps[:, b, 0:3], func=AF.Identity,
                                       bias=b_modT[:, l, mi:mi + 1], scale=1.0),
                [("ps", b), ("small", 0)], [("modT", l)])
        proj_fm(din["w_mod"], l, 16, 0, 6 * D, lambda k: cTb[:, k, :], [("cTb", 0)], ev_mod, colsplit=[(0, 3)])
        for which, (nT, off) in enumerate([(n1T, 16), (n2T, 64)]):
            dve(lambda e, l=l, which=which, off=off: e.tensor_scalar(
                out=scp[:, l, which], in0=modT[:, l, off:off + 16, :], scalar1=1.0, scalar2=None, op0=ALU.add),
                [("modT", l)], [("scp", l)])
            dve(lambda e, l=l, which=which, nT=nT: e.tensor_tensor(
                out=scp[:, l, which], in0=scp[:, l, which],
                in1=nT[:, l, :].unsqueeze(2).broadcast_to([128, 16, 3]), op=ALU.mult),
                [("scp", l), ("small", 0)], [("scp", l)])

    def hkeys():
        return [("hT", k, hi) for k in range(16) for hi in range(2)]

    def norm_phase(l, which, seqs):
        shoff = 0 if which == 0 else 48
        act(lambda e: e.activation(out=sq[:], in_=xT[:], func=AF.Square),
            [("xT", k, hi) for k in range(16) for hi in range(2)], [("sq", 0)])
        for hi, (a0, a1) in enumerate(HALVES):
            b = next_bank()

            def mm(e, a0=a0, a1=a1, b=b):
                for k in range(16):
                    ins = e.matmul(ps[:, b, 0:a1 - a0], lhsT=ones_b[:], rhs=sq[:, k, a0:a1],
                                   start=(k == 0), stop=(k == 15))
                return ins
            pe(mm, [("sq", 0), ("ones_b", 0)], [("ps", b)])
            act(lambda e, a0=a0, a1=a1, b=b: e.activation(out=rstd[:, a0:a1], in_=ps[:, b, 0:a1 - a0],
                                                         func=AF.Sqrt, bias=EPS, scale=1.0 / D),
                [("ps", b)], [("rstd", hi)])
            dve(lambda e, a0=a0, a1=a1: e.reciprocal(out=rstd[:, a0:a1], in_=rstd[:, a0:a1]),
                [("rstd", hi)], [("rstd", hi)])
        for k in range(16):
            tb = k % 2
            for (a0, a1, s) in ALLSEG:
                sidx = seqs[s]
                dve(lambda e, k=k, a0=a0, a1=a1, sidx=sidx, tb=tb: e.scalar_tensor_tensor(
                    out=ntmp[:, tb, a0:a1], in0=xT[:, k, a0:a1], scalar=scp[:, l, which, k, sidx:sidx + 1],
                    in1=rstd[:, a0:a1], op0=ALU.mult, op1=ALU.mult),
                    [("xT", k, 0), ("xT", k, 1), ("rstd", 0), ("rstd", 1), ("scp", l)], [("ntmp", tb, s)])
                act(lambda e, k=k, a0=a0, a1=a1, sidx=sidx, tb=tb: e.activation(
                    out=hT[:, k, a0:a1], in_=ntmp[:, tb, a0:a1], func=AF.Identity,
                    bias=modT[:, l, shoff + k, sidx:sidx + 1], scale=1.0),
                    [("ntmp", tb, s), ("modT", l)], [("hT", k, 0), ("hT", k, 1)])

    def layer(g, l):
        sidx_s = g % 2
        seqs = [0, 1 + sidx_s]
        write_s = g < 2
        last_g = (g == n_groups - 1)
        chk("xload")
        norm_phase(l, 0, seqs)
        chk("norm1")

        def ev_u(mi, hi, b):
            a0, a1 = HALVES[hi]
            act(lambda e: e.copy(out=uT[:, mi, a0:a1], in_=ps[:, b, 0:a1 - a0]), [("ps", b)], [("uT", mi, hi)])
        proj_fm(din["w_in"], l, 16, 0, 1024, lambda k: hT[:, k, :], hkeys(), ev_u)

        chk("uproj")
        dve(lambda e: e.memset(Vaug[:, :, :, 64:65], 1.0), [], [("Vaug", t_) for t_ in range(7)])
        if g > 0:
            dve(lambda e: e.tensor_copy(out=KT[:, :, 0:128], in_=haloKT[:, l]), [("haloKT", l)], [("KT", 0)])
            dve(lambda e: e.tensor_copy(out=Vaug[:, 0, :, 0:64], in_=haloV[:, l, :, 0:64]), [("haloV", l)], [("Vaug", 0)])
        P.add("sp", lambda e: e.dma_start(out=ckst[:], in_=din["ck"][l, sidx_s]), W=[("ckst", 0)], dma="ckst")
        P.add("sp", lambda e: e.dma_start(out=cvst[:], in_=din["cv"][l, sidx_s]), W=[("cvst", 0)], dma="cvst")
        if write_s:
            P.add("sp", lambda e: e.dma_start(out=dout["sk"][l, sidx_s, 0:64, :], in_=din["ck"][l, sidx_s, 64:128, :]),
                  W=[("o_skc", 0)], dma="o_skc")
            P.add("sp", lambda e: e.dma_start(out=dout["sv"][l, sidx_s, 0:64, :], in_=din["cv"][l, sidx_s, 64:128, :]),
                  W=[("o_svc", 0)], dma="o_svc")
        dve(lambda e: e.tensor_copy(out=ckb[:], in_=ckst[:]), [("ckst", 0)], [("ckb", 0)])
        dve(lambda e: e.tensor_copy(out=Vaug[:, 5, :, 0:64], in_=cvst[:].rearrange("p (h d) -> p h d", h=4)),
            [("cvst", 0)], [("Vaug", 5)])
        b = next_bank()
        psb = ps[:, b, :].bitcast(BF16)

        def tr_ck(e, psb=psb):
            for h in range(4):
                ins = e.transpose(out=psb[0:64, h * 128:(h + 1) * 128], in_=ckb[:, h * 64:(h + 1) * 64],
                                  identity=ident_b[:])
            return ins
        pe(tr_ck, [("ckb", 0), ("ident_b", 0)], [("ps", b)])
        act(lambda e, psb=psb: e.copy(out=KT[:, :, 640:768], in_=psb[0:64, 0:512].rearrange("p (h t) -> p h t", h=4)),
            [("ps", b)], [("KT", 5)])

        chk("halo")
        for wt in range(6):
            view, wkey = wnext(wsrc(din["w_in"], l, 0, 16, 1024 + 256 * wt, 256), 128, 16, 256)
            for tt in range(5):
                rows = 128 if tt < 4 else 64
                c0 = tt * 128
                b = next_bank()

                def mm(e, view=view, c0=c0, rows=rows, b=b):
                    for k in range(16):
                        ins = e.matmul(ps[0:rows, b, 0:256], lhsT=hT[:, k, c0:c0 + rows], rhs=view[:, k, :],
                                       start=(k == 0), stop=(k == 15))
                    return ins
                pe(mm, [wkey] + hkeys(), [("ps", b)])
                src = ps[0:rows, b, 0:256]
                _qs = _os.environ.get("KQSUB", "")
                state["qit"] = state.get("qit", 0) + 1
                if state["qit"] > int(_os.environ.get("KQKV", "1000")):
                    raise _Stop()
                if _qs == "mm":
                    continue
                if wt == 5:
                    vt = tt + 1 if tt < 4 else 6
                    dve(lambda e, src=src, rows=rows, vt=vt: e.tensor_copy(
                        out=Vaug[0:rows, vt, :, 0:64], in_=src.rearrange("p (h d) -> p h d", h=4)),
                        [("ps", b)], [("Vaug", vt)])
                    need_out = (tt == 3 and last_g) or (tt == 4 and write_s)
                    if need_out:
                        vb = tt % 2
                        dve(lambda e, src=src, rows=rows, vb=vb: e.tensor_copy(out=vf[0:rows, vb, :], in_=src),
                            [("ps", b)], [("vf", vb)])
                        dst = dout["pv"][l] if tt == 3 else dout["sv"][l, sidx_s, 64:128, :]
                        P.add("sp", lambda e, dst=dst, rows=rows, vb=vb: e.dma_start(out=dst, in_=vf[0:rows, vb, :]),
                              R=[("vf", vb)], W=[("o_v", vb)], dma=("o_v", vb))
                    continue
                isk = (wt == 4)
                gtab = gk if isk else gq
                qb = (wt * 5 + tt) % 2
                act(lambda e, src=src, rows=rows: e.activation(out=qsq[0:rows, :], in_=src, func=AF.Square),
                    [("ps", b)], [("qsq", 0)])
                dve(lambda e, rows=rows, qb=qb: e.tensor_reduce(
                    out=qss[0:rows, qb, :], in_=qsq[0:rows, :].rearrange("p (h d) -> p h d", h=4), axis=AX.X, op=ALU.add),
                    [("qsq", 0)], [("qss", qb)])
                act(lambda e, rows=rows, qb=qb: e.activation(out=qss[0:rows, qb, :], in_=qss[0:rows, qb, :],
                                                            func=AF.Sqrt, bias=EPS, scale=1.0 / 64),
                    [("qss", qb)], [("qss", qb)])
                dve(lambda e, rows=rows, qb=qb: e.reciprocal(out=qss[0:rows, qb, :], in_=qss[0:rows, qb, :]),
                    [("qss", qb)], [("qss", qb)])
                dve(lambda e, src=src, rows=rows, qb=qb: e.tensor_tensor(
                    out=qn[0:rows, qb, :].rearrange("p (h d) -> p h d", h=4),
                    in0=src.rearrange("p (h d) -> p h d", h=4),
                    in1=qss[0:rows, qb, :].unsqueeze(2).broadcast_to([rows, 4, 64]), op=ALU.mult),
                    [("ps", b), ("qss", qb)], [("qn", qb)])
                dve(lambda e, rows=rows, qb=qb, gtab=gtab: e.tensor_tensor(
                    out=qn[0:rows, qb, :].rearrange("p (h d) -> p h d", h=4),
                    in0=qn[0:rows, qb, :].rearrange("p (h d) -> p h d", h=4),
                    in1=gtab[0:rows, l, :].unsqueeze(1).broadcast_to([rows, 4, 64]), op=ALU.mult),
                    [("qn", qb), ("small", 0)], [("qn", qb)])
                if _qs == "norm":
                    continue
                qv = qn[0:rows, qb, :].rearrange("p (h d) -> p h d", h=4)
                cosb = ropec[0:rows, tt, :].unsqueeze(1).broadcast_to([rows, 4, 8])
                sinb = ropes[0:rows, tt, :].unsqueeze(1).broadcast_to([rows, 4, 8])
                for i_, (xa, tb_) in enumerate([(qv[:, :, 0:8], cosb), (qv[:, :, 8:16], sinb),
                                                (qv[:, :, 8:16], cosb), (qv[:, :, 0:8], sinb)]):
                    dve(lambda e, xa=xa, tb_=tb_, i_=i_, rows=rows: e.tensor_tensor(
                        out=qrt[0:rows, i_], in0=xa, in1=tb_, op=ALU.mult),
                        [("qn", qb), ("rope", 0)], [("qrt", i_)])
                dve(lambda e, qv=qv, rows=rows: e.tensor_tensor(out=qv[:, :, 0:8], in0=qrt[0:rows, 0],
                                                               in1=qrt[0:rows, 1], op=ALU.subtract),
                    [("qrt", 0), ("qrt", 1)], [("qn", qb)])
                dve(lambda e, qv=qv, rows=rows: e.tensor_tensor(out=qv[:, :, 8:16], in0=qrt[0:rows, 2],
                                                               in1=qrt[0:rows, 3], op=ALU.add),
                    [("qrt", 2), ("qrt", 3)], [("qn", qb)])
                act(lambda e, rows=rows, qb=qb: e.copy(out=qtm[0:rows, qb, :], in_=qn[0:rows, qb, :]),
                    [("qn", qb)], [("qtm", qb)])
                if _qs == "rope":
                    continue
                if isk:
                    need_out = ((tt == 3 and last_g) or (tt == 4 and write_s)) and not _os.environ.get("KNOKDMA")
                    if need_out:
                        dst = dout["pk"][l] if tt == 3 else dout["sk"][l, sidx_s, 64:128, :]
                        P.add("sp", lambda e, dst=dst, rows=rows, qb=qb: e.dma_start(out=dst, in_=qn[0:rows, qb, :]),
                              R=[("qn", qb)], W=[("o_k", qb)], dma=("o_k", qb))
                if _qs == "kdma":
                    continue
                b2 = next_bank()
                psb2 = ps[:, b2, :].bitcast(BF16)

                def trq(e, psb2=psb2, rows=rows, qb=qb):
                    for h in range(4):
                        ins = e.transpose(out=psb2[0:64, h * 128:h * 128 + rows],
                                          in_=qtm[0:rows, qb, h * 64:(h + 1) * 64], identity=ident_b[0:rows, 0:rows])
                    return ins
                pe(trq, [("qtm", qb), ("ident_b", 0)], [("ps", b2)])
                pv_ = psb2[0:64, 0:512].rearrange("p (h t) -> p h t", h=4)[:, :, 0:rows]
                if isk:
                    kc0 = 128 + c0 if tt < 4 else 768
                    kkey = ("KT", tt + 1 if tt < 4 else 6)
                    act(lambda e, pv_=pv_, kc0=kc0, rows=rows: e.copy(out=KT[:, :, kc0:kc0 + rows], in_=pv_),
                        [("ps", b2)], [kkey])
                else:
                    act(lambda e, pv_=pv_, c0=c0, rows=rows, wt=wt: e.copy(
                        out=QT[:, 4 * wt:4 * wt + 4, c0:c0 + rows], in_=pv_),
                        [("ps", b2)], [("QT", wt, tt)])

        chk("qkv")
        if not last_g:
            dve(lambda e: e.tensor_copy(out=haloKT[:, l], in_=KT[:, :, 512:640]), [("KT", 4)], [("haloKT", l)])
            dve(lambda e: e.tensor_copy(out=haloV[:, l, :, 0:64], in_=Vaug[:, 4, :, 0:64]), [("Vaug", 4)], [("haloV", l)])

        chunks = [("p", lc) for lc in range(8)] + [("s", 0)]
        it = 0
        for (kind, lc) in chunks:
            if kind == "p":
                qc0 = 64 * lc
                tA, tB = lc // 2, lc // 2 + 1
                kcA, kcB = 128 * tA, 128 * tB
                odd = lc % 2
                skipA = (g == 0 and lc < 2)
                tt_q = lc // 2
            else:
                qc0 = 512
                tA, tB = 5, 6
                kcA, kcB = 640, 768
                odd = 0
                skipA = False
                tt_q = 4
            rA = (64, 128) if odd else (0, 128)
            rB = (0, 128) if odd else (0, 64)
            for h in range(4):
                pb = it % 2
                it += 1
                bS = next_bank()
                qkeys = [("QT", h, tt_q)]
                parts = []
                if not skipA:
                    parts.append((0, tA, kcA, rA, 128))
                parts.append((1, tB, kcB, rB, rB[1]))
                for (slot, tX, kc, rr_, mrows) in parts:
                    pe(lambda e, slot=slot, kc=kc, mrows=mrows, bS=bS, h=h, qc0=qc0: e.matmul(
                        ps[0:mrows, bS, slot * 256:(slot + 1) * 256], lhsT=KT[:, h, kc:kc + mrows],
                        rhs=QT[:, 4 * h:4 * h + 4, qc0:qc0 + 64], start=True, stop=True),
                        [("KT", tX)] + qkeys, [("ps", bS)])
                    act(lambda e, slot=slot, mrows=mrows, bS=bS, pb=pb: e.activation(
                        out=pT[0:mrows, pb, slot, :], in_=ps[0:mrows, bS, slot * 256:(slot + 1) * 256],
                        func=AF.Exp, scale=0.125),
                        [("ps", bS)], [("pT", pb, slot)])
                bO = next_bank()

                def pvmm(e, parts=parts, bO=bO, pb=pb, h=h):
                    n = len(parts)
                    for i_, (slot, tX, kc, rr_, mrows) in enumerate(parts):
                        e.matmul(ps[0:64, bO, 0:256], lhsT=Vaug[rr_[0]:rr_[1], tX, h, 0:64],
                                 rhs=pT[rr_[0]:rr_[1], pb, slot, :], start=(i_ == 0), stop=(i_ == n - 1))
                    for i_, (slot, tX, kc, rr_, mrows) in enumerate(parts):
                        ins = e.matmul(ps[0:64, bO, 256:512], lhsT=ones_b[rr_[0]:rr_[1], 0:64],
                                       rhs=pT[rr_[0]:rr_[1], pb, slot, :], start=(i_ == 0), stop=(i_ == n - 1))
                    return ins
                pe(pvmm, [("pT", pb, s_[0]) for s_ in parts] + [("Vaug", s_[1]) for s_ in parts] + [("ones_b", 0)],
                   [("ps", bO)])
                dve(lambda e, bO=bO, pb=pb, h=h: e.tensor_tensor(
                    out=adn[:, pb, :].rearrange("p (r q) -> p r q", r=4),
                    in0=ps[0:64, bO, 256:512].rearrange("p (r q) -> p r q", r=4),
                    in1=esink[0:64, l, 4 * h:4 * h + 4].unsqueeze(2).broadcast_to([64, 4, 64]), op=ALU.add),
                    [("ps", bO), ("esink", 0)], [("adn", pb)])
                dve(lambda e, pb=pb: e.reciprocal(out=adn[:, pb, :], in_=adn[:, pb, :]), [("adn", pb)], [("adn", pb)])
                dve(lambda e, bO=bO, pb=pb, h=h, qc0=qc0: e.tensor_tensor(
                    out=attnT[:, 4 * h:4 * h + 4, qc0:qc0 + 64],
                    in0=ps[0:64, bO, 0:256].rearrange("p (r q) -> p r q", r=4),
                    in1=adn[:, pb, :].rearrange("p (r q) -> p r q", r=4), op=ALU.mult),
                    [("ps", bO), ("adn", pb)], [("attnT", h, qc0)])

        chk("attn")
        ssm_phase(g, l, sidx_s, write_s, last_g)
        chk("ssm")

        def ev_glu(mi, hi, b):
            a0, a1 = HALVES[hi]
            act(lambda e: e.activation(out=ntmp[:, hi, 0:a1 - a0], in_=ps[:, b, 0:a1 - a0], func=AF.Sigmoid,
                                       bias=b_gluT[:, l, mi:mi + 1], scale=1.0),
                [("ps", b), ("small", 0)], [("ntmp", hi, 0), ("ntmp", hi, 1)])
            dve(lambda e: e.tensor_tensor(out=ssm_outT[:, mi, a0:a1], in0=ntmp[:, hi, 0:a1 - a0],
                                          in1=zT[:, mi, a0:a1], op=ALU.mult),
                [("ntmp", hi, 0), ("ntmp", hi, 1), ("zT", mi)], [("ssm_outT", mi, hi)])
        proj_fm(din["w_glu"], l, 8, 0, 1024, lambda k: zT[:, k, :], [("zT", k) for k in range(8)], ev_glu)

        chk("glu")
        def ev_gate(boff):
            def ev(mi, hi, b):
                a0, a1 = HALVES[hi]
                act(lambda e: e.activation(out=gaT[:, mi, a0:a1], in_=ps[:, b, 0:a1 - a0], func=AF.Sigmoid,
                                           bias=b_gateT[:, l, boff + mi:boff + mi + 1], scale=1.0),
                    [("ps", b), ("small", 0)], [("gaT", mi, hi)])
            return ev
        proj_fm(din["w_gate"], l, 16, 0, D, lambda k: hT[:, k, :], hkeys(), ev_gate(0))

        def ev_ps(mi, hi, b):
            a0, a1 = HALVES[hi]
            dve(lambda e: e.tensor_tensor(out=mixedT[:, mi, a0:a1], in0=ps[:, b, 0:a1 - a0], in1=gaT[:, mi, a0:a1],
                                          op=ALU.mult), [("ps", b), ("gaT", mi, hi)], [("mixedT", mi, hi)])
        proj_fm(din["w_proj_ssm"], l, 8, 0, D, lambda k: ssm_outT[:, k, :],
                [("ssm_outT", k, hi) for k in range(8) for hi in range(2)], ev_ps)
        proj_fm(din["w_gate"], l, 16, D, D, lambda k: hT[:, k, :], hkeys(), ev_gate(16))

        def ev_pa(mi, hi, b):
            a0, a1 = HALVES[hi]
            dve(lambda e: e.tensor_tensor(out=ntmp[:, hi, 0:a1 - a0], in0=ps[:, b, 0:a1 - a0], in1=gaT[:, mi, a0:a1],
                                          op=ALU.mult), [("ps", b), ("gaT", mi, hi)], [("ntmp", hi, 0), ("ntmp", hi, 1)])
            dve(lambda e: e.tensor_tensor(out=mixedT[:, mi, a0:a1], in0=ntmp[:, hi, 0:a1 - a0],
                                          in1=mixedT[:, mi, a0:a1], op=ALU.add),
                [("ntmp", hi, 0), ("ntmp", hi, 1), ("mixedT", mi, hi)], [("mixedT", mi, hi)])
        akeys = [("attnT", h, qc) for h in range(4) for qc in list(range(0, 512, 64)) + [512]]
        proj_fm(din["w_proj_attn"], l, 16, 0, D, lambda k: attnT[:, k, :], akeys, ev_pa, kp=64)

        chk("merge")
        def ev_res(goff):
            def ev(mi, hi, b):
                for (a0, a1, s) in SEGS[hi]:
                    sidx = seqs[s]
                    h0_ = HALVES[hi][0]
                    dve(lambda e, a0=a0, a1=a1, sidx=sidx, h0_=h0_: e.scalar_tensor_tensor(
                        out=xT[:, mi, a0:a1], in0=ps[:, b, a0 - h0_:a1 - h0_],
                        scalar=modT[:, l, goff + mi, sidx:sidx + 1], in1=xT[:, mi, a0:a1],
                        op0=ALU.mult, op1=ALU.add),
                        [("ps", b), ("modT", l), ("xT", mi, hi)], [("xT", mi, hi)])
            return ev
        proj_fm(din["w_out"], l, 16, 0, D, lambda k: mixedT[:, k, :],
                [("mixedT", k, hi) for k in range(16) for hi in range(2)], ev_res(32))

        chk("outproj")
        norm_phase(l, 1, seqs)
        for sl in range(2):
            for jb in range(11):
                cb = sl * 2816 + jb * 256
                def ev_g(mi, hi, b):
                    a0, a1 = HALVES[hi]
                    act(lambda e: e.activation(out=rstd_g[hi][mi % 2][:, 0:a1 - a0], in_=ps[:, b, 0:a1 - a0],
                                               func=AF.Silu),
                        [("ps", b)], [("gs", mi % 2, hi)])
                proj_fm(din["w_ffn_gate"], l, 16, cb, 256, lambda k: hT[:, k, :], hkeys(), ev_g)

                def ev_up(mi, hi, b, jb=jb):
                    a0, a1 = HALVES[hi]
                    dve(lambda e: e.tensor_tensor(out=actT[:, 2 * jb + mi, a0:a1], in0=ps[:, b, 0:a1 - a0],
                                                  in1=rstd_g[hi][mi % 2][:, 0:a1 - a0], op=ALU.mult),
                        [("ps", b), ("gs", mi % 2, hi)], [("actT", 2 * jb + mi, hi)])
                proj_fm(din["w_ffn_up"], l, 16, cb, 256, lambda k: hT[:, k, :], hkeys(), ev_up)
            wdn = din["w_ffn_down"][:, sl * 2816:(sl + 1) * 2816, :]
            proj_fm(wdn, l, 22, 0, D, lambda k: actT[:, k, :],
                    [("actT", k, hi) for k in range(22) for hi in range(2)], ev_res(80))

    gsb = av("gs", X0 + 27648, [128, 4, 288], BF16)
    rstd_g = [[gsb[:, hi * 2 + m2, :] for m2 in range(2)] for hi in range(2)]

    def rr(out_ap, in_ap, shift, wtmp, ktmp, Rk, Wk, tmpk):
        dve(lambda e: e.tensor_scalar(out=wtmp, in0=in_ap, scalar1=float(shift), scalar2=None, op0=ALU.add),
            Rk, [tmpk[0]])
        dve(lambda e: e.tensor_scalar(out=ktmp, in0=wtmp, scalar1=1.0 / TWO_PI, scalar2=MAGIC, op0=ALU.mult, op1=ALU.add),
            [tmpk[0]], [tmpk[1]])
        dve(lambda e: e.tensor_scalar(out=ktmp, in0=ktmp, scalar1=-MAGIC, scalar2=None, op0=ALU.add),
            [tmpk[1]], [tmpk[1]])
        dve(lambda e: e.scalar_tensor_tensor(out=wtmp, in0=ktmp, scalar=-TWO_PI, in1=wtmp, op0=ALU.mult, op1=ALU.add),
            [tmpk[0], tmpk[1]], [tmpk[0]])
        dve(lambda e: e.tensor_scalar(out=out_ap, in0=wtmp, scalar1=3.1415925, scalar2=-3.1415925, op0=ALU.min, op1=ALU.max),
            [tmpk[0]], Wk)

    dt8 = sb("dt8", [128, 8])
    zsm = sb("zsm", [128, 3, 4])
    trst = ckst[0:32, :].rearrange("p (a b) -> p a b", a=2)

    def ssm_phase(g, l, sidx_s, write_s, last_g):
        P.add("sp", lambda e: e.dma_start(out=h0s[:, 0, :], in_=din["st_re"][l, sidx_s]), W=[("h0s", 0)], dma="h0s0")
        P.add("sp", lambda e: e.dma_start(out=h0s[:, 1, :], in_=din["st_im"][l, sidx_s]), W=[("h0s", 1)], dma="h0s1")
        T0, T1, T2, T3 = (tmpA[:, i_, :] for i_ in range(4))
        for ct in range(8):
            cs = slice(ct * 512, (ct + 1) * 512)
            SpF = SpT[:].rearrange("p a b c -> p a (b c)")
            P.add("pool", lambda e, ct=ct: e.dma_start(out=BBs[:, 0, :], in_=din["BBre"][l, :, ct, :]), W=[("BBs", 0)], dma="BBs0")
            P.add("pool", lambda e, ct=ct: e.dma_start(out=BBs[:, 1, :], in_=din["BBim"][l, :, ct, :]), W=[("BBs", 1)], dma="BBs1")
            P.add("pool", lambda e, ct=ct: e.dma_start(out=CCs[:, 0], in_=din["CCre"][l, :, 4 * ct:4 * ct + 4, :]), W=[("CCs", 0)], dma="CCs0")
            P.add("pool", lambda e, ct=ct: e.dma_start(out=CCs[:, 1], in_=din["CCim"][l, :, 4 * ct:4 * ct + 4, :]), W=[("CCs", 1)], dma="CCs1")
            if g > 0:
                P.add("sp", lambda e, ct=ct: e.dma_start(out=SmF[:], in_=scrS[l, ct]), R=[("scrS", l, ct)], W=[("SmF", 0), ("SmF", 1)], dma="scrS_ld")
                P.add("sp", lambda e, ct=ct: e.dma_start(out=SpF, in_=scrP[l, ct]), R=[("scrP", l, ct)], W=[("SpT", 0), ("SpT", 1)], dma="scrP_ld")
            else:
                P.add("sp", lambda e, cs=cs: e.dma_start(out=tabA[:, 0, :], in_=din["are_b"][:, l, cs]), W=[("tabA", 0)], dma="tabA0")
                P.add("sp", lambda e, cs=cs: e.dma_start(out=tabA[:, 1, :], in_=din["aim_b"][:, l, cs]), W=[("tabA", 1)], dma="tabA1")
                are, aim = tabA[:, 0, :], tabA[:, 1, :]
                act(lambda e, ct=ct: e.activation(out=dt8[:], in_=ldt_b[:, l, 8 * ct:8 * ct + 8], func=AF.Exp),
                    [("small", 0)], [("dt8", 0)])
                dtb = dt8[:].unsqueeze(2).broadcast_to([128, 8, 64])
                v3 = lambda a: a.rearrange("p (g q) -> p g q", g=8)
                zr, zi = SmF[:, 0, :], SmF[:, 1, :]
                dve(lambda e: e.tensor_tensor(out=v3(zr), in0=v3(are), in1=dtb, op=ALU.mult), [("tabA", 0), ("dt8", 0)], [("SmF", 0)])
                dve(lambda e: e.tensor_tensor(out=v3(zi), in0=v3(aim), in1=dtb, op=ALU.mult), [("tabA", 1), ("dt8", 0)], [("SmF", 1)])
                act(lambda e: e.activation(out=T0, in_=zr, func=AF.Exp), [("SmF", 0)], [("T", 0)])
                rr(T1, zi, 0.0, T1, T3, [("SmF", 1)], [("T", 1)], [("T", 1), ("T", 3)])
                act(lambda e: e.activation(out=T1, in_=T1, func=AF.Sin), [("T", 1)], [("T", 1)])
                rr(T2, zi, math.pi / 2, T2, T3, [("SmF", 1)], [("T", 2)], [("T", 2), ("T", 3)])
                act(lambda e: e.activation(out=T2, in_=T2, func=AF.Sin), [("T", 2)], [("T", 2)])
                dve(lambda e: e.tensor_tensor(out=T2, in0=T2, in1=T0, op=ALU.mult), [("T", 2), ("T", 0)], [("T", 2)])
                dve(lambda e: e.tensor_scalar(out=T2, in0=T2, scalar1=-1.0, scalar2=None, op0=ALU.add), [("T", 2)], [("T", 2)])
                dve(lambda e: e.tensor_tensor(out=T1, in0=T1, in1=T0, op=ALU.mult), [("T", 1), ("T", 0)], [("T", 1)])
                dve(lambda e: e.tensor_tensor(out=T0, in0=are, in1=are, op=ALU.mult), [("tabA", 0), ("T", 0)], [("T", 0)])
                dve(lambda e: e.tensor_tensor(out=T3, in0=aim, in1=aim, op=ALU.mult), [("tabA", 1)], [("T", 3)])
                dve(lambda e: e.tensor_tensor(out=T0, in0=T0, in1=T3, op=ALU.add), [("T", 0), ("T", 3)], [("T", 0)])
                dve(lambda e: e.reciprocal(out=T0, in_=T0), [("T", 0)], [("T", 0)])
                Gf = Gs[:].rearrange("p a b c -> p (a b c)")
                F0, F1 = Gf[:, 0:512], Gf[:, 512:1024]
                dve(lambda e: e.tensor_tensor(out=T3, in0=T2, in1=are, op=ALU.mult), [("T", 2), ("tabA", 0)], [("T", 3)])
                dve(lambda e: e.tensor_tensor(out=F0, in0=T1, in1=aim, op=ALU.mult), [("T", 1), ("tabA", 1)], [("Gs", 0)])
                dve(lambda e: e.tensor_tensor(out=F0, in0=F0, in1=T3, op=ALU.add), [("Gs", 0), ("T", 3)], [("Gs", 0)])
                dve(lambda e: e.tensor_tensor(out=F0, in0=F0, in1=T0, op=ALU.mult), [("Gs", 0), ("T", 0)], [("Gs", 0)])
                dve(lambda e: e.tensor_tensor(out=T3, in0=T1, in1=are, op=ALU.mult), [("T", 1), ("tabA", 0)], [("T", 3)])
                dve(lambda e: e.tensor_tensor(out=F1, in0=T2, in1=aim, op=ALU.mult), [("T", 2), ("tabA", 1)], [("Gs", 1)])
                dve(lambda e: e.tensor_tensor(out=F1, in0=T3, in1=F1, op=ALU.subtract), [("Gs", 1), ("T", 3)], [("Gs", 1)])
                dve(lambda e: e.tensor_tensor(out=F1, in0=F1, in1=T0, op=ALU.mult), [("Gs", 1), ("T", 0)], [("Gs", 1)])
                fre, fim = tabA[:, 0, :], tabA[:, 1, :]
                dve(lambda e: e.tensor_copy(out=fre, in_=F0), [("Gs", 0)], [("tabA", 0)])
                dve(lambda e: e.tensor_copy(out=fim, in_=F1), [("Gs", 1)], [("tabA", 1)])
                act(lambda e: e.activation(out=T0, in_=zr, func=AF.Exp, scale=negsp1[:, 0:1]), [("SmF", 0), ("small", 0), ("T", 0)], [("T", 0)])
                dve(lambda e: e.tensor_scalar(out=F0, in0=zi, scalar1=negsp1[:, 0:1], scalar2=None, op0=ALU.mult),
                    [("SmF", 1), ("small", 0)], [("Gs", 0)])
                rr(T1, F0, 0.0, T1, T3, [("Gs", 0)], [("T", 1)], [("T", 1), ("T", 3)])
                act(lambda e: e.activation(out=T1, in_=T1, func=AF.Sin), [("T", 1)], [("T", 1)])
                rr(T2, F0, math.pi / 2, T2, T3, [("Gs", 0)], [("T", 2)], [("T", 2), ("T", 3)])
                act(lambda e: e.activation(out=T2, in_=T2, func=AF.Sin), [("T", 2)], [("T", 2)])
                dve(lambda e: e.tensor_tensor(out=T1, in0=T1, in1=T0, op=ALU.mult), [("T", 1), ("T", 0)], [("T", 1)])
                dve(lambda e: e.tensor_tensor(out=T2, in0=T2, in1=T0, op=ALU.mult), [("T", 2), ("T", 0)], [("T", 2)])
                dve(lambda e: e.tensor_tensor(out=T0, in0=fre, in1=T2, op=ALU.mult), [("tabA", 0), ("T", 2)], [("T", 0)])
                dve(lambda e: e.tensor_tensor(out=T3, in0=fim, in1=T1, op=ALU.mult), [("tabA", 1), ("T", 1)], [("T", 3)])
                dve(lambda e: e.tensor_tensor(out=SmF[:, 0, :], in0=T0, in1=T3, op=ALU.subtract), [("T", 0), ("T", 3)], [("SmF", 0)])
                dve(lambda e: e.tensor_tensor(out=T0, in0=fre, in1=T1, op=ALU.mult), [("tabA", 0), ("T", 1)], [("T", 0)])
                dve(lambda e: e.tensor_tensor(out=T3, in0=fim, in1=T2, op=ALU.mult), [("tabA", 1), ("T", 2)], [("T", 3)])
                dve(lambda e: e.tensor_tensor(out=SmF[:, 1, :], in0=T0, in1=T3, op=ALU.add), [("T", 0), ("T", 3)], [("SmF", 1)])
                js = slice(4 * ct, 4 * ct + 4)
                act(lambda e, js=js: e.activation(out=zsm[:, 2, :], in_=ldt_sm[:, l, js], func=AF.Exp), [("small", 0)], [("zsm", 2)])
                dve(lambda e, js=js: e.tensor_tensor(out=zsm[:, 0, :], in0=are_sm[:, l, js], in1=zsm[:, 2, :], op=ALU.mult),
                    [("small", 0), ("zsm", 2)], [("zsm", 0)])
                dve(lambda e, js=js: e.tensor_tensor(out=zsm[:, 1, :], in0=aim_sm[:, l, js], in1=zsm[:, 2, :], op=ALU.mult),
                    [("small", 0), ("zsm", 2)], [("zsm", 1)])
                t3 = lambda a: a.rearrange("p (j t) -> p j t", j=4)
                tpb = tp1[:].unsqueeze(1).broadcast_to([128, 4, 128])
                dve(lambda e: e.tensor_tensor(out=t3(T0), in0=zsm[:, 0, :].unsqueeze(2).broadcast_to([128, 4, 128]), in1=tpb, op=ALU.mult),
                    [("zsm", 0), ("small", 0), ("T", 0)], [("T", 0)])
                act(lambda e: e.activation(out=T0, in_=T0, func=AF.Exp), [("T", 0)], [("T", 0)])
                dve(lambda e: e.tensor_tensor(out=t3(F0), in0=zsm[:, 1, :].unsqueeze(2).broadcast_to([128, 4, 128]), in1=tpb, op=ALU.mult),
                    [("zsm", 1), ("small", 0)], [("Gs", 0)])
                rr(T1, F0, 0.0, T1, T3, [("Gs", 0)], [("T", 1)], [("T", 1), ("T", 3)])
                act(lambda e: e.activation(out=T1, in_=T1, func=AF.Sin), [("T", 1)], [("T", 1)])
                rr(T2, F0, math.pi / 2, T2, T3, [("Gs", 0)], [("T", 2)], [("T", 2), ("T", 3)])
                act(lambda e: e.activation(out=T2, in_=T2, func=AF.Sin), [("T", 2)], [("T", 2)])
                dve(lambda e: e.tensor_tensor(out=SpF[:, 0, :], in0=T2, in1=T0, op=ALU.mult), [("T", 2), ("T", 0)], [("SpT", 0)])
                dve(lambda e: e.tensor_tensor(out=SpF[:, 1, :], in0=T1, in1=T0, op=ALU.mult), [("T", 1), ("T", 0)], [("SpT", 1)])


                P.add("sp", lambda e, ct=ct: e.dma_start(out=scrS[l, ct], in_=SmF[:]), R=[("SmF", 0), ("SmF", 1)], W=[("scrS", l, ct)], dma="scrS_st")
                P.add("sp", lambda e, ct=ct: e.dma_start(out=scrP[l, ct], in_=SpF), R=[("SpT", 0), ("SpT", 1)], W=[("scrP", l, ct)], dma="scrP_st")

            chunk_list = [("p", c_) for c_ in range(4)] + [("s", 0)]
            for (kind, c_) in chunk_list:
                rows = 128 if kind == "p" else 64
                c0 = 128 * c_ if kind == "p" else 512
                zero_h0 = (kind == "p" and g == 0 and c_ == 0)
                bR, bI = next_bank(), next_bank()
                pe(lambda e, rows=rows, c0=c0, bR=bR, ct=ct: e.matmul(ps[0:rows, bR, :], lhsT=uT[:, ct, c0:c0 + rows],
                                                                     rhs=BBs[:, 0, :], start=True, stop=True),
                   [("uT", ct, 0), ("uT", ct, 1), ("BBs", 0)], [("ps", bR)])
                pe(lambda e, rows=rows, c0=c0, bI=bI, ct=ct: e.matmul(ps[0:rows, bI, :], lhsT=uT[:, ct, c0:c0 + rows],
                                                                     rhs=BBs[:, 1, :], start=True, stop=True),
                   [("uT", ct, 0), ("uT", ct, 1), ("BBs", 1)], [("ps", bI)])
                R_ = slice(0, rows)
                dve(lambda e, R_=R_, bR=bR: e.tensor_tensor(out=T0[R_], in0=ps[R_, bR, :], in1=SmF[R_, 0, :], op=ALU.mult),
                    [("ps", bR), ("SmF", 0), ("T", 0)], [("T", 0)])
                dve(lambda e, R_=R_, bI=bI: e.tensor_tensor(out=T1[R_], in0=ps[R_, bI, :], in1=SmF[R_, 1, :], op=ALU.mult),
                    [("ps", bI), ("SmF", 1), ("T", 1)], [("T", 1)])
                dve(lambda e, R_=R_: e.tensor_tensor(out=Wt[R_, 0, :], in0=T0[R_], in1=T1[R_], op=ALU.subtract),
                    [("T", 0), ("T", 1)], [("Wt", 0)])
                dve(lambda e, R_=R_, bI=bI: e.tensor_tensor(out=T2[R_], in0=ps[R_, bI, :], in1=SmF[R_, 0, :], op=ALU.mult),
                    [("ps", bI), ("SmF", 0), ("T", 2)], [("T", 2)])
                dve(lambda e, R_=R_, bR=bR: e.tensor_tensor(out=T3[R_], in0=ps[R_, bR, :], in1=SmF[R_, 1, :], op=ALU.mult),
                    [("ps", bR), ("SmF", 1), ("T", 3)], [("T", 3)])
                dve(lambda e, R_=R_: e.tensor_tensor(out=Wt[R_, 1, :], in0=T2[R_], in1=T3[R_], op=ALU.add),
                    [("T", 2), ("T", 3)], [("Wt", 1)])
                bG = [next_bank(), next_bank()]
                for ri in range(2):
                    def gmm(e, ri=ri, rows=rows, bgi=bG[ri]):
                        for jj in range(4):
                            ins = e.matmul(ps[:, bgi, jj * 128:jj * 128 + rows], lhsT=Wt[0:rows, ri, jj * 128:(jj + 1) * 128],
                                           rhs=tri_b[0:rows, 0:rows], start=True, stop=True)
                        return ins
                    pe(gmm, [("Wt", ri), ("tri_b", 0)], [("ps", bG[ri])])
                for ri in range(2):
                    if zero_h0:
                        act(lambda e, ri=ri, rows=rows, bgi=bG[ri]: e.copy(
                            out=Gs[:, ri, :, 0:rows], in_=ps[:, bgi, :].rearrange("p (j t) -> p j t", j=4)[:, :, 0:rows]),
                            [("ps", bG[ri])], [("Gs", ri)])
                    else:
                        for jj in range(4):
                            if kind == "p":
                                hsrc = h0p[:, l, ri, 4 * ct + jj:4 * ct + jj + 1]
                                hk = ("h0p", l)
                            else:
                                hsrc = h0s[:, ri, 4 * ct + jj:4 * ct + jj + 1]
                                hk = ("h0s", ri)
                            act(lambda e, ri=ri, jj=jj, rows=rows, bgi=bG[ri], hsrc=hsrc: e.activation(
                                out=Gs[:, ri, jj, 0:rows], in_=ps[:, bgi, jj * 128:jj * 128 + rows], func=AF.Identity,
                                bias=hsrc, scale=1.0),
                                [("ps", bG[ri]), hk], [("Gs", ri)])
                tv = lambda a, rows=rows: a.rearrange("p (j t) -> p j t", j=4)[:, :, 0:rows]
                Gr, Gi = Gs[:, 0, :, 0:rows], Gs[:, 1, :, 0:rows]
                Sr, Si = SpT[:, 0, :, 0:rows], SpT[:, 1, :, 0:rows]
                dve(lambda e, tv=tv, Gr=Gr, Sr=Sr: e.tensor_tensor(out=tv(T0), in0=Sr, in1=Gr, op=ALU.mult), [("SpT", 0), ("Gs", 0), ("T", 0)], [("T", 0)])
                dve(lambda e, tv=tv, Gi=Gi, Si=Si: e.tensor_tensor(out=tv(T1), in0=Si, in1=Gi, op=ALU.mult), [("SpT", 1), ("Gs", 1), ("T", 1)], [("T", 1)])
                dve(lambda e, tv=tv, rows=rows: e.tensor_tensor(out=hS[:, 0, :, 0:rows], in0=tv(T0), in1=tv(T1), op=ALU.subtract),
                    [("T", 0), ("T", 1)], [("hS", 0)])
                dve(lambda e, tv=tv, Gi=Gi, Sr=Sr: e.tensor_tensor(out=tv(T2), in0=Sr, in1=Gi, op=ALU.mult), [("SpT", 0), ("Gs", 1), ("T", 2)], [("T", 2)])
                dve(lambda e, tv=tv, Gr=Gr, Si=Si: e.tensor_tensor(out=tv(T3), in0=Si, in1=Gr, op=ALU.mult), [("SpT", 1), ("Gs", 0), ("T", 3)], [("T", 3)])
                dve(lambda e, tv=tv, rows=rows: e.scalar_tensor_tensor(out=hS[:, 1, :, 0:rows], in0=tv(T2), scalar=-1.0, in1=tv(T3),
                                                                     op0=ALU.mult, op1=ALU.subtract),
                    [("T", 2), ("T", 3)], [("hS", 1)])
                lastc = lambda a, rows=rows: a.rearrange("p (j t) -> p j t", j=4)[:, :, rows - 1]
                if kind == "p":
                    dre, dim_, dk = h0p[:, l, 0, 4 * ct:4 * ct + 4], h0p[:, l, 1, 4 * ct:4 * ct + 4], [("h0p", l)]
                else:
                    dre, dim_, dk = hends[:, 0, 4 * ct:4 * ct + 4], hends[:, 1, 4 * ct:4 * ct + 4], [("hends", 0)]
                dve(lambda e, lastc=lastc, dre=dre: e.tensor_tensor(out=dre, in0=lastc(T0), in1=lastc(T1), op=ALU.subtract),
                    [("T", 0), ("T", 1)] + dk, dk)
                dve(lambda e, lastc=lastc, dim_=dim_: e.tensor_tensor(out=dim_, in0=lastc(T2), in1=lastc(T3), op=ALU.add),
                    [("T", 2), ("T", 3)] + dk, dk)
                bY = next_bank()

                def ymm(e, rows=rows, bY=bY):
                    for jj in range(4):
                        e.matmul(ps[:, bY, 0:rows], lhsT=CCs[:, 0, jj, :], rhs=hS[:, 0, jj, 0:rows], start=(jj == 0), stop=False)
                    for jj in range(4):
                        ins = e.matmul(ps[:, bY, 0:rows], lhsT=CCs[:, 1, jj, :], rhs=hS[:, 1, jj, 0:rows], start=False, stop=(jj == 3))
                    return ins
                pe(ymm, [("CCs", 0), ("CCs", 1), ("hS", 0), ("hS", 1)], [("ps", bY)])
                dve(lambda e, rows=rows, bY=bY, c0=c0, ct=ct: e.scalar_tensor_tensor(
                    out=ypre[:, c0:c0 + rows], in0=uT[:, ct, c0:c0 + rows], scalar=dTs[:, l, ct:ct + 1],
                    in1=ps[:, bY, 0:rows], op0=ALU.mult, op1=ALU.add),
                    [("ps", bY), ("uT", ct, 0), ("uT", ct, 1), ("small", 0)], [("ypre", c0)])
            yk = [("ypre", c) for c in (0, 128, 256, 384, 512)]
            Tf = tmpA[:].rearrange("p a b -> p (a b)")
            Y0, Y1 = Tf[:, 0:T], Tf[:, 1024:1024 + T]
            ka, kb = [("T", 0), ("T", 1)], [("T", 2), ("T", 3)]
            act(lambda e: e.activation(out=Y0, in_=ypre[:], func=AF.Square), yk + ka, ka)
            dve(lambda e: e.tensor_scalar(out=Y0, in0=Y0, scalar1=0.044715, scalar2=1.0, op0=ALU.mult, op1=ALU.add), ka, ka)
            dve(lambda e: e.tensor_tensor(out=Y0, in0=Y0, in1=ypre[:], op=ALU.mult), ka + yk, ka)
            act(lambda e: e.activation(out=Y1, in_=Y0, func=AF.Sigmoid, scale=1.5957691216057308), ka + kb, kb)
            dve(lambda e, ct=ct: e.tensor_tensor(out=zT[:, ct, :], in0=Y1, in1=ypre[:], op=ALU.mult), kb + yk, [("zT", ct)])
        def state_out(src_re, src_im, skeys, dre, dim_, tag):
            b = next_bank()

            def trs(e, b=b):
                e.transpose(out=ps[0:32, b, 0:128], in_=src_re, identity=ident_f[:])
                return e.transpose(out=ps[0:32, b, 128:256], in_=src_im, identity=ident_f[:])
            pe(trs, skeys + [("ident_f", 0)], [("ps", b)])
            act(lambda e, b=b: e.copy(out=trst[:].rearrange("p a b -> p (a b)"), in_=ps[0:32, b, 0:256]), [("ps", b)], [("ckst", 0)])
            P.add("sp", lambda e: e.dma_start(out=dre, in_=trst[:, 0, :]), R=[("ckst", 0)], W=[("o_st", tag, 0)], dma=("o_st", 0))
            P.add("sp", lambda e: e.dma_start(out=dim_, in_=trst[:, 1, :]), R=[("ckst", 0)], W=[("o_st", tag, 1)], dma=("o_st", 1))
        if write_s:
            state_out(hends[:, 0, :], hends[:, 1, :], [("hends", 0)], dout["sre"][l, sidx_s], dout["sim"][l, sidx_s], "s")
        if last_g:
            state_out(h0p[:, l, 0, :], h0p[:, l, 1, :], [("h0p", l)], dout["pre"][l], dout["pim"][l], "p")

    def _groups():
        for g in range(n_groups):
            P.add("sp", lambda e, g=g: e.dma_start(out=ropec[:], in_=din["ropec"][g]), W=[("rope", 0)], dma="ropec")
            P.add("sp", lambda e, g=g: e.dma_start(out=ropes[:], in_=din["ropes"][g]), W=[("rope", 0)], dma="ropes")
            sidx_s = g % 2
            for tt in range(5):
                rows = 128 if tt < 4 else 64
                c0 = tt * 128
                src = din["xp"][g * TP + c0: g * TP + c0 + 128, :] if tt < 4 else din["xs"][sidx_s]
                xb = tt % 2
                P.add("sp", lambda e, src=src, rows=rows, xb=xb: e.dma_start(out=xstage[0:rows, xb, :], in_=src),
                      W=[("xstage", xb)], dma=("xin", xb))
                for kq in range(4):
                    b = next_bank()

                    def trx(e, b=b, rows=rows, xb=xb, kq=kq):
                        for k4 in range(4):
                            k = kq * 4 + k4
                            ins = e.transpose(out=ps[:, b, k4 * 128:k4 * 128 + rows], in_=xstage[0:rows, xb, k * 128:(k + 1) * 128],
                                              identity=ident_f[0:rows, 0:rows])
                        return ins
                    pe(trx, [("xstage", xb), ("ident_f", 0)], [("ps", b)])
                    hi_keys = [("xT", kq * 4 + k4, hi) for k4 in range(4) for hi in range(2)]
                    act(lambda e, b=b, rows=rows, kq=kq, c0=c0: e.copy(
                        out=xT[:, kq * 4:kq * 4 + 4, c0:c0 + rows],
                        in_=ps[:, b, :].rearrange("p (k t) -> p k t", k=4)[:, :, 0:rows]),
                        [("ps", b)], hi_keys)
            for l in range(n_layers):
                layer(g, l)
            for tt in range(5):
                rows = 128 if tt < 4 else 64
                c0 = tt * 128
                if tt == 4 and g >= 2:
                    continue
                xb = tt % 2
                for kq in range(4):
                    b = next_bank()

                    def tro(e, b=b, rows=rows, kq=kq, c0=c0):
                        for k4 in range(4):
                            k = kq * 4 + k4
                            ins = e.transpose(out=ps[0:rows, b, k4 * 128:(k4 + 1) * 128], in_=xT[:, k, c0:c0 + rows],
                                              identity=ident_f[:])
                        return ins
                    pe(tro, [("xT", kq * 4 + k4, hi) for k4 in range(4) for hi in range(2)] + [("ident_f", 0)], [("ps", b)])
                    act(lambda e, b=b, rows=rows, kq=kq, xb=xb: e.copy(out=xstage[0:rows, xb, kq * 512:(kq + 1) * 512],
                                                                      in_=ps[0:rows, b, :]),
                        [("ps", b)], [("xstage", xb)])
                dst = dout["yp"][g * TP + c0: g * TP + c0 + 128, :] if tt < 4 else dout["ys"][sidx_s]
                P.add("sp", lambda e, dst=dst, rows=rows, xb=xb: e.dma_start(out=dst, in_=xstage[0:rows, xb, :]),
                      R=[("xstage", xb)], W=[("xstage", xb), ("o_y", xb)], dma=("o_y", xb))

    try:
        chk("setup")
        _groups()
    except _Stop:
        pass

    okeys = [k for k in P.lastw if isinstance(k[0], str) and k[0].startswith("o_")]
    P.add("sp", lambda e: e.nop(), R=okeys)
    P.emit(nc)
    return nc


def _rep(a, n=128):
    return np.ascontiguousarray(np.broadcast_to(a[None], (n,) + a.shape))


def prep_shared(inp):
    f = np.float32
    sh = {}
    for k in ["w_mod", "w_in", "w_glu", "w_gate", "w_proj_ssm", "w_proj_attn", "w_out", "w_ffn_gate", "w_ffn_up",
              "w_ffn_down"]:
        sh[k] = np.ascontiguousarray(inp[k], dtype=f)

    def colT(a, nt):
        return np.ascontiguousarray(a.reshape(L, nt, 128).transpose(2, 0, 1))
    sh["b_modT"] = colT(inp["b_mod"], 96)
    sh["n1T"] = colT(inp["norm1_g"], 16)
    sh["n2T"] = colT(inp["norm2_g"], 16)
    sh["b_gateT"] = colT(inp["b_gate"], 32)
    sh["b_gluT"] = colT(inp["b_glu"], 8)
    sh["dT"] = colT(inp["ssm_d"], 8)
    sh["gq"] = _rep(inp["q_norm_g"])
    sh["gk"] = _rep(inp["k_norm_g"])
    sh["sinkT"] = _rep(inp["attn_sink"])
    sh["are_b"] = _rep(inp["ssm_a_re"].reshape(L, 4096))
    sh["aim_b"] = _rep(inp["ssm_a_im"].reshape(L, 4096))
    sh["ldt_b"] = _rep(inp["ssm_log_dt"])

    def sm(a):
        return np.ascontiguousarray(a.reshape(L, 32, 2, 64).transpose(2, 3, 0, 1).reshape(128, L, 32))
    sh["are_sm"] = sm(inp["ssm_a_re"])
    sh["aim_sm"] = sm(inp["ssm_a_im"])
    sh["ldt_sm"] = sm(np.broadcast_to(inp["ssm_log_dt"][:, :, None], (L, 64, 64)))
    for nm, src in [("BBre", inp["ssm_b_re"]), ("BBim", inp["ssm_b_im"])]:
        bb = np.zeros((L, 8, 8, 16, 8, 64), f)
        s5 = src.reshape(L, 8, 8, 64, 16)
        for gl in range(8):
            bb[:, :, gl, :, gl, :] = s5[:, :, gl].transpose(0, 1, 3, 2)
        sh[nm] = np.ascontiguousarray(bb.transpose(0, 2, 3, 1, 4, 5).reshape(L, 128, 8, 512))
    for nm, src in [("CCre", inp["ssm_c_re"]), ("CCim", inp["ssm_c_im"])]:
        cc = np.zeros((L, 2, 64, 32, 8, 16), f)
        s6 = src.reshape(L, 32, 2, 16, 64)
        for j in range(32):
            for q in range(2):
                cc[:, q, :, j, (2 * j + q) % 8, :] = s6[:, j, q].transpose(0, 2, 1)
        sh[nm] = np.ascontiguousarray(cc.reshape(L, 128, 32, 128))
    half = 8
    inv_freq = (np.float32(500000.0) ** (-np.arange(half, dtype=f) / np.float32(half))).astype(f)
    rc = np.zeros((NGRP, 128, 5, 8), f)
    rs = np.zeros((NGRP, 128, 5, 8), f)
    for g in range(NGRP):
        for tt in range(5):
            if tt < 4:
                pos = (g * TP + tt * 128 + np.arange(128)).astype(f)
            else:
                pos = np.concatenate([(2048 + np.arange(64)).astype(f), np.zeros(64, f)])
            ang = (pos[:, None] * inv_freq[None, :]).astype(f)
            rc[g, :, tt] = np.cos(ang)
            rs[g, :, tt] = np.sin(ang)
    sh["ropec"] = rc
    sh["ropes"] = rs
    sh["tri"] = np.triu(np.ones((128, 128), f))
    sh["tp1"] = _rep(np.arange(1, 129, dtype=f))
    sh["negsp1"] = -np.arange(1, 129, dtype=f).reshape(128, 1)
    return sh


def prep_core(inp, c):
    f = np.float32
    b = c % 4
    m = {}
    m["xp"] = np.ascontiguousarray(inp["x_prompt"][b], dtype=f)
    m["xs"] = np.ascontiguousarray(inp["x_sample"][2 * c:2 * c + 2], dtype=f)
    m["ck"] = np.ascontiguousarray(inp["cache_k"][:, 2 * c:2 * c + 2].reshape(L, 2, 128, 256), dtype=f)
    m["cv"] = np.ascontiguousarray(inp["cache_v"][:, 2 * c:2 * c + 2].reshape(L, 2, 128, 256), dtype=f)

    def st(a):
        return np.ascontiguousarray(a.reshape(L, 2, 32, 2, 64).transpose(0, 1, 3, 4, 2).reshape(L, 2, 128, 32), dtype=f)
    m["st_re"] = st(inp["state_ssm_re"][:, 2 * c:2 * c + 2])
    m["st_im"] = st(inp["state_ssm_im"][:, 2 * c:2 * c + 2])
    cc = np.stack([inp["c_prompt"][b], inp["c_sample"][2 * c], inp["c_sample"][2 * c + 1]], axis=1)
    m["cT"] = np.ascontiguousarray(cc.reshape(16, 128, 3).transpose(1, 0, 2), dtype=f)
    return m


_NC_CACHE = {}


def kernel(**inputs):
    inp = {k: np.asarray(v) for k, v in inputs.items()}
    if "nc" not in _NC_CACHE:
        _NC_CACHE["nc"] = build_program()
    nc = _NC_CACHE["nc"]
    sh = prep_shared(inp)
    in_maps = []
    for c in range(8):
        m = dict(sh)
        m.update(prep_core(inp, c))
        in_maps.append(m)
    res = run_bass_kernel_spmd(nc, in_maps, core_ids=list(range(8)))
    r = res.results
    f = np.float32
    y_prompt = np.stack([r[b]["yp"] for b in range(4)]).astype(f)
    y_sample = np.concatenate([r[c]["ys"] for c in range(8)], axis=0).astype(f)
    pk = np.stack([r[b]["pk"] for b in range(4)], axis=1).reshape(L, 4, 128, 4, 64).astype(f)
    pv = np.stack([r[b]["pv"] for b in range(4)], axis=1).reshape(L, 4, 128, 4, 64).astype(f)
    pre = np.stack([r[b]["pre"] for b in range(4)], axis=1).reshape(L, 4, 64, 64).astype(f)
    pim = np.stack([r[b]["pim"] for b in range(4)], axis=1).reshape(L, 4, 64, 64).astype(f)
    sk = np.concatenate([r[c]["sk"] for c in range(8)], axis=1).reshape(L, 16, 128, 4, 64).astype(f)
    sv = np.concatenate([r[c]["sv"] for c in range(8)], axis=1).reshape(L, 16, 128, 4, 64).astype(f)
    sre = np.concatenate([r[c]["sre"] for c in range(8)], axis=1).reshape(L, 16, 64, 64).astype(f)
    sim = np.concatenate([r[c]["sim"] for c in range(8)], axis=1).reshape(L, 16, 64, 64).astype(f)
    return (y_prompt, y_sample, pk, pv, pre, pim, sk, sv, sre, sim)
```

```python
import contextlib
import math

import numpy as np
import concourse.bass as bass
import concourse.mybir as mybir
from concourse.bass_utils import run_bass_kernel_spmd

F32 = mybir.dt.float32
BF16 = mybir.dt.bfloat16
AF = mybir.ActivationFunctionType
ALU = mybir.AluOpType
AX = mybir.AxisListType

ENGS = ("pe", "act", "dve", "pool", "sp")

D = 2048
L = 4
TP = 512
TS = 64
T = TP + TS
NGRP = 4
DFF = 5632
EPS = 1e-6
MAGIC = 12582912.0
TWO_PI = 2.0 * math.pi


class _Stop(Exception):
    pass


def chk(name):
    import os
    if os.environ.get("KSTOP", "") == name:
        raise _Stop()


class Prog:
    def __init__(self):
        self.ops = []
        self.lastw = {}
        self.readers = {}
        self.nameacc = {}
        self.aliases = {}
        self.dma_count = {}

    def alias(self, a, b):
        self.aliases.setdefault(a, set()).add(b)
        self.aliases.setdefault(b, set()).add(a)

    def add(self, eng, fn, R=(), W=(), dma=None):
        oid = len(self.ops)
        deps = set()
        names = set()
        for k in R:
            names.add(k[0])
            if k in self.lastw:
                deps.add(self.lastw[k])
        for k in W:
            names.add(k[0])
            if k in self.lastw:
                deps.add(self.lastw[k])
            for r in self.readers.get(k, ()):
                deps.add(r)
        for n in names:
            for o in self.aliases.get(n, ()):
                for v in self.nameacc.get(o, {}).values():
                    deps.add(v)
        op = dict(id=oid, eng=eng, fn=fn, deps=deps, dma=dma, inc=False)
        if dma is not None:
            self.dma_count[dma] = self.dma_count.get(dma, 0) + 1
            op["dma_val"] = 16 * self.dma_count[dma]
        self.ops.append(op)
        for k in W:
            self.lastw[k] = oid
            self.readers[k] = []
        for k in R:
            if k not in W:
                self.readers.setdefault(k, []).append(oid)
        for n in names:
            self.nameacc.setdefault(n, {})[(eng, dma)] = oid
        return oid

    def emit(self, nc):
        ops = self.ops
        needed = set()
        for op in ops:
            for d in op["deps"]:
                p = ops[d]
                if p["dma"] is None:
                    if p["eng"] == "pe" and op["eng"] == "pe":
                        continue
                    needed.add(d)
        cnt = {e: 0 for e in ENGS}
        for op in ops:
            if op["dma"] is None and op["id"] in needed:
                cnt[op["eng"]] += 1
                op["inc"] = True
                op["val"] = cnt[op["eng"]]
        waited = {}
        for op in ops:
            w = {}
            for d in op["deps"]:
                p = ops[d]
                if p["dma"] is not None:
                    key = ("dma", p["dma"])
                    val = p["dma_val"]
                else:
                    if p["eng"] == "pe" and op["eng"] == "pe":
                        continue
                    key = ("eng", p["eng"])
                    val = p["val"]
                if w.get(key, 0) < val:
                    w[key] = val
            mw = waited.setdefault(op["eng"], {})
            waits = []
            for key, val in w.items():
                if mw.get(key, 0) >= val:
                    continue
                mw[key] = val
                waits.append((key, val))
            op["waits"] = waits
        dma_keys = sorted(self.dma_count.keys(), key=str)
        if _os.environ.get("KSIM"):
            semv = {}
            per = {e: [op for op in ops if op["eng"] == e] for e in ENGS}
            ptr = {e: 0 for e in ENGS}
            prog = True
            while prog:
                prog = False
                for e in ENGS:
                    while ptr[e] < len(per[e]):
                        op = per[e][ptr[e]]
                        if all(semv.get(k, 0) >= v for k, v in op["waits"]):
                            if op["dma"] is not None:
                                semv[("dma", op["dma"])] = semv.get(("dma", op["dma"]), 0) + 16
                            elif op["inc"]:
                                semv[("eng", e)] = semv.get(("eng", e), 0) + 1
                            ptr[e] += 1
                            prog = True
                        else:
                            break
            for e in ENGS:
                if ptr[e] < len(per[e]):
                    op = per[e][ptr[e]]
                    print("DEADLOCK", e, ptr[e], len(per[e]), op["id"], op["waits"], {k: semv.get(k, 0) for k, _ in op["waits"]})
            print("SIM done", {e: (ptr[e], len(per[e])) for e in ENGS}, {k: v for k, v in semv.items()}, flush=True)
        with contextlib.ExitStack() as st:
            sems = {}
            for e in ENGS:
                sems[("eng", e)] = st.enter_context(nc.semaphore("s_" + e))
            for i, k in enumerate(dma_keys):
                sems[("dma", k)] = st.enter_context(nc.semaphore("d%d" % i))
            block = st.enter_context(nc.Block())
            per_eng = {e: [op for op in ops if op["eng"] == e] for e in ENGS}

            def run(engobj, lst):
                for op in lst:
                    for key, val in op["waits"]:
                        engobj.wait_ge(sems[key], val)
                    ins = op["fn"](engobj)
                    if op["dma"] is not None:
                        ins.then_inc(sems[("dma", op["dma"])], 16)
                    elif op["inc"]:
                        ins.then_inc(sems[("eng", op["eng"])], 1)

            @block.tensor
            def _(e):
                run(e, per_eng["pe"])

            @block.scalar
            def _(e):
                run(e, per_eng["act"])

            @block.vector
            def _(e):
                run(e, per_eng["dve"])

            @block.gpsimd
            def _(e):
                run(e, per_eng["pool"])

            @block.sync
            def _(e):
                run(e, per_eng["sp"])
        return cnt


import os as _os
WL = int(_os.environ.get("KWL", L))
IN_SHAPES = {
    "xp": [2048, D], "xs": [2, TS, D],
    "ck": [L, 2, 128, 256], "cv": [L, 2, 128, 256],
    "st_re": [L, 2, 128, 32], "st_im": [L, 2, 128, 32],
    "cT": [128, 16, 3],
    "w_mod": [WL, D, 6 * D], "w_in": [WL, D, 2560], "w_glu": [WL, 1024, 1024],
    "w_gate": [WL, D, 2 * D], "w_proj_ssm": [WL, 1024, D], "w_proj_attn": [WL, 1024, D],
    "w_out": [WL, D, D], "w_ffn_gate": [WL, D, DFF], "w_ffn_up": [WL, D, DFF],
    "w_ffn_down": [WL, DFF, D],
    "b_modT": [128, L, 96], "n1T": [128, L, 16], "n2T": [128, L, 16],
    "b_gateT": [128, L, 32], "b_gluT": [128, L, 8], "dT": [128, L, 8],
    "gq": [128, L, 64], "gk": [128, L, 64], "sinkT": [128, L, 16],
    "are_b": [128, L, 4096], "aim_b": [128, L, 4096], "ldt_b": [128, L, 64],
    "are_sm": [128, L, 32], "aim_sm": [128, L, 32], "ldt_sm": [128, L, 32],
    "BBre": [L, 128, 8, 512], "BBim": [L, 128, 8, 512],
    "CCre": [L, 128, 32, 128], "CCim": [L, 128, 32, 128],
    "ropec": [NGRP, 128, 5, 8], "ropes": [NGRP, 128, 5, 8],
    "tri": [128, 128], "tp1": [128, 128], "negsp1": [128, 1],
}
OUT_SHAPES = {
    "yp": [2048, D], "ys": [2, TS, D],
    "pk": [L, 128, 256], "pv": [L, 128, 256],
    "pre": [L, 32, 128], "pim": [L, 32, 128],
    "sk": [L, 2, 128, 256], "sv": [L, 2, 128, 256],
    "sre": [L, 2, 32, 128], "sim": [L, 2, 32, 128],
}

HALVES = [(0, 288), (288, 576)]
SEGS = [[(0, 288, 0)], [(288, 512, 0), (512, 576, 1)]]
ALLSEG = [(0, 512, 0), (512, 576, 1)]


def build_program(n_layers=L, n_groups=NGRP):
    nc = bass.Bass("TRN2", target_bir_lowering=False)
    P = Prog()
    din = {k: nc.dram_tensor(k, s, F32, kind="ExternalInput").ap() for k, s in IN_SHAPES.items()}
    dout = {k: nc.dram_tensor(k, s, F32, kind="ExternalOutput").ap() for k, s in OUT_SHAPES.items()}
    scrS = nc.dram_tensor("scrS", [L, 8, 128, 2, 512], F32).ap()
    scrP = nc.dram_tensor("scrP", [L, 8, 128, 2, 512], F32).ap()
    st = contextlib.ExitStack()

    def sb(name, shape, dt=F32):
        return st.enter_context(nc.sbuf_tensor("sb_" + name, shape, dt))

    xT = sb("xT", [128, 16, T])
    modT = sb("modT", [128, L, 96, 3])
    scp = sb("scp", [128, L, 2, 16, 3])
    ones_b = sb("ones_b", [128, 128], BF16)
    ident_f = sb("ident_f", [128, 128])
    ident_b = sb("ident_b", [128, 128], BF16)
    tri_b = sb("tri_b", [128, 128], BF16)
    tp1 = sb("tp1", [128, 128])
    negsp1 = sb("negsp1", [128, 1])
    cTb = sb("cTb", [128, 16, 3], BF16)
    b_modT = sb("b_modT", [128, L, 96])
    n1T = sb("n1T", [128, L, 16])
    n2T = sb("n2T", [128, L, 16])
    b_gateT = sb("b_gateT", [128, L, 32])
    b_gluT = sb("b_gluT", [128, L, 8])
    dTs = sb("dTs", [128, L, 8])
    gq = sb("gq", [128, L, 64])
    gk = sb("gk", [128, L, 64])
    esink = sb("esink", [128, L, 16])
    are_sm = sb("are_sm", [128, L, 32])
    aim_sm = sb("aim_sm", [128, L, 32])
    ldt_sm = sb("ldt_sm", [128, L, 32])
    ldt_b = sb("ldt_b", [128, L, 64])
    haloKT = sb("haloKT", [64, L, 4, 128], BF16)
    haloV = sb("haloV", [128, L, 4, 65], BF16)
    h0p = sb("h0p", [128, L, 2, 32])
    hends = sb("hends", [128, 2, 32])
    h0s = sb("h0s", [128, 2, 32])
    ropec = sb("ropec", [128, 5, 8])
    ropes = sb("ropes", [128, 5, 8])
    rstd = sb("rstd", [128, T])
    ntmp = sb("ntmp", [128, 2, T])
    ring = sb("ring", [128, 5, 4096], BF16)
    qsq = sb("qsq", [128, 256])
    qn = sb("qn", [128, 2, 256])
    qss = sb("qss", [128, 2, 4])
    qrt = sb("qrt", [128, 4, 4, 8])
    qtm = sb("qtm", [128, 2, 256], BF16)
    vf = sb("vf", [128, 2, 256])
    ckst = sb("ckst", [128, 256])
    cvst = sb("cvst", [128, 256])
    ckb = sb("ckb", [128, 256], BF16)
    pT = sb("pT", [128, 2, 2, 256], BF16)
    adn = sb("adn", [64, 2, 256])
    ARENA = 36864 + 53248 + 512
    arena = sb("arena", [128, ARENA // 2], BF16)
    ps = st.enter_context(nc.psum_tensor("ps", [128, 8, 512], F32))

    X0 = 36864
    arena_bufs = {}

    def av(name, off, shape, dt, parts=128):
        nel = int(np.prod(shape[1:]))
        esz = 2 if dt == BF16 else 4
        nbytes = nel * esz
        assert off % 4 == 0 and off + nbytes <= ARENA, (name, off, nbytes)
        v = arena[0:parts, off // 2: (off + nbytes) // 2]
        if dt != BF16:
            v = v.bitcast(dt)
        if len(shape) == 3:
            v = v.rearrange("p (a b) -> p a b", a=shape[1])
        elif len(shape) == 4:
            v = v.rearrange("p (a b c) -> p a b c", a=shape[1], b=shape[2])
        for n2, (o2, b2) in arena_bufs.items():
            if off < o2 + b2 and o2 < off + nbytes:
                P.alias(name, n2)
        arena_bufs[name] = (off, nbytes)
        return v

    hT = av("hT", 0, [128, 16, T], BF16)
    attnT = av("attnT", 18432, [64, 16, T], BF16, parts=64)
    sq = av("sq", X0, [128, 16, T], BF16)
    actT = av("actT", X0, [128, 22, T], BF16)
    xstage = av("xstage", X0, [128, 2, 2048], F32)
    uT = av("uT", X0, [128, 8, T], BF16)
    QT = av("QT", X0 + 9216, [64, 16, T], BF16, parts=64)
    KT = av("KT", X0 + 27648, [64, 4, 832], BF16, parts=64)
    Vaug = av("Vaug", X0 + 34304, [128, 7, 4, 65], BF16)
    zT = av("zT", X0 + 9216, [128, 8, T], BF16)
    ssm_outT = av("ssm_outT", X0 + 18432, [128, 8, T], BF16)
    gaT = av("gaT", X0, [128, 16, T], BF16)
    mixedT = av("mixedT", X0 + 27648, [128, 16, T], BF16)
    SW = X0 + 18432
    SmF = av("SmF", SW, [128, 2, 512], F32)
    SpT = av("SpT", SW + 4096, [128, 2, 4, 128], F32)
    BBs = av("BBs", SW + 8192, [128, 2, 512], BF16)
    CCs = av("CCs", SW + 10240, [128, 2, 4, 128], BF16)
    Wt = av("Wt", SW + 12288, [128, 2, 512], BF16)
    Gs = av("Gs", SW + 14336, [128, 2, 4, 128], F32)
    hS = av("hS", SW + 18432, [128, 2, 4, 128], BF16)
    tmpA = av("T", SW + 20480, [128, 4, 512], F32)
    tabA = av("tabA", SW + 28672, [128, 2, 512], F32)
    ypre = av("ypre", SW + 32768, [128, T], F32)
    assert SW + 32768 + T * 4 <= ARENA

    state = dict(bank=0, wslot=0)

    def next_bank():
        b = state["bank"]
        state["bank"] = (b + 1) % 7
        return b

    def dve(fn, R, W):
        return P.add("dve", fn, R, W)

    def act(fn, R, W):
        return P.add("act", fn, R, W)

    def pe(fn, R, W):
        return P.add("pe", fn, R, W)

    def wnext(src, parts, kt, ncols):
        s = state["wslot"]
        state["wslot"] = (s + 1) % 5
        view = ring[0:parts, s, 0:kt * ncols].rearrange("p (k n) -> p k n", k=kt)
        P.add("pool", lambda e: e.dma_start(out=view, in_=src), W=[("ring", s)], dma=("ring", s))
        return view, ("ring", s)

    def wsrc(w, l, k0, kt, c0, ncols, kp=128):
        return w[l, k0 * kp:(k0 + kt) * kp, c0:c0 + ncols].rearrange("(k p) n -> p k n", p=kp)

    def proj_fm(w, l, ktot, c0, ncols, rhs_fn, rhs_keys, evac, colsplit=HALVES, kp=128):
        kt_max = 16
        tile_cols = 4096 // min(ktot, kt_max)
        tile_cols = min(tile_cols, 512, ncols)
        kchunks = [(k0, min(kt_max, ktot - k0)) for k0 in range(0, ktot, kt_max)]
        for cb in range(c0, c0 + ncols, tile_cols):
            nm = tile_cols // 128
            groups = [(m, hi) for m in range(nm) for hi in range(len(colsplit))]
            if len(kchunks) == 1:
                k0, kt = kchunks[0]
                view, wkey = wnext(wsrc(w, l, k0, kt, cb, tile_cols, kp), kp, kt, tile_cols)
                for (m, hi) in groups:
                    b = next_bank()
                    a0, a1 = colsplit[hi]

                    def mm(e, view=view, m=m, a0=a0, a1=a1, b=b, kt=kt):
                        for k in range(kt):
                            ins = e.matmul(ps[:, b, 0:a1 - a0], lhsT=view[:, k, m * 128:(m + 1) * 128],
                                           rhs=rhs_fn(k)[:, a0:a1], start=(k == 0), stop=(k == kt - 1))
                        return ins
                    pe(mm, R=[wkey] + rhs_keys, W=[("ps", b)])
                    evac((cb - c0) // 128 + m, hi, b)
            else:
                banks = {gk_: next_bank() for gk_ in groups}
                for (k0, kt) in kchunks:
                    view, wkey = wnext(wsrc(w, l, k0, kt, cb, tile_cols, kp), kp, kt, tile_cols)
                    for (m, hi) in groups:
                        b = banks[(m, hi)]
                        a0, a1 = colsplit[hi]

                        def mm(e, view=view, m=m, a0=a0, a1=a1, b=b, kt=kt, k0=k0):
                            for k in range(kt):
                                ins = e.matmul(ps[:, b, 0:a1 - a0], lhsT=view[:, k, m * 128:(m + 1) * 128],
                                               rhs=rhs_fn(k0 + k)[:, a0:a1], start=(k0 + k == 0),
                                               stop=(k0 + k == ktot - 1))
                            return ins
                        pe(mm, R=[wkey] + rhs_keys, W=[("ps", b)])
                for (m, hi) in groups:
                    evac((cb - c0) // 128 + m, hi, banks[(m, hi)])

    def small_load(dst, src, eng="sp"):
        P.add(eng, lambda e: e.dma_start(out=dst, in_=src), W=[("small", 0)], dma="small" + eng)

    for dst, nm in [(b_modT, "b_modT"), (n1T, "n1T"), (n2T, "n2T"), (b_gateT, "b_gateT"), (b_gluT, "b_gluT"),
                    (dTs, "dT"), (gq, "gq"), (gk, "gk"), (esink, "sinkT"), (are_sm, "are_sm"),
                    (aim_sm, "aim_sm"), (ldt_sm, "ldt_sm"), (ldt_b, "ldt_b"), (tp1, "tp1"), (negsp1, "negsp1")]:
        small_load(dst[:], din[nm])
    P.add("pool", lambda e: e.dma_start(out=tri_b[:], in_=din["tri"]), W=[("tri_b", 0)], dma="tri")
    P.add("pool", lambda e: e.dma_start(out=cTb[:], in_=din["cT"]), W=[("cTb", 0)], dma="cTb")
    dve(lambda e: e.memset(ones_b[:], 1.0), [], [("ones_b", 0)])
    dve(lambda e: e.memset(ident_f[:], 0.0), [], [("ident_f", 0)])
    P.add("pool", lambda e: e.affine_select(out=ident_f[:], in_=ident_f[:], pattern=[[-1, 128]],
                                            compare_op=ALU.not_equal, fill=1.0, base=0, channel_multiplier=1),
          R=[("ident_f", 0)], W=[("ident_f", 0)])
    dve(lambda e: e.tensor_copy(out=ident_b[:], in_=ident_f[:]), [("ident_f", 0)], [("ident_b", 0)])
    act(lambda e: e.activation(out=esink[:], in_=esink[:], func=AF.Exp), [("small", 0)], [("esink", 0)])
    dve(lambda e: e.memset(h0p[:], 0.0), [], [("h0p", l_) for l_ in range(L)])

    def mod_part(l, c0, ncols):
        def ev_mod(mi, hi, b, l=l, c0=c0):
            mg = c0 // 128 + mi
            act(lambda e: e.activation(out=modT[:, l, mg, :], in_=ps[:, b, 0:3], func=AF.Identity,
                                       bias=b_modT[:, l, mg:mg + 1], scale=1.0),
                [("ps", b), ("small", 0)], [("modT", l)])
        proj_fm(din["w_mod"], l, 16, c0, ncols, lambda k: cTb[:, k, :], [("cTb", 0)], ev_mod, colsplit=[(0, 3)])

    def mod_derive(l):
        for which, (nT, off) in enumerate([(n1T, 16), (n2T, 64)]):
            dve(lambda e, l=l, which=which, off=off: e.tensor_scalar(
                out=scp[:, l, which], in0=modT[:, l, off:off + 16, :], scalar1=1.0, scalar2=None, op0=ALU.add),
                [("modT", l)], [("scp", l)])
            dve(lambda e, l=l, which=which, nT=nT: e.tensor_tensor(
                out=scp[:, l, which], in0=scp[:, l, which],
                in1=nT[:, l, :].unsqueeze(2).broadcast_to([128, 16, 3]), op=ALU.mult),
                [("scp", l), ("small", 0)], [("scp", l)])

    mod_part(0, 0, 6 * D)
    mod_derive(0)

    def hkeys():
        return [("hT", k, hi) for k in range(16) for hi in range(2)]

    def norm_phase(l, which, seqs):
        shoff = 0 if which == 0 else 48
        act(lambda e: e.activation(out=sq[:], in_=xT[:], func=AF.Square),
            [("xT", k, hi) for k in range(16) for hi in range(2)], [("sq", 0)])
        for hi, (a0, a1) in enumerate(HALVES):
            b = next_bank()

            def mm(e, a0=a0, a1=a1, b=b):
                for k in range(16):
                    ins = e.matmul(ps[:, b, 0:a1 - a0], lhsT=ones_b[:], rhs=sq[:, k, a0:a1],
                                   start=(k == 0), stop=(k == 15))
                return ins
            pe(mm, [("sq", 0), ("ones_b", 0)], [("ps", b)])
            act(lambda e, a0=a0, a1=a1, b=b: e.activation(out=rstd[:, a0:a1], in_=ps[:, b, 0:a1 - a0],
                                                         func=AF.Sqrt, bias=EPS, scale=1.0 / D),
                [("ps", b)], [("rstd", hi)])
            dve(lambda e, a0=a0, a1=a1: e.reciprocal(out=rstd[:, a0:a1], in_=rstd[:, a0:a1]),
                [("rstd", hi)], [("rstd", hi)])
        for k in range(16):
            tb = k % 2
            for (a0, a1, s) in ALLSEG:
                sidx = seqs[s]
                dve(lambda e, k=k, a0=a0, a1=a1, sidx=sidx, tb=tb: e.scalar_tensor_tensor(
                    out=ntmp[:, tb, a0:a1], in0=xT[:, k, a0:a1], scalar=scp[:, l, which, k, sidx:sidx + 1],
                    in1=rstd[:, a0:a1], op0=ALU.mult, op1=ALU.mult),
                    [("xT", k, 0), ("xT", k, 1), ("rstd", 0), ("rstd", 1), ("scp", l)], [("ntmp", tb, s)])
                act(lambda e, k=k, a0=a0, a1=a1, sidx=sidx, tb=tb: e.activation(
                    out=hT[:, k, a0:a1], in_=ntmp[:, tb, a0:a1], func=AF.Identity,
                    bias=modT[:, l, shoff + k, sidx:sidx + 1], scale=1.0),
                    [("ntmp", tb, s), ("modT", l)], [("hT", k, 0), ("hT", k, 1)])

    def layer(g, l):
        sidx_s = g % 2
        seqs = [0, 1 + sidx_s]
        write_s = g < 2
        last_g = (g == n_groups - 1)
        chk("xload")
        norm_phase(l, 0, seqs)
        chk("norm1")

        def ev_u(mi, hi, b):
            a0, a1 = HALVES[hi]
            act(lambda e: e.copy(out=uT[:, mi, a0:a1], in_=ps[:, b, 0:a1 - a0]), [("ps", b)], [("uT", mi, hi)])
        proj_fm(din["w_in"], l, 16, 0, 1024, lambda k: hT[:, k, :], hkeys(), ev_u)

        chk("uproj")
        dve(lambda e: e.memset(Vaug[:, :, :, 64:65], 1.0), [], [("Vaug", t_) for t_ in range(7)])
        if g > 0:
            dve(lambda e: e.tensor_copy(out=KT[:, :, 0:128], in_=haloKT[:, l]), [("haloKT", l)], [("KT", 0)])
            dve(lambda e: e.tensor_copy(out=Vaug[:, 0, :, 0:64], in_=haloV[:, l, :, 0:64]), [("haloV", l)], [("Vaug", 0)])
        P.add("sp", lambda e: e.dma_start(out=ckst[:], in_=din["ck"][l, sidx_s]), W=[("ckst", 0)], dma="ckst")
        P.add("sp", lambda e: e.dma_start(out=cvst[:], in_=din["cv"][l, sidx_s]), W=[("cvst", 0)], dma="cvst")
        if write_s:
            P.add("sp", lambda e: e.dma_start(out=dout["sk"][l, sidx_s, 0:64, :], in_=din["ck"][l, sidx_s, 64:128, :]),
                  W=[("o_skc", 0)], dma="o_skc")
            P.add("sp", lambda e: e.dma_start(out=dout["sv"][l, sidx_s, 0:64, :], in_=din["cv"][l, sidx_s, 64:128, :]),
                  W=[("o_svc", 0)], dma="o_svc")
        dve(lambda e: e.tensor_copy(out=ckb[:], in_=ckst[:]), [("ckst", 0)], [("ckb", 0)])
        dve(lambda e: e.tensor_copy(out=Vaug[:, 5, :, 0:64], in_=cvst[:].rearrange("p (h d) -> p h d", h=4)),
            [("cvst", 0)], [("Vaug", 5)])
        b = next_bank()
        psb = ps[:, b, :].bitcast(BF16)

        def tr_ck(e, psb=psb):
            for h in range(4):
                ins = e.transpose(out=psb[0:64, h * 128:(h + 1) * 128], in_=ckb[:, h * 64:(h + 1) * 64],
                                  identity=ident_b[:])
            return ins
        pe(tr_ck, [("ckb", 0), ("ident_b", 0)], [("ps", b)])
        act(lambda e, psb=psb: e.copy(out=KT[:, :, 640:768], in_=psb[0:64, 0:512].rearrange("p (h t) -> p h t", h=4)),
            [("ps", b)], [("KT", 5)])

        chk("halo")
        for wt in range(6):
            view, wkey = wnext(wsrc(din["w_in"], l, 0, 16, 1024 + 256 * wt, 256), 128, 16, 256)
            for tt in range(5):
                rows = 128 if tt < 4 else 64
                c0 = tt * 128
                b = next_bank()

                def mm(e, view=view, c0=c0, rows=rows, b=b):
                    for k in range(16):
                        ins = e.matmul(ps[0:rows, b, 0:256], lhsT=hT[:, k, c0:c0 + rows], rhs=view[:, k, :],
                                       start=(k == 0), stop=(k == 15))
                    return ins
                pe(mm, [wkey] + hkeys(), [("ps", b)])
                src = ps[0:rows, b, 0:256]
                _qs = _os.environ.get("KQSUB", "")
                state["qit"] = state.get("qit", 0) + 1
                if state["qit"] > int(_os.environ.get("KQKV", "1000")):
                    raise _Stop()
                if _qs == "mm":
                    continue
                if wt == 5:
                    vt = tt + 1 if tt < 4 else 6
                    dve(lambda e, src=src, rows=rows, vt=vt: e.tensor_copy(
                        out=Vaug[0:rows, vt, :, 0:64], in_=src.rearrange("p (h d) -> p h d", h=4)),
                        [("ps", b)], [("Vaug", vt)])
                    need_out = (tt == 3 and last_g) or (tt == 4 and write_s)
                    if need_out:
                        vb = tt % 2
                        dve(lambda e, src=src, rows=rows, vb=vb: e.tensor_copy(out=vf[0:rows, vb, :], in_=src),
                            [("ps", b)], [("vf", vb)])
                        dst = dout["pv"][l] if tt == 3 else dout["sv"][l, sidx_s, 64:128, :]
                        P.add("sp", lambda e, dst=dst, rows=rows, vb=vb: e.dma_start(out=dst, in_=vf[0:rows, vb, :]),
                              R=[("vf", vb)], W=[("o_v", vb)], dma=("o_v", vb))
                    continue
                isk = (wt == 4)
                gtab = gk if isk else gq
                qb = (wt * 5 + tt) % 2
                act(lambda e, src=src, rows=rows: e.activation(out=qsq[0:rows, :], in_=src, func=AF.Square),
                    [("ps", b)], [("qsq", 0)])
                dve(lambda e, rows=rows, qb=qb: e.tensor_reduce(
                    out=qss[0:rows, qb, :], in_=qsq[0:rows, :].rearrange("p (h d) -> p h d", h=4), axis=AX.X, op=ALU.add),
                    [("qsq", 0)], [("qss", qb)])
                act(lambda e, rows=rows, qb=qb: e.activation(out=qss[0:rows, qb, :], in_=qss[0:rows, qb, :],
                                                            func=AF.Sqrt, bias=EPS, scale=1.0 / 64),
                    [("qss", qb)], [("qss", qb)])
                dve(lambda e, rows=rows, qb=qb: e.reciprocal(out=qss[0:rows, qb, :], in_=qss[0:rows, qb, :]),
                    [("qss", qb)], [("qss", qb)])
                dve(lambda e, src=src, rows=rows, qb=qb: e.tensor_tensor(
                    out=qn[0:rows, qb, :].rearrange("p (h d) -> p h d", h=4),
                    in0=src.rearrange("p (h d) -> p h d", h=4),
                    in1=qss[0:rows, qb, :].unsqueeze(2).broadcast_to([rows, 4, 64]), op=ALU.mult),
                    [("ps", b), ("qss", qb)], [("qn", qb)])
                dve(lambda e, rows=rows, qb=qb, gtab=gtab: e.tensor_tensor(
                    out=qn[0:rows, qb, :].rearrange("p (h d) -> p h d", h=4),
                    in0=qn[0:rows, qb, :].rearrange("p (h d) -> p h d", h=4),
                    in1=gtab[0:rows, l, :].unsqueeze(1).broadcast_to([rows, 4, 64]), op=ALU.mult),
                    [("qn", qb), ("small", 0)], [("qn", qb)])
                if _qs == "norm":
                    continue
                qv = qn[0:rows, qb, :].rearrange("p (h d) -> p h d", h=4)
                cosb = ropec[0:rows, tt, :].unsqueeze(1).broadcast_to([rows, 4, 8])
                sinb = ropes[0:rows, tt, :].unsqueeze(1).broadcast_to([rows, 4, 8])
                for i_, (xa, tb_) in enumerate([(qv[:, :, 0:8], cosb), (qv[:, :, 8:16], sinb),
                                                (qv[:, :, 8:16], cosb), (qv[:, :, 0:8], sinb)]):
                    dve(lambda e, xa=xa, tb_=tb_, i_=i_, rows=rows: e.tensor_tensor(
                        out=qrt[0:rows, i_], in0=xa, in1=tb_, op=ALU.mult),
                        [("qn", qb), ("rope", 0)], [("qrt", i_)])
                dve(lambda e, qv=qv, rows=rows: e.tensor_tensor(out=qv[:, :, 0:8], in0=qrt[0:rows, 0],
                                                               in1=qrt[0:rows, 1], op=ALU.subtract),
                    [("qrt", 0), ("qrt", 1)], [("qn", qb)])
                dve(lambda e, qv=qv, rows=rows: e.tensor_tensor(out=qv[:, :, 8:16], in0=qrt[0:rows, 2],
                                                               in1=qrt[0:rows, 3], op=ALU.add),
                    [("qrt", 2), ("qrt", 3)], [("qn", qb)])
                act(lambda e, rows=rows, qb=qb: e.copy(out=qtm[0:rows, qb, :], in_=qn[0:rows, qb, :]),
                    [("qn", qb)], [("qtm", qb)])
                if _qs == "rope":
                    continue
                if isk:
                    need_out = ((tt == 3 and last_g) or (tt == 4 and write_s)) and not _os.environ.get("KNOKDMA")
                    if need_out:
                        dst = dout["pk"][l] if tt == 3 else dout["sk"][l, sidx_s, 64:128, :]
                        P.add("sp", lambda e, dst=dst, rows=rows, qb=qb: e.dma_start(out=dst, in_=qn[0:rows, qb, :]),
                              R=[("qn", qb)], W=[("o_k", qb)], dma=("o_k", qb))
                if _qs == "kdma":
                    continue
                b2 = next_bank()
                psb2 = ps[:, b2, :].bitcast(BF16)

                def trq(e, psb2=psb2, rows=rows, qb=qb):
                    for h in range(4):
                        ins = e.transpose(out=psb2[0:64, h * 128:h * 128 + rows],
                                          in_=qtm[0:rows, qb, h * 64:(h + 1) * 64], identity=ident_b[0:rows, 0:rows])
                    return ins
                pe(trq, [("qtm", qb), ("ident_b", 0)], [("ps", b2)])
                pv_ = psb2[0:64, 0:512].rearrange("p (h t) -> p h t", h=4)[:, :, 0:rows]
                if isk:
                    kc0 = 128 + c0 if tt < 4 else 768
                    kkey = ("KT", tt + 1 if tt < 4 else 6)
                    act(lambda e, pv_=pv_, kc0=kc0, rows=rows: e.copy(out=KT[:, :, kc0:kc0 + rows], in_=pv_),
                        [("ps", b2)], [kkey])
                else:
                    act(lambda e, pv_=pv_, c0=c0, rows=rows, wt=wt: e.copy(
                        out=QT[:, 4 * wt:4 * wt + 4, c0:c0 + rows], in_=pv_),
                        [("ps", b2)], [("QT", wt, tt)])

        chk("qkv")
        if not last_g:
            dve(lambda e: e.tensor_copy(out=haloKT[:, l], in_=KT[:, :, 512:640]), [("KT", 4)], [("haloKT", l)])
            dve(lambda e: e.tensor_copy(out=haloV[:, l, :, 0:64], in_=Vaug[:, 4, :, 0:64]), [("Vaug", 4)], [("haloV", l)])

        chunks = [("p", lc) for lc in range(8)] + [("s", 0)]
        it = 0
        for (kind, lc) in chunks:
            if kind == "p":
                qc0 = 64 * lc
                tA, tB = lc // 2, lc // 2 + 1
                kcA, kcB = 128 * tA, 128 * tB
                odd = lc % 2
                skipA = (g == 0 and lc < 2)
                tt_q = lc // 2
            else:
                qc0 = 512
                tA, tB = 5, 6
                kcA, kcB = 640, 768
                odd = 0
                skipA = False
                tt_q = 4
            rA = (64, 128) if odd else (0, 128)
            rB = (0, 128) if odd else (0, 64)
            for h in range(4):
                pb = it % 2
                it += 1
                bS = next_bank()
                qkeys = [("QT", h, tt_q)]
                parts = []
                if not skipA:
                    parts.append((0, tA, kcA, rA, 128))
                parts.append((1, tB, kcB, rB, rB[1]))
                for (slot, tX, kc, rr_, mrows) in parts:
                    pe(lambda e, slot=slot, kc=kc, mrows=mrows, bS=bS, h=h, qc0=qc0: e.matmul(
                        ps[0:mrows, bS, slot * 256:(slot + 1) * 256], lhsT=KT[:, h, kc:kc + mrows],
                        rhs=QT[:, 4 * h:4 * h + 4, qc0:qc0 + 64], start=True, stop=True),
                        [("KT", tX)] + qkeys, [("ps", bS)])
                    act(lambda e, slot=slot, mrows=mrows, bS=bS, pb=pb: e.activation(
                        out=pT[0:mrows, pb, slot, :], in_=ps[0:mrows, bS, slot * 256:(slot + 1) * 256],
                        func=AF.Exp, scale=0.125),
                        [("ps", bS)], [("pT", pb, slot)])
                bO = next_bank()

                def pvmm(e, parts=parts, bO=bO, pb=pb, h=h):
                    n = len(parts)
                    for i_, (slot, tX, kc, rr_, mrows) in enumerate(parts):
                        e.matmul(ps[0:64, bO, 0:256], lhsT=Vaug[rr_[0]:rr_[1], tX, h, 0:64],
                                 rhs=pT[rr_[0]:rr_[1], pb, slot, :], start=(i_ == 0), stop=(i_ == n - 1))
                    for i_, (slot, tX, kc, rr_, mrows) in enumerate(parts):
                        ins = e.matmul(ps[0:64, bO, 256:512], lhsT=ones_b[rr_[0]:rr_[1], 0:64],
                                       rhs=pT[rr_[0]:rr_[1], pb, slot, :], start=(i_ == 0), stop=(i_ == n - 1))
                    return ins
                pe(pvmm, [("pT", pb, s_[0]) for s_ in parts] + [("Vaug", s_[1]) for s_ in parts] + [("ones_b", 0)],
                   [("ps", bO)])
                dve(lambda e, bO=bO, pb=pb, h=h: e.tensor_tensor(
                    out=adn[:, pb, :].rearrange("p (r q) -> p r q", r=4),
                    in0=ps[0:64, bO, 256:512].rearrange("p (r q) -> p r q", r=4),
                    in1=esink[0:64, l, 4 * h:4 * h + 4].unsqueeze(2).broadcast_to([64, 4, 64]), op=ALU.add),
                    [("ps", bO), ("esink", 0)], [("adn", pb)])
                dve(lambda e, pb=pb: e.reciprocal(out=adn[:, pb, :], in_=adn[:, pb, :]), [("adn", pb)], [("adn", pb)])
                dve(lambda e, bO=bO, pb=pb, h=h, qc0=qc0: e.tensor_tensor(
                    out=attnT[:, 4 * h:4 * h + 4, qc0:qc0 + 64],
                    in0=ps[0:64, bO, 0:256].rearrange("p (r q) -> p r q", r=4),
                    in1=adn[:, pb, :].rearrange("p (r q) -> p r q", r=4), op=ALU.mult),
                    [("ps", bO), ("adn", pb)], [("attnT", h, qc0)])

        chk("attn")
        ssm_phase(g, l, sidx_s, write_s, last_g)
        chk("ssm")

        def ev_glu(mi, hi, b):
            a0, a1 = HALVES[hi]
            act(lambda e: e.activation(out=ntmp[:, hi, 0:a1 - a0], in_=ps[:, b, 0:a1 - a0], func=AF.Sigmoid,
                                       bias=b_gluT[:, l, mi:mi + 1], scale=1.0),
                [("ps", b), ("small", 0)], [("ntmp", hi, 0), ("ntmp", hi, 1)])
            dve(lambda e: e.tensor_tensor(out=ssm_outT[:, mi, a0:a1], in0=ntmp[:, hi, 0:a1 - a0],
                                          in1=zT[:, mi, a0:a1], op=ALU.mult),
                [("ntmp", hi, 0), ("ntmp", hi, 1), ("zT", mi)], [("ssm_outT", mi, hi)])
        proj_fm(din["w_glu"], l, 8, 0, 1024, lambda k: zT[:, k, :], [("zT", k) for k in range(8)], ev_glu)

        chk("glu")
        def ev_gate(boff):
            def ev(mi, hi, b):
                a0, a1 = HALVES[hi]
                act(lambda e: e.activation(out=gaT[:, mi, a0:a1], in_=ps[:, b, 0:a1 - a0], func=AF.Sigmoid,
                                           bias=b_gateT[:, l, boff + mi:boff + mi + 1], scale=1.0),
                    [("ps", b), ("small", 0)], [("gaT", mi, hi)])
            return ev
        proj_fm(din["w_gate"], l, 16, 0, D, lambda k: hT[:, k, :], hkeys(), ev_gate(0))

        def ev_ps(mi, hi, b):
            a0, a1 = HALVES[hi]
            dve(lambda e: e.tensor_tensor(out=mixedT[:, mi, a0:a1], in0=ps[:, b, 0:a1 - a0], in1=gaT[:, mi, a0:a1],
                                          op=ALU.mult), [("ps", b), ("gaT", mi, hi)], [("mixedT", mi, hi)])
        proj_fm(din["w_proj_ssm"], l, 8, 0, D, lambda k: ssm_outT[:, k, :],
                [("ssm_outT", k, hi) for k in range(8) for hi in range(2)], ev_ps)
        proj_fm(din["w_gate"], l, 16, D, D, lambda k: hT[:, k, :], hkeys(), ev_gate(16))

        def ev_pa(mi, hi, b):
            a0, a1 = HALVES[hi]
            dve(lambda e: e.tensor_tensor(out=ntmp[:, hi, 0:a1 - a0], in0=ps[:, b, 0:a1 - a0], in1=gaT[:, mi, a0:a1],
                                          op=ALU.mult), [("ps", b), ("gaT", mi, hi)], [("ntmp", hi, 0), ("ntmp", hi, 1)])
            dve(lambda e: e.tensor_tensor(out=mixedT[:, mi, a0:a1], in0=ntmp[:, hi, 0:a1 - a0],
                                          in1=mixedT[:, mi, a0:a1], op=ALU.add),
                [("ntmp", hi, 0), ("ntmp", hi, 1), ("mixedT", mi, hi)], [("mixedT", mi, hi)])
        akeys = [("attnT", h, qc) for h in range(4) for qc in list(range(0, 512, 64)) + [512]]
        proj_fm(din["w_proj_attn"], l, 16, 0, D, lambda k: attnT[:, k, :], akeys, ev_pa, kp=64)

        chk("merge")
        def ev_res(goff):
            def ev(mi, hi, b):
                for (a0, a1, s) in SEGS[hi]:
                    sidx = seqs[s]
                    h0_ = HALVES[hi][0]
                    dve(lambda e, a0=a0, a1=a1, sidx=sidx, h0_=h0_: e.scalar_tensor_tensor(
                        out=xT[:, mi, a0:a1], in0=ps[:, b, a0 - h0_:a1 - h0_],
                        scalar=modT[:, l, goff + mi, sidx:sidx + 1], in1=xT[:, mi, a0:a1],
                        op0=ALU.mult, op1=ALU.add),
                        [("ps", b), ("modT", l), ("xT", mi, hi)], [("xT", mi, hi)])
            return ev
        proj_fm(din["w_out"], l, 16, 0, D, lambda k: mixedT[:, k, :],
                [("mixedT", k, hi) for k in range(16) for hi in range(2)], ev_res(32))

        chk("outproj")
        norm_phase(l, 1, seqs)
        for sl in range(2):
            for jb in range(11):
                cb = sl * 2816 + jb * 256
                def ev_g(mi, hi, b):
                    a0, a1 = HALVES[hi]
                    act(lambda e: e.activation(out=rstd_g[hi][mi % 2][:, 0:a1 - a0], in_=ps[:, b, 0:a1 - a0],
                                               func=AF.Silu),
                        [("ps", b)], [("gs", mi % 2, hi)])
                proj_fm(din["w_ffn_gate"], l, 16, cb, 256, lambda k: hT[:, k, :], hkeys(), ev_g)

                def ev_up(mi, hi, b, jb=jb):
                    a0, a1 = HALVES[hi]
                    dve(lambda e: e.tensor_tensor(out=actT[:, 2 * jb + mi, a0:a1], in0=ps[:, b, 0:a1 - a0],
                                                  in1=rstd_g[hi][mi % 2][:, 0:a1 - a0], op=ALU.mult),
                        [("ps", b), ("gs", mi % 2, hi)], [("actT", 2 * jb + mi, hi)])
                proj_fm(din["w_ffn_up"], l, 16, cb, 256, lambda k: hT[:, k, :], hkeys(), ev_up)
            wdn = din["w_ffn_down"][:, sl * 2816:(sl + 1) * 2816, :]
            proj_fm(wdn, l, 22, 0, D, lambda k: actT[:, k, :],
                    [("actT", k, hi) for k in range(22) for hi in range(2)], ev_res(80))

    gsb = av("gs", X0 + 27648, [128, 4, 288], BF16)
    rstd_g = [[gsb[:, hi * 2 + m2, :] for m2 in range(2)] for hi in range(2)]

    def rr(out_ap, in_ap, shift, wtmp, ktmp, Rk, Wk, tmpk):
        dve(lambda e: e.tensor_scalar(out=wtmp, in0=in_ap, scalar1=float(shift), scalar2=None, op0=ALU.add),
            Rk, [tmpk[0]])
        dve(lambda e: e.tensor_scalar(out=ktmp, in0=wtmp, scalar1=1.0 / TWO_PI, scalar2=MAGIC, op0=ALU.mult, op1=ALU.add),
            [tmpk[0]], [tmpk[1]])
        dve(lambda e: e.tensor_scalar(out=ktmp, in0=ktmp, scalar1=-MAGIC, scalar2=None, op0=ALU.add),
            [tmpk[1]], [tmpk[1]])
        dve(lambda e: e.scalar_tensor_tensor(out=wtmp, in0=ktmp, scalar=-TWO_PI, in1=wtmp, op0=ALU.mult, op1=ALU.add),
            [tmpk[0], tmpk[1]], [tmpk[0]])
        dve(lambda e: e.tensor_scalar(out=out_ap, in0=wtmp, scalar1=3.1415925, scalar2=-3.1415925, op0=ALU.min, op1=ALU.max),
            [tmpk[0]], Wk)

    dt8 = sb("dt8", [128, 8])
    zsm = sb("zsm", [128, 3, 4])
    trst = ckst[0:32, :].rearrange("p (a b) -> p a b", a=2)

    def ssm_phase(g, l, sidx_s, write_s, last_g):
        P.add("sp", lambda e: e.dma_start(out=h0s[:, 0, :], in_=din["st_re"][l, sidx_s]), W=[("h0s", 0)], dma="h0s0")
        P.add("sp", lambda e: e.dma_start(out=h0s[:, 1, :], in_=din["st_im"][l, sidx_s]), W=[("h0s", 1)], dma="h0s1")
        T0, T1, T2, T3 = (tmpA[:, i_, :] for i_ in range(4))
        for ct in range(8):
            cs = slice(ct * 512, (ct + 1) * 512)
            SpF = SpT[:].rearrange("p a b c -> p a (b c)")
            P.add("pool", lambda e, ct=ct: e.dma_start(out=BBs[:, 0, :], in_=din["BBre"][l, :, ct, :]), W=[("BBs", 0)], dma="BBs0")
            P.add("pool", lambda e, ct=ct: e.dma_start(out=BBs[:, 1, :], in_=din["BBim"][l, :, ct, :]), W=[("BBs", 1)], dma="BBs1")
            P.add("pool", lambda e, ct=ct: e.dma_start(out=CCs[:, 0], in_=din["CCre"][l, :, 4 * ct:4 * ct + 4, :]), W=[("CCs", 0)], dma="CCs0")
            P.add("pool", lambda e, ct=ct: e.dma_start(out=CCs[:, 1], in_=din["CCim"][l, :, 4 * ct:4 * ct + 4, :]), W=[("CCs", 1)], dma="CCs1")
            if g > 0:
                P.add("sp", lambda e, ct=ct: e.dma_start(out=SmF[:], in_=scrS[l, ct]), R=[("scrS", l, ct)], W=[("SmF", 0), ("SmF", 1)], dma="scrS_ld")
                P.add("sp", lambda e, ct=ct: e.dma_start(out=SpF, in_=scrP[l, ct]), R=[("scrP", l, ct)], W=[("SpT", 0), ("SpT", 1)], dma="scrP_ld")
            else:
                P.add("sp", lambda e, cs=cs: e.dma_start(out=tabA[:, 0, :], in_=din["are_b"][:, l, cs]), W=[("tabA", 0)], dma="tabA0")
                P.add("sp", lambda e, cs=cs: e.dma_start(out=tabA[:, 1, :], in_=din["aim_b"][:, l, cs]), W=[("tabA", 1)], dma="tabA1")
                are, aim = tabA[:, 0, :], tabA[:, 1, :]
                act(lambda e, ct=ct: e.activation(out=dt8[:], in_=ldt_b[:, l, 8 * ct:8 * ct + 8], func=AF.Exp),
                    [("small", 0)], [("dt8", 0)])
                dtb = dt8[:].unsqueeze(2).broadcast_to([128, 8, 64])
                v3 = lambda a: a.rearrange("p (g q) -> p g q", g=8)
                zr, zi = SmF[:, 0, :], SmF[:, 1, :]
                dve(lambda e: e.tensor_tensor(out=v3(zr), in0=v3(are), in1=dtb, op=ALU.mult), [("tabA", 0), ("dt8", 0)], [("SmF", 0)])
                dve(lambda e: e.tensor_tensor(out=v3(zi), in0=v3(aim), in1=dtb, op=ALU.mult), [("tabA", 1), ("dt8", 0)], [("SmF", 1)])
                act(lambda e: e.activation(out=T0, in_=zr, func=AF.Exp), [("SmF", 0)], [("T", 0)])
                rr(T1, zi, 0.0, T1, T3, [("SmF", 1)], [("T", 1)], [("T", 1), ("T", 3)])
                act(lambda e: e.activation(out=T1, in_=T1, func=AF.Sin), [("T", 1)], [("T", 1)])
                rr(T2, zi, math.pi / 2, T2, T3, [("SmF", 1)], [("T", 2)], [("T", 2), ("T", 3)])
                act(lambda e: e.activation(out=T2, in_=T2, func=AF.Sin), [("T", 2)], [("T", 2)])
                dve(lambda e: e.tensor_tensor(out=T2, in0=T2, in1=T0, op=ALU.mult), [("T", 2), ("T", 0)], [("T", 2)])
                dve(lambda e: e.tensor_scalar(out=T2, in0=T2, scalar1=-1.0, scalar2=None, op0=ALU.add), [("T", 2)], [("T", 2)])
                dve(lambda e: e.tensor_tensor(out=T1, in0=T1, in1=T0, op=ALU.mult), [("T", 1), ("T", 0)], [("T", 1)])
                dve(lambda e: e.tensor_tensor(out=T0, in0=are, in1=are, op=ALU.mult), [("tabA", 0), ("T", 0)], [("T", 0)])
                dve(lambda e: e.tensor_tensor(out=T3, in0=aim, in1=aim, op=ALU.mult), [("tabA", 1)], [("T", 3)])
                dve(lambda e: e.tensor_tensor(out=T0, in0=T0, in1=T3, op=ALU.add), [("T", 0), ("T", 3)], [("T", 0)])
                dve(lambda e: e.reciprocal(out=T0, in_=T0), [("T", 0)], [("T", 0)])
                Gf = Gs[:].rearrange("p a b c -> p (a b c)")
                F0, F1 = Gf[:, 0:512], Gf[:, 512:1024]
                dve(lambda e: e.tensor_tensor(out=T3, in0=T2, in1=are, op=ALU.mult), [("T", 2), ("tabA", 0)], [("T", 3)])
                dve(lambda e: e.tensor_tensor(out=F0, in0=T1, in1=aim, op=ALU.mult), [("T", 1), ("tabA", 1)], [("Gs", 0)])
                dve(lambda e: e.tensor_tensor(out=F0, in0=F0, in1=T3, op=ALU.add), [("Gs", 0), ("T", 3)], [("Gs", 0)])
                dve(lambda e: e.tensor_tensor(out=F0, in0=F0, in1=T0, op=ALU.mult), [("Gs", 0), ("T", 0)], [("Gs", 0)])
                dve(lambda e: e.tensor_tensor(out=T3, in0=T1, in1=are, op=ALU.mult), [("T", 1), ("tabA", 0)], [("T", 3)])
                dve(lambda e: e.tensor_tensor(out=F1, in0=T2, in1=aim, op=ALU.mult), [("T", 2), ("tabA", 1)], [("Gs", 1)])
                dve(lambda e: e.tensor_tensor(out=F1, in0=T3, in1=F1, op=ALU.subtract), [("Gs", 1), ("T", 3)], [("Gs", 1)])
                dve(lambda e: e.tensor_tensor(out=F1, in0=F1, in1=T0, op=ALU.mult), [("Gs", 1), ("T", 0)], [("Gs", 1)])
                fre, fim = tabA[:, 0, :], tabA[:, 1, :]
                dve(lambda e: e.tensor_copy(out=fre, in_=F0), [("Gs", 0)], [("tabA", 0)])
                dve(lambda e: e.tensor_copy(out=fim, in_=F1), [("Gs", 1)], [("tabA", 1)])
                act(lambda e: e.activation(out=T0, in_=zr, func=AF.Exp, scale=negsp1[:, 0:1]), [("SmF", 0), ("small", 0), ("T", 0)], [("T", 0)])
                dve(lambda e: e.tensor_scalar(out=F0, in0=zi, scalar1=negsp1[:, 0:1], scalar2=None, op0=ALU.mult),
                    [("SmF", 1), ("small", 0)], [("Gs", 0)])
                rr(T1, F0, 0.0, T1, T3, [("Gs", 0)], [("T", 1)], [("T", 1), ("T", 3)])
                act(lambda e: e.activation(out=T1, in_=T1, func=AF.Sin), [("T", 1)], [("T", 1)])
                rr(T2, F0, math.pi / 2, T2, T3, [("Gs", 0)], [("T", 2)], [("T", 2), ("T", 3)])
                act(lambda e: e.activation(out=T2, in_=T2, func=AF.Sin), [("T", 2)], [("T", 2)])
                dve(lambda e: e.tensor_tensor(out=T1, in0=T1, in1=T0, op=ALU.mult), [("T", 1), ("T", 0)], [("T", 1)])
                dve(lambda e: e.tensor_tensor(out=T2, in0=T2, in1=T0, op=ALU.mult), [("T", 2), ("T", 0)], [("T", 2)])
                dve(lambda e: e.tensor_tensor(out=T0, in0=fre, in1=T2, op=ALU.mult), [("tabA", 0), ("T", 2)], [("T", 0)])
                dve(lambda e: e.tensor_tensor(out=T3, in0=fim, in1=T1, op=ALU.mult), [("tabA", 1), ("T", 1)], [("T", 3)])
                dve(lambda e: e.tensor_tensor(out=SmF[:, 0, :], in0=T0, in1=T3, op=ALU.subtract), [("T", 0), ("T", 3)], [("SmF", 0)])
                dve(lambda e: e.tensor_tensor(out=T0, in0=fre, in1=T1, op=ALU.mult), [("tabA", 0), ("T", 1)], [("T", 0)])
                dve(lambda e: e.tensor_tensor(out=T3, in0=fim, in1=T2, op=ALU.mult), [("tabA", 1), ("T", 2)], [("T", 3)])
                dve(lambda e: e.tensor_tensor(out=SmF[:, 1, :], in0=T0, in1=T3, op=ALU.add), [("T", 0), ("T", 3)], [("SmF", 1)])
                js = slice(4 * ct, 4 * ct + 4)
                act(lambda e, js=js: e.activation(out=zsm[:, 2, :], in_=ldt_sm[:, l, js], func=AF.Exp), [("small", 0)], [("zsm", 2)])
                dve(lambda e, js=js: e.tensor_tensor(out=zsm[:, 0, :], in0=are_sm[:, l, js], in1=zsm[:, 2, :], op=ALU.mult),
                    [("small", 0), ("zsm", 2)], [("zsm", 0)])
                dve(lambda e, js=js: e.tensor_tensor(out=zsm[:, 1, :], in0=aim_sm[:, l, js], in1=zsm[:, 2, :], op=ALU.mult),
                    [("small", 0), ("zsm", 2)], [("zsm", 1)])
                t3 = lambda a: a.rearrange("p (j t) -> p j t", j=4)
                tpb = tp1[:].unsqueeze(1).broadcast_to([128, 4, 128])
                dve(lambda e: e.tensor_tensor(out=t3(T0), in0=zsm[:, 0, :].unsqueeze(2).broadcast_to([128, 4, 128]), in1=tpb, op=ALU.mult),
                    [("zsm", 0), ("small", 0), ("T", 0)], [("T", 0)])
                act(lambda e: e.activation(out=T0, in_=T0, func=AF.Exp), [("T", 0)], [("T", 0)])
                dve(lambda e: e.tensor_tensor(out=t3(F0), in0=zsm[:, 1, :].unsqueeze(2).broadcast_to([128, 4, 128]), in1=tpb, op=ALU.mult),
                    [("zsm", 1), ("small", 0)], [("Gs", 0)])
                rr(T1, F0, 0.0, T1, T3, [("Gs", 0)], [("T", 1)], [("T", 1), ("T", 3)])
                act(lambda e: e.activation(out=T1, in_=T1, func=AF.Sin), [("T", 1)], [("T", 1)])
                rr(T2, F0, math.pi / 2, T2, T3, [("Gs", 0)], [("T", 2)], [("T", 2), ("T", 3)])
                act(lambda e: e.activation(out=T2, in_=T2, func=AF.Sin), [("T", 2)], [("T", 2)])
                dve(lambda e: e.tensor_tensor(out=SpF[:, 0, :], in0=T2, in1=T0, op=ALU.mult), [("T", 2), ("T", 0)], [("SpT", 0)])
                dve(lambda e: e.tensor_tensor(out=SpF[:, 1, :], in0=T1, in1=T0, op=ALU.mult), [("T", 1), ("T", 0)], [("SpT", 1)])


                P.add("sp", lambda e, ct=ct: e.dma_start(out=scrS[l, ct], in_=SmF[:]), R=[("SmF", 0), ("SmF", 1)], W=[("scrS", l, ct)], dma="scrS_st")
                P.add("sp", lambda e, ct=ct: e.dma_start(out=scrP[l, ct], in_=SpF), R=[("SpT", 0), ("SpT", 1)], W=[("scrP", l, ct)], dma="scrP_st")

            chunk_list = [("p", c_) for c_ in range(4)] + [("s", 0)]
            for (kind, c_) in chunk_list:
                rows = 128 if kind == "p" else 64
                c0 = 128 * c_ if kind == "p" else 512
                zero_h0 = (kind == "p" and g == 0 and c_ == 0)
                bR, bI = next_bank(), next_bank()
                pe(lambda e, rows=rows, c0=c0, bR=bR, ct=ct: e.matmul(ps[0:rows, bR, :], lhsT=uT[:, ct, c0:c0 + rows],
                                                                     rhs=BBs[:, 0, :], start=True, stop=True),
                   [("uT", ct, 0), ("uT", ct, 1), ("BBs", 0)], [("ps", bR)])
                pe(lambda e, rows=rows, c0=c0, bI=bI, ct=ct: e.matmul(ps[0:rows, bI, :], lhsT=uT[:, ct, c0:c0 + rows],
                                                                     rhs=BBs[:, 1, :], start=True, stop=True),
                   [("uT", ct, 0), ("uT", ct, 1), ("BBs", 1)], [("ps", bI)])
                R_ = slice(0, rows)
                dve(lambda e, R_=R_, bR=bR: e.tensor_tensor(out=T0[R_], in0=ps[R_, bR, :], in1=SmF[R_, 0, :], op=ALU.mult),
                    [("ps", bR), ("SmF", 0), ("T", 0)], [("T", 0)])
                dve(lambda e, R_=R_, bI=bI: e.tensor_tensor(out=T1[R_], in0=ps[R_, bI, :], in1=SmF[R_, 1, :], op=ALU.mult),
                    [("ps", bI), ("SmF", 1), ("T", 1)], [("T", 1)])
                dve(lambda e, R_=R_: e.tensor_tensor(out=Wt[R_, 0, :], in0=T0[R_], in1=T1[R_], op=ALU.subtract),
                    [("T", 0), ("T", 1)], [("Wt", 0)])
                dve(lambda e, R_=R_, bI=bI: e.tensor_tensor(out=T2[R_], in0=ps[R_, bI, :], in1=SmF[R_, 0, :], op=ALU.mult),
                    [("ps", bI), ("SmF", 0), ("T", 2)], [("T", 2)])
                dve(lambda e, R_=R_, bR=bR: e.tensor_tensor(out=T3[R_], in0=ps[R_, bR, :], in1=SmF[R_, 1, :], op=ALU.mult),
                    [("ps", bR), ("SmF", 1), ("T", 3)], [("T", 3)])
                dve(lambda e, R_=R_: e.tensor_tensor(out=Wt[R_, 1, :], in0=T2[R_], in1=T3[R_], op=ALU.add),
                    [("T", 2), ("T", 3)], [("Wt", 1)])
                bG = [next_bank(), next_bank()]
                for ri in range(2):
                    def gmm(e, ri=ri, rows=rows, bgi=bG[ri]):
                        for jj in range(4):
                            ins = e.matmul(ps[:, bgi, jj * 128:jj * 128 + rows], lhsT=Wt[0:rows, ri, jj * 128:(jj + 1) * 128],
                                           rhs=tri_b[0:rows, 0:rows], start=True, stop=True)
                        return ins
                    pe(gmm, [("Wt", ri), ("tri_b", 0)], [("ps", bG[ri])])
                for ri in range(2):
                    if zero_h0:
                        act(lambda e, ri=ri, rows=rows, bgi=bG[ri]: e.copy(
                            out=Gs[:, ri, :, 0:rows], in_=ps[:, bgi, :].rearrange("p (j t) -> p j t", j=4)[:, :, 0:rows]),
                            [("ps", bG[ri])], [("Gs", ri)])
                    else:
                        for jj in range(4):
                            if kind == "p":
                                hsrc = h0p[:, l, ri, 4 * ct + jj:4 * ct + jj + 1]
                                hk = ("h0p", l)
                            else:
                                hsrc = h0s[:, ri, 4 * ct + jj:4 * ct + jj + 1]
                                hk = ("h0s", ri)
                            act(lambda e, ri=ri, jj=jj, rows=rows, bgi=bG[ri], hsrc=hsrc: e.activation(
                                out=Gs[:, ri, jj, 0:rows], in_=ps[:, bgi, jj * 128:jj * 128 + rows], func=AF.Identity,
                                bias=hsrc, scale=1.0),
                                [("ps", bG[ri]), hk], [("Gs", ri)])
                tv = lambda a, rows=rows: a.rearrange("p (j t) -> p j t", j=4)[:, :, 0:rows]
                Gr, Gi = Gs[:, 0, :, 0:rows], Gs[:, 1, :, 0:rows]
                Sr, Si = SpT[:, 0, :, 0:rows], SpT[:, 1, :, 0:rows]
                dve(lambda e, tv=tv, Gr=Gr, Sr=Sr: e.tensor_tensor(out=tv(T0), in0=Sr, in1=Gr, op=ALU.mult), [("SpT", 0), ("Gs", 0), ("T", 0)], [("T", 0)])
                dve(lambda e, tv=tv, Gi=Gi, Si=Si: e.tensor_tensor(out=tv(T1), in0=Si, in1=Gi, op=ALU.mult), [("SpT", 1), ("Gs", 1), ("T", 1)], [("T", 1)])
                dve(lambda e, tv=tv, rows=rows: e.tensor_tensor(out=hS[:, 0, :, 0:rows], in0=tv(T0), in1=tv(T1), op=ALU.subtract),
                    [("T", 0), ("T", 1)], [("hS", 0)])
                dve(lambda e, tv=tv, Gi=Gi, Sr=Sr: e.tensor_tensor(out=tv(T2), in0=Sr, in1=Gi, op=ALU.mult), [("SpT", 0), ("Gs", 1), ("T", 2)], [("T", 2)])
                dve(lambda e, tv=tv, Gr=Gr, Si=Si: e.tensor_tensor(out=tv(T3), in0=Si, in1=Gr, op=ALU.mult), [("SpT", 1), ("Gs", 0), ("T", 3)], [("T", 3)])
                dve(lambda e, tv=tv, rows=rows: e.scalar_tensor_tensor(out=hS[:, 1, :, 0:rows], in0=tv(T2), scalar=-1.0, in1=tv(T3),
                                                                     op0=ALU.mult, op1=ALU.subtract),
                    [("T", 2), ("T", 3)], [("hS", 1)])
                lastc = lambda a, rows=rows: a.rearrange("p (j t) -> p j t", j=4)[:, :, rows - 1]
                if kind == "p":
                    dre, dim_, dk = h0p[:, l, 0, 4 * ct:4 * ct + 4], h0p[:, l, 1, 4 * ct:4 * ct + 4], [("h0p", l)]
                else:
                    dre, dim_, dk = hends[:, 0, 4 * ct:4 * ct + 4], hends[:, 1, 4 * ct:4 * ct + 4], [("hends", 0)]
                dve(lambda e, lastc=lastc, dre=dre: e.tensor_tensor(out=dre, in0=lastc(T0), in1=lastc(T1), op=ALU.subtract),
                    [("T", 0), ("T", 1)] + dk, dk)
                dve(lambda e, lastc=lastc, dim_=dim_: e.tensor_tensor(out=dim_, in0=lastc(T2), in1=lastc(T3), op=ALU.add),
                    [("T", 2), ("T", 3)] + dk, dk)
                bY = next_bank()

                def ymm(e, rows=rows, bY=bY):
                    for jj in range(4):
                        e.matmul(ps[:, bY, 0:rows], lhsT=CCs[:, 0, jj, :], rhs=hS[:, 0, jj, 0:rows], start=(jj == 0), stop=False)
                    for jj in range(4):
                        ins = e.matmul(ps[:, bY, 0:rows], lhsT=CCs[:, 1, jj, :], rhs=hS[:, 1, jj, 0:rows], start=False, stop=(jj == 3))
                    return ins
                pe(ymm, [("CCs", 0), ("CCs", 1), ("hS", 0), ("hS", 1)], [("ps", bY)])
                dve(lambda e, rows=rows, bY=bY, c0=c0, ct=ct: e.scalar_tensor_tensor(
                    out=ypre[:, c0:c0 + rows], in0=uT[:, ct, c0:c0 + rows], scalar=dTs[:, l, ct:ct + 1],
                    in1=ps[:, bY, 0:rows], op0=ALU.mult, op1=ALU.add),
                    [("ps", bY), ("uT", ct, 0), ("uT", ct, 1), ("small", 0)], [("ypre", c0)])
            yk = [("ypre", c) for c in (0, 128, 256, 384, 512)]
            Tf = tmpA[:].rearrange("p a b -> p (a b)")
            Y0, Y1 = Tf[:, 0:T], Tf[:, 1024:1024 + T]
            ka, kb = [("T", 0), ("T", 1)], [("T", 2), ("T", 3)]
            act(lambda e: e.activation(out=Y0, in_=ypre[:], func=AF.Square), yk + ka, ka)
            dve(lambda e: e.tensor_scalar(out=Y0, in0=Y0, scalar1=0.044715, scalar2=1.0, op0=ALU.mult, op1=ALU.add), ka, ka)
            dve(lambda e: e.tensor_tensor(out=Y0, in0=Y0, in1=ypre[:], op=ALU.mult), ka + yk, ka)
            act(lambda e: e.activation(out=Y1, in_=Y0, func=AF.Sigmoid, scale=1.5957691216057308), ka + kb, kb)
            dve(lambda e, ct=ct: e.tensor_tensor(out=zT[:, ct, :], in0=Y1, in1=ypre[:], op=ALU.mult), kb + yk, [("zT", ct)])
            if g == 0 and l + 1 < n_layers:
                mod_part(l + 1, ct * 1536, 1536)
        if g == 0 and l + 1 < n_layers:
            mod_derive(l + 1)
        def state_out(src_re, src_im, skeys, dre, dim_, tag):
            b = next_bank()

            def trs(e, b=b):
                e.transpose(out=ps[0:32, b, 0:128], in_=src_re, identity=ident_f[:])
                return e.transpose(out=ps[0:32, b, 128:256], in_=src_im, identity=ident_f[:])
            pe(trs, skeys + [("ident_f", 0)], [("ps", b)])
            act(lambda e, b=b: e.copy(out=trst[:].rearrange("p a b -> p (a b)"), in_=ps[0:32, b, 0:256]), [("ps", b)], [("ckst", 0)])
            P.add("sp", lambda e: e.dma_start(out=dre, in_=trst[:, 0, :]), R=[("ckst", 0)], W=[("o_st", tag, 0)], dma=("o_st", 0))
            P.add("sp", lambda e: e.dma_start(out=dim_, in_=trst[:, 1, :]), R=[("ckst", 0)], W=[("o_st", tag, 1)], dma=("o_st", 1))
        if write_s:
            state_out(hends[:, 0, :], hends[:, 1, :], [("hends", 0)], dout["sre"][l, sidx_s], dout["sim"][l, sidx_s], "s")
        if last_g:
            state_out(h0p[:, l, 0, :], h0p[:, l, 1, :], [("h0p", l)], dout["pre"][l], dout["pim"][l], "p")

    def _groups():
        for g in range(n_groups):
            P.add("sp", lambda e, g=g: e.dma_start(out=ropec[:], in_=din["ropec"][g]), W=[("rope", 0)], dma="ropec")
            P.add("sp", lambda e, g=g: e.dma_start(out=ropes[:], in_=din["ropes"][g]), W=[("rope", 0)], dma="ropes")
            sidx_s = g % 2
            for tt in range(5):
                rows = 128 if tt < 4 else 64
                c0 = tt * 128
                src = din["xp"][g * TP + c0: g * TP + c0 + 128, :] if tt < 4 else din["xs"][sidx_s]
                xb = tt % 2
                P.add("sp", lambda e, src=src, rows=rows, xb=xb: e.dma_start(out=xstage[0:rows, xb, :], in_=src),
                      W=[("xstage", xb)], dma=("xin", xb))
                for kq in range(4):
                    b = next_bank()

                    def trx(e, b=b, rows=rows, xb=xb, kq=kq):
                        for k4 in range(4):
                            k = kq * 4 + k4
                            ins = e.transpose(out=ps[:, b, k4 * 128:k4 * 128 + rows], in_=xstage[0:rows, xb, k * 128:(k + 1) * 128],
                                              identity=ident_f[0:rows, 0:rows])
                        return ins
                    pe(trx, [("xstage", xb), ("ident_f", 0)], [("ps", b)])
                    hi_keys = [("xT", kq * 4 + k4, hi) for k4 in range(4) for hi in range(2)]
                    act(lambda e, b=b, rows=rows, kq=kq, c0=c0: e.copy(
                        out=xT[:, kq * 4:kq * 4 + 4, c0:c0 + rows],
                        in_=ps[:, b, :].rearrange("p (k t) -> p k t", k=4)[:, :, 0:rows]),
                        [("ps", b)], hi_keys)
            for l in range(n_layers):
                layer(g, l)
            for tt in range(5):
                rows = 128 if tt < 4 else 64
                c0 = tt * 128
                if tt == 4 and g >= 2:
                    continue
                xb = tt % 2
                for kq in range(4):
                    b = next_bank()

                    def tro(e, b=b, rows=rows, kq=kq, c0=c0):
                        for k4 in range(4):
                            k = kq * 4 + k4
                            ins = e.transpose(out=ps[0:rows, b, k4 * 128:(k4 + 1) * 128], in_=xT[:, k, c0:c0 + rows],
                                              identity=ident_f[:])
                        return ins
                    pe(tro, [("xT", kq * 4 + k4, hi) for k4 in range(4) for hi in range(2)] + [("ident_f", 0)], [("ps", b)])
                    act(lambda e, b=b, rows=rows, kq=kq, xb=xb: e.copy(out=xstage[0:rows, xb, kq * 512:(kq + 1) * 512],
                                                                      in_=ps[0:rows, b, :]),
                        [("ps", b)], [("xstage", xb)])
                dst = dout["yp"][g * TP + c0: g * TP + c0 + 128, :] if tt < 4 else dout["ys"][sidx_s]
                P.add("sp", lambda e, dst=dst, rows=rows, xb=xb: e.dma_start(out=dst, in_=xstage[0:rows, xb, :]),
                      R=[("xstage", xb)], W=[("xstage", xb), ("o_y", xb)], dma=("o_y", xb))

    try:
        chk("setup")
        _groups()
    except _Stop:
        pass

    okeys = [k for k in P.lastw if isinstance(k[0], str) and k[0].startswith("o_")]
    P.add("sp", lambda e: e.nop(), R=okeys)
    P.emit(nc)
    return nc


def _rep(a, n=128):
    return np.ascontiguousarray(np.broadcast_to(a[None], (n,) + a.shape))


def prep_shared(inp):
    f = np.float32
    sh = {}
    for k in ["w_mod", "w_in", "w_glu", "w_gate", "w_proj_ssm", "w_proj_attn", "w_out", "w_ffn_gate", "w_ffn_up",
              "w_ffn_down"]:
        sh[k] = np.ascontiguousarray(inp[k], dtype=f)

    def colT(a, nt):
        return np.ascontiguousarray(a.reshape(L, nt, 128).transpose(2, 0, 1))
    sh["b_modT"] = colT(inp["b_mod"], 96)
    sh["n1T"] = colT(inp["norm1_g"], 16)
    sh["n2T"] = colT(inp["norm2_g"], 16)
    sh["b_gateT"] = colT(inp["b_gate"], 32)
    sh["b_gluT"] = colT(inp["b_glu"], 8)
    sh["dT"] = colT(inp["ssm_d"], 8)
    sh["gq"] = _rep(inp["q_norm_g"])
    sh["gk"] = _rep(inp["k_norm_g"])
    sh["sinkT"] = _rep(inp["attn_sink"])
    sh["are_b"] = _rep(inp["ssm_a_re"].reshape(L, 4096))
    sh["aim_b"] = _rep(inp["ssm_a_im"].reshape(L, 4096))
    sh["ldt_b"] = _rep(inp["ssm_log_dt"])

    def sm(a):
        return np.ascontiguousarray(a.reshape(L, 32, 2, 64).transpose(2, 3, 0, 1).reshape(128, L, 32))
    sh["are_sm"] = sm(inp["ssm_a_re"])
    sh["aim_sm"] = sm(inp["ssm_a_im"])
    sh["ldt_sm"] = sm(np.broadcast_to(inp["ssm_log_dt"][:, :, None], (L, 64, 64)))
    for nm, src in [("BBre", inp["ssm_b_re"]), ("BBim", inp["ssm_b_im"])]:
        bb = np.zeros((L, 8, 8, 16, 8, 64), f)
        s5 = src.reshape(L, 8, 8, 64, 16)
        for gl in range(8):
            bb[:, :, gl, :, gl, :] = s5[:, :, gl].transpose(0, 1, 3, 2)
        sh[nm] = np.ascontiguousarray(bb.transpose(0, 2, 3, 1, 4, 5).reshape(L, 128, 8, 512))
    for nm, src in [("CCre", inp["ssm_c_re"]), ("CCim", inp["ssm_c_im"])]:
        cc = np.zeros((L, 2, 64, 32, 8, 16), f)
        s6 = src.reshape(L, 32, 2, 16, 64)
        for j in range(32):
            for q in range(2):
                cc[:, q, :, j, (2 * j + q) % 8, :] = s6[:, j, q].transpose(0, 2, 1)
        sh[nm] = np.ascontiguousarray(cc.reshape(L, 128, 32, 128))
    half = 8
    inv_freq = (np.float32(500000.0) ** (-np.arange(half, dtype=f) / np.float32(half))).astype(f)
    rc = np.zeros((NGRP, 128, 5, 8), f)
    rs = np.zeros((NGRP, 128, 5, 8), f)
    for g in range(NGRP):
        for tt in range(5):
            if tt < 4:
                pos = (g * TP + tt * 128 + np.arange(128)).astype(f)
            else:
                pos = np.concatenate([(2048 + np.arange(64)).astype(f), np.zeros(64, f)])
            ang = (pos[:, None] * inv_freq[None, :]).astype(f)
            rc[g, :, tt] = np.cos(ang)
            rs[g, :, tt] = np.sin(ang)
    sh["ropec"] = rc
    sh["ropes"] = rs
    sh["tri"] = np.triu(np.ones((128, 128), f))
    sh["tp1"] = _rep(np.arange(1, 129, dtype=f))
    sh["negsp1"] = -np.arange(1, 129, dtype=f).reshape(128, 1)
    return sh


def prep_core(inp, c):
    f = np.float32
    b = c % 4
    m = {}
    m["xp"] = np.ascontiguousarray(inp["x_prompt"][b], dtype=f)
    m["xs"] = np.ascontiguousarray(inp["x_sample"][2 * c:2 * c + 2], dtype=f)
    m["ck"] = np.ascontiguousarray(inp["cache_k"][:, 2 * c:2 * c + 2].reshape(L, 2, 128, 256), dtype=f)
    m["cv"] = np.ascontiguousarray(inp["cache_v"][:, 2 * c:2 * c + 2].reshape(L, 2, 128, 256), dtype=f)

    def st(a):
        return np.ascontiguousarray(a.reshape(L, 2, 32, 2, 64).transpose(0, 1, 3, 4, 2).reshape(L, 2, 128, 32), dtype=f)
    m["st_re"] = st(inp["state_ssm_re"][:, 2 * c:2 * c + 2])
    m["st_im"] = st(inp["state_ssm_im"][:, 2 * c:2 * c + 2])
    cc = np.stack([inp["c_prompt"][b], inp["c_sample"][2 * c], inp["c_sample"][2 * c + 1]], axis=1)
    m["cT"] = np.ascontiguousarray(cc.reshape(16, 128, 3).transpose(1, 0, 2), dtype=f)
    return m


_NC_CACHE = {}


def kernel(**inputs):
    inp = {k: np.asarray(v) for k, v in inputs.items()}
    if "nc" not in _NC_CACHE:
        _NC_CACHE["nc"] = build_program()
    nc = _NC_CACHE["nc"]
    sh = prep_shared(inp)
    in_maps = []
    for c in range(8):
        m = dict(sh)
        m.update(prep_core(inp, c))
        in_maps.append(m)
    res = run_bass_kernel_spmd(nc, in_maps, core_ids=list(range(8)))
    r = res.results
    f = np.float32
    y_prompt = np.stack([r[b]["yp"] for b in range(4)]).astype(f)
    y_sample = np.concatenate([r[c]["ys"] for c in range(8)], axis=0).astype(f)
    pk = np.stack([r[b]["pk"] for b in range(4)], axis=1).reshape(L, 4, 128, 4, 64).astype(f)
    pv = np.stack([r[b]["pv"] for b in range(4)], axis=1).reshape(L, 4, 128, 4, 64).astype(f)
    pre = np.stack([r[b]["pre"] for b in range(4)], axis=1).reshape(L, 4, 64, 64).astype(f)
    pim = np.stack([r[b]["pim"] for b in range(4)], axis=1).reshape(L, 4, 64, 64).astype(f)
    sk = np.concatenate([r[c]["sk"] for c in range(8)], axis=1).reshape(L, 16, 128, 4, 64).astype(f)
    sv = np.concatenate([r[c]["sv"] for c in range(8)], axis=1).reshape(L, 16, 128, 4, 64).astype(f)
    sre = np.concatenate([r[c]["sre"] for c in range(8)], axis=1).reshape(L, 16, 64, 64).astype(f)
    sim = np.concatenate([r[c]["sim"] for c in range(8)], axis=1).reshape(L, 16, 64, 64).astype(f)
    return (y_prompt, y_sample, pk, pv, pre, pim, sk, sv, sre, sim)
```
